# Optimizing a Trainium2 kernel written in Bass

```python
import math
import jax, jax.numpy as jnp
from jax import lax
import numpy as np

D_MODEL = 1024
BATCH = 16
SEQ = 2048
DEPTH = 2

N_MIXERS = 2
N_HEADS = 16
HEAD_DIM = D_MODEL // N_HEADS
NSA_KV_GROUPS = 4
NSA_HPG = N_HEADS // NSA_KV_GROUPS
CMP_BLOCK = 32
CMP_STRIDE = 16
SEL_BLOCK = 64
SEL_TOPK = 16
N_LOCAL_BLOCKS = 2
WINDOW = 512
SEL_Q_CHUNK = 16
Q_BLOCK = 128
NSA_IN = N_HEADS * HEAD_DIM + 6 * NSA_KV_GROUPS * HEAD_DIM + 3 * N_HEADS
FOX_IN = 3 * N_HEADS * HEAD_DIM + N_HEADS
REL_BUCKETS = 32
REL_MAX_DIST = 128
D_FF = 4 * D_MODEL
RMS_EPS = 1e-6
N_NSA_LAYERS = (DEPTH + 1) // 2
N_FOX_LAYERS = DEPTH // 2
NEG_INF = -1e30
FORCE_SCORE = 1e9

kernel_name = "nsa_fox_interleaved_hybrid"


def rms_norm(x, g):
    xf = x.astype(jnp.float32)
    y = xf * lax.rsqrt(jnp.mean(xf * xf, axis=-1, keepdims=True) + RMS_EPS)
    return (y * g.astype(jnp.float32)).astype(x.dtype)


def rel_bucket(dist):
    n = jnp.maximum(dist, 0)
    max_exact = REL_BUCKETS // 2
    nf = jnp.maximum(n, 1).astype(jnp.float32)
    large = max_exact + (jnp.log(nf / max_exact) / math.log(REL_MAX_DIST / max_exact)
                         * (REL_BUCKETS - max_exact)).astype(jnp.int32)
    large = jnp.minimum(large, REL_BUCKETS - 1)
    return jnp.where(n < max_exact, n, large)


def masked_softmax(logits, mask):
    logits = jnp.where(mask, logits.astype(jnp.float32), NEG_INF)
    m = jnp.max(logits, axis=-1, keepdims=True)
    p = jnp.exp(logits - m) * mask
    return p / jnp.maximum(jnp.sum(p, axis=-1, keepdims=True), 1e-30)


def nsa_mixer(h, w_in, pe_k, wk1, wk2, pe_v, wv1, wv2, w_out, rel_bias):
    B, S, _ = h.shape
    H, G, Hg, Dh = N_HEADS, NSA_KV_GROUPS, NSA_HPG, HEAD_DIM
    proj = h @ w_in
    q = proj[..., :H * Dh].reshape(B, S, G, Hg, Dh).transpose(0, 2, 3, 1, 4) * (Dh ** -0.5)

    def kv(i):
        off = H * Dh + i * G * Dh
        return proj[..., off:off + G * Dh].reshape(B, S, G, Dh).transpose(0, 2, 1, 3)

    k_c, v_c, k_s, v_s, k_w, v_w = [kv(i) for i in range(6)]
    gates = jax.nn.sigmoid(proj[..., H * Dh + 6 * G * Dh:].astype(jnp.float32))
    gates = gates.reshape(B, S, 3, G, Hg).transpose(2, 0, 3, 4, 1)[..., None]
    t_pos = jnp.arange(S)
    g_ar = jnp.arange(G)
    rb_g = rel_bias.reshape(REL_BUCKETS, G, Hg).transpose(1, 0, 2)

    n_cmp = (S - CMP_BLOCK) // CMP_STRIDE + 1
    starts = jnp.arange(n_cmp) * CMP_STRIDE
    idx = starts[:, None] + jnp.arange(CMP_BLOCK)[None, :]

    def compress(t, pe, w1, w2):
        blocks = t[:, :, idx] + pe
        flat = blocks.reshape(B, G, n_cmp, CMP_BLOCK * Dh)
        return jax.nn.gelu(flat @ w1) @ w2

    kc = compress(k_c, pe_k, wk1, wk2)
    vc = compress(v_c, pe_v, wv1, wv2)
    dist_c = t_pos[:, None] - (starts + CMP_BLOCK - 1)[None, :]
    bias_c = rel_bias[rel_bucket(dist_c)].transpose(2, 0, 1).reshape(G, Hg, S, n_cmp)
    s_cmp = jnp.einsum('bghqd,bgcd->bghqc', q, kc).astype(jnp.float32) + bias_c
    p_cmp = masked_softmax(s_cmp, dist_c >= 0)
    o_cmp = jnp.einsum('bghqc,bgcd->bghqd', p_cmp.astype(vc.dtype), vc)

    n_sel = S // SEL_BLOCK
    n_topk = min(SEL_TOPK, n_sel)
    cells = jnp.arange(n_cmp)[:, None] + jnp.arange(CMP_BLOCK // CMP_STRIDE)[None, :]
    overlap = jnp.sum(jax.nn.one_hot(cells // (SEL_BLOCK // CMP_STRIDE), n_sel, dtype=jnp.float32), axis=1)
    imp = jnp.einsum('bghqc,cj->bgqj', p_cmp, overlap)
    j = jnp.arange(n_sel)
    rel_blk = (t_pos // SEL_BLOCK)[:, None] - j[None, :]
    forced = (j[None, :] == 0) | ((rel_blk >= 0) & (rel_blk < N_LOCAL_BLOCKS))
    visible = rel_blk >= 0
    score = jnp.where(visible, jnp.where(forced, FORCE_SCORE, imp), NEG_INF)
    _, sel_idx = lax.top_k(score, n_topk)

    kb = k_s.reshape(B, G, n_sel, SEL_BLOCK * Dh)
    vb = v_s.reshape(B, G, n_sel, SEL_BLOCK * Dh)
    n_chunk = S // SEL_Q_CHUNK
    q_ch = q.reshape(B, G, Hg, n_chunk, SEL_Q_CHUNK, Dh).transpose(3, 0, 1, 2, 4, 5)
    idx_ch = sel_idx.reshape(B, G, n_chunk, SEL_Q_CHUNK, n_topk).transpose(2, 0, 1, 3, 4)
    n_keys = n_topk * SEL_BLOCK

    def sel_chunk(args):
        qc, ic, c = args
        flat_idx = ic.reshape(B, G, SEL_Q_CHUNK * n_topk)[..., None]
        kg = jnp.take_along_axis(kb, flat_idx, axis=2).reshape(B, G, SEL_Q_CHUNK, n_keys, Dh)
        vg = jnp.take_along_axis(vb, flat_idx, axis=2).reshape(B, G, SEL_Q_CHUNK, n_keys, Dh)
        tq = c * SEL_Q_CHUNK + jnp.arange(SEL_Q_CHUNK)
        kpos = (ic[..., None] * SEL_BLOCK + jnp.arange(SEL_BLOCK)).reshape(B, G, SEL_Q_CHUNK, n_keys)
        dist = tq[None, None, :, None] - kpos
        bias = rb_g[g_ar[None, :, None, None], rel_bucket(dist)]
        s = jnp.einsum('bghqd,bgqkd->bghqk', qc, kg).astype(jnp.float32) + bias.transpose(0, 1, 4, 2, 3)
        p = masked_softmax(s, (dist >= 0)[:, :, None])
        return jnp.einsum('bghqk,bgqkd->bghqd', p.astype(vg.dtype), vg)

    o_sel = lax.map(sel_chunk, (q_ch, idx_ch, jnp.arange(n_chunk)))
    o_sel = o_sel.transpose(1, 2, 3, 0, 4, 5).reshape(B, G, Hg, S, Dh)

    n_qb = S // Q_BLOCK
    span = WINDOW + Q_BLOCK
    kp = jnp.pad(k_w, ((0, 0), (0, 0), (WINDOW, 0), (0, 0)))
    vp = jnp.pad(v_w, ((0, 0), (0, 0), (WINDOW, 0), (0, 0)))
    q_blk = q.reshape(B, G, Hg, n_qb, Q_BLOCK, Dh).transpose(3, 0, 1, 2, 4, 5)

    def win_block(args):
        qb_, i = args
        start = i * Q_BLOCK
        kblk = lax.dynamic_slice_in_dim(kp, start, span, axis=2)
        vblk = lax.dynamic_slice_in_dim(vp, start, span, axis=2)
        tq = start + jnp.arange(Q_BLOCK)
        kpos = start - WINDOW + jnp.arange(span)
        dist = tq[:, None] - kpos[None, :]
        mask = (dist >= 0) & (dist < WINDOW) & (kpos[None, :] >= 0)
        bias = rel_bias[rel_bucket(dist)].transpose(2, 0, 1).reshape(G, Hg, Q_BLOCK, span)
        s = jnp.einsum('bghqd,bgkd->bghqk', qb_, kblk).astype(jnp.float32) + bias
        p = masked_softmax(s, mask)
        return jnp.einsum('bghqk,bgkd->bghqd', p.astype(vblk.dtype), vblk)

    o_win = lax.map(win_block, (q_blk, jnp.arange(n_qb)))
    o_win = o_win.transpose(1, 2, 3, 0, 4, 5).reshape(B, G, Hg, S, Dh)

    o = (gates[0] * o_cmp + gates[1] * o_sel + gates[2] * o_win).astype(h.dtype)
    o = o.transpose(0, 3, 1, 2, 4).reshape(B, S, H * Dh)
    return o @ w_out


def fox_mixer(h, w_in, b_f, w_out):
    B, S, _ = h.shape
    H, Dh = N_HEADS, HEAD_DIM
    proj = h @ w_in

    def heads(i):
        return proj[..., i * H * Dh:(i + 1) * H * Dh].reshape(B, S, H, Dh).transpose(0, 2, 1, 3)

    q = heads(0) * (Dh ** -0.5)
    k = heads(1)
    v = heads(2)
    f_logit = proj[..., 3 * H * Dh:].astype(jnp.float32) + b_f.astype(jnp.float32)
    c = jnp.cumsum(jax.nn.log_sigmoid(f_logit), axis=1).transpose(0, 2, 1)
    n_qb = S // Q_BLOCK
    q_blk = q.reshape(B, H, n_qb, Q_BLOCK, Dh).transpose(2, 0, 1, 3, 4)
    c_blk = c.reshape(B, H, n_qb, Q_BLOCK).transpose(2, 0, 1, 3)
    kpos = jnp.arange(S)

    def blk(args):
        qb_, cq, i = args
        tq = i * Q_BLOCK + jnp.arange(Q_BLOCK)
        s = (jnp.einsum('bhqd,bhkd->bhqk', qb_, k).astype(jnp.float32)
             + cq[..., None] - c[:, :, None, :])
        p = masked_softmax(s, tq[:, None] >= kpos[None, :])
        return jnp.einsum('bhqk,bhkd->bhqd', p.astype(v.dtype), v)

    o = lax.map(blk, (q_blk, c_blk, jnp.arange(n_qb)))
    o = o.transpose(1, 0, 3, 2, 4).reshape(B, S, H * Dh)
    return o @ w_out


def sq_relu_mlp(h, w1, w2):
    a = jax.nn.relu(h @ w1)
    return (a * a) @ w2


def setup_inputs(seed: int = 0) -> dict:
    key = jax.random.key(seed)
    ks = jax.random.split(key, 24)
    f32 = jnp.float32
    Dh = HEAD_DIM

    def nrm(k, shape, scale):
        return jax.random.normal(k, shape, f32) * scale

    return {
        "x": nrm(ks[0], (BATCH, SEQ, D_MODEL), 1.0),
        "rel_bias": nrm(ks[1], (REL_BUCKETS, N_HEADS), 0.2),
        "norm_mix": 1.0 + nrm(ks[2], (DEPTH, D_MODEL), 0.05),
        "norm_mlp": 1.0 + nrm(ks[3], (DEPTH, D_MODEL), 0.05),
        "nsa_w_in": nrm(ks[4], (N_NSA_LAYERS, D_MODEL, NSA_IN), D_MODEL ** -0.5),
        "nsa_pe_k": nrm(ks[5], (N_NSA_LAYERS, CMP_BLOCK, Dh), 0.1),
        "nsa_wk1": nrm(ks[6], (N_NSA_LAYERS, CMP_BLOCK * Dh, Dh), (CMP_BLOCK * Dh) ** -0.5),
        "nsa_wk2": nrm(ks[7], (N_NSA_LAYERS, Dh, Dh), Dh ** -0.5),
        "nsa_pe_v": nrm(ks[8], (N_NSA_LAYERS, CMP_BLOCK, Dh), 0.1),
        "nsa_wv1": nrm(ks[9], (N_NSA_LAYERS, CMP_BLOCK * Dh, Dh), (CMP_BLOCK * Dh) ** -0.5),
        "nsa_wv2": nrm(ks[10], (N_NSA_LAYERS, Dh, Dh), Dh ** -0.5),
        "nsa_w_out": nrm(ks[11], (N_NSA_LAYERS, N_HEADS * Dh, D_MODEL), (N_HEADS * Dh) ** -0.5),
        "fox_w_in": nrm(ks[12], (N_FOX_LAYERS, D_MODEL, FOX_IN), D_MODEL ** -0.5),
        "fox_b_f": jax.random.uniform(ks[13], (N_FOX_LAYERS, N_HEADS), f32, 1.0, 4.0),
        "fox_w_out": nrm(ks[14], (N_FOX_LAYERS, N_HEADS * Dh, D_MODEL), (N_HEADS * Dh) ** -0.5),
        "mlp_w1": nrm(ks[15], (DEPTH, D_MODEL, D_FF), D_MODEL ** -0.5),
        "mlp_w2": nrm(ks[16], (DEPTH, D_FF, D_MODEL), D_FF ** -0.5),
        "final_norm": 1.0 + nrm(ks[17], (D_MODEL,), 0.05),
    }


def reference(x, rel_bias, norm_mix, norm_mlp, nsa_w_in, nsa_pe_k, nsa_wk1, nsa_wk2,
              nsa_pe_v, nsa_wv1, nsa_wv2, nsa_w_out, fox_w_in, fox_b_f, fox_w_out,
              mlp_w1, mlp_w2, final_norm):
    for i in range(DEPTH):
        hn = rms_norm(x, norm_mix[i])
        li = i // N_MIXERS
        if i % N_MIXERS == 0:
            mix = nsa_mixer(hn, nsa_w_in[li], nsa_pe_k[li], nsa_wk1[li], nsa_wk2[li],
                            nsa_pe_v[li], nsa_wv1[li], nsa_wv2[li], nsa_w_out[li], rel_bias)
        else:
            mix = fox_mixer(hn, fox_w_in[li], fox_b_f[li], fox_w_out[li])
        x = x + mix.astype(x.dtype)
        hn = rms_norm(x, norm_mlp[i])
        x = x + sq_relu_mlp(hn, mlp_w1[i], mlp_w2[i]).astype(x.dtype)
    return rms_norm(x, final_norm)
```

```python
import math
import numpy as np
import ml_dtypes
from contextlib import ExitStack
import concourse.bass as bass
import concourse.mybir as mybir
from concourse.bass_utils import run_bass_kernel_spmd

F32 = mybir.dt.float32
BF16 = mybir.dt.bfloat16
AF = mybir.ActivationFunctionType
ALU = mybir.AluOpType

S = 2048
D = 1024
DFF = 4096
H = 16
DH = 64
G = 4
NSA_IN = 2608
FOX_IN = 3088
EPS = 1e-6
NEG = -30000.0
T1W = 1600
T1Z = 600
SEM_ROLL = 30000


class Buf:
    __slots__ = ("name", "w", "r")

    def __init__(self, name=""):
        self.name = name
        self.w = {}
        self.r = {}


class Prog:
    ENG = ("pe", "act", "dve", "pool", "sp")

    def __init__(self, nc, es):
        self.nc = nc
        self.es = es
        self.ops = {e: [] for e in self.ENG}
        self.nsem = 0
        self.esem = {e: self._newsem() for e in ("pe", "act", "dve", "pool")}
        self.ecnt = {e: 0 for e in self.esem}
        self.waited = {e: {} for e in self.ENG}
        self.dsem = {}
        self.dcnt = {}
        self.allsems = {}
        self.out_tokens = []

    def _newsem(self):
        self.nsem += 1
        return self.es.enter_context(self.nc.semaphore("sm%d" % self.nsem))

    def _deps(self, eng, reads, writes):
        need = {}
        pe_sem = self.esem["pe"]

        def add(sem, val, src):
            if src == "pe" and eng == "pe" and sem is pe_sem:
                return
            if need.get(sem, 0) < val:
                need[sem] = val

        for b in reads:
            for sem, (val, src) in b.w.items():
                add(sem, val, src)
        for b in writes:
            for sem, (val, src) in b.w.items():
                add(sem, val, src)
            for sem, (val, src) in b.r.items():
                add(sem, val, src)
        waits = []
        wd = self.waited[eng]
        for sem, val in need.items():
            if wd.get(sem, 0) < val:
                wd[sem] = val
                waits.append((sem, val))
        return waits

    def _commit(self, tok, reads, writes):
        sem, val, src = tok
        self.allsems[sem] = val
        for b in reads:
            if b.r.get(sem, (0, None))[0] < val:
                b.r[sem] = (val, src)
        for b in writes:
            if b.w.get(sem, (0, None))[0] < val:
                b.w[sem] = (val, src)

    def op(self, eng, fn, reads=(), writes=()):
        if self.ecnt[eng] >= SEM_ROLL:
            self.esem[eng] = self._newsem()
            self.ecnt[eng] = 0
        waits = self._deps(eng, reads, writes)
        self.ecnt[eng] += 1
        sem = self.esem[eng]
        self.ops[eng].append((waits, fn, sem, 1))
        self._commit((sem, self.ecnt[eng], eng), reads, writes)

    def dma(self, q, out, in_, reads, writes, key, is_output=False, **kw):
        if key in self.dsem and self.dcnt[key] >= 2000:
            del self.dsem[key]
        if key not in self.dsem:
            self.dsem[key] = self._newsem()
            self.dcnt[key] = 0
        waits = self._deps(q, reads, writes)
        sem = self.dsem[key]
        self.dcnt[key] += 1
        self.ops[q].append((waits, (lambda e: e.dma_start(out=out, in_=in_, **kw)), sem, 16))
        tok = (sem, 16 * self.dcnt[key], q)
        self._commit(tok, reads, writes)
        if is_output:
            self.out_tokens.append(tok)

    def barrier(self):
        snap = dict(self.allsems)
        for eng in self.ENG:
            wd = self.waited[eng]
            waits = []
            for sem, val in snap.items():
                if wd.get(sem, 0) < val:
                    wd[sem] = val
                    waits.append((sem, val))
            if waits:
                self.ops[eng].append((waits, None, None, 0))

    def emit(self):
        nc = self.nc
        fin = {}
        for sem, val, _ in self.out_tokens:
            fin[sem] = max(fin.get(sem, 0), val)
        with nc.Block() as block:
            def run(eng_name):
                def body(e):
                    for waits, fn, sem, inc in self.ops[eng_name]:
                        for s, v in waits:
                            e.wait_ge(s, v)
                        if fn is not None:
                            fn(e).then_inc(sem, inc)
                    if eng_name == "sp":
                        for s, v in fin.items():
                            e.wait_ge(s, v)
                return body
            block.tensor(run("pe"))
            block.scalar(run("act"))
            block.vector(run("dve"))
            block.gpsimd(run("pool"))
            block.sync(run("sp"))


DTSIZE = {F32: 4, BF16: 2}


class Arena:
    def __init__(self, base_ap, nwords):
        self.base = base_ap
        self.n = nwords
        self.off = 0

    def mark(self):
        return self.off

    def reset(self, m):
        self.off = m

    def alloc(self, shape, dt):
        p = shape[0]
        free = list(shape[1:])
        nel = int(np.prod(free))
        words = (nel * DTSIZE[dt] + 3) // 4
        words = (words + 7) // 8 * 8
        assert self.off + words <= self.n, ("arena overflow", self.off, words, self.n)
        v = self.base[0:p, self.off:self.off + words]
        self.off += words
        if dt != F32:
            v = v.bitcast(dt)
        v = v[:, 0:nel]
        if len(free) == 2:
            v = v.rearrange("p (a b) -> p a b", b=free[1])
        elif len(free) == 3:
            v = v.rearrange("p (a b c) -> p a b c", b=free[1], c=free[2])
        return v


def rel_bucket_np(d):
    d = np.asarray(d)
    n = np.maximum(d, 0)
    nf = np.maximum(n, 1).astype(np.float32)
    large = 16 + (np.log(nf / np.float32(16)) / np.float32(math.log(8.0)) * np.float32(16)).astype(np.int32)
    large = np.minimum(large, 31)
    return np.where(n < 16, n, large)


def host_consts():
    bf = ml_dtypes.bfloat16
    c = {}
    d = np.arange(T1W) - T1Z
    b = np.where(d < 0, 32, np.where(d < 128, rel_bucket_np(d), 31))
    oh = np.zeros((33, T1W), np.float32)
    oh[b, np.arange(T1W)] = 1.0
    c["c_oh1d"] = oh.astype(bf)
    selp = np.zeros((64, 4, 127), np.float32)
    for j in range(4):
        for rp in range(64):
            cc = 32 * (j - 1) + 63 - rp
            if 0 <= cc < 127:
                selp[rp, j, cc] = 1.0
    c["c_selp"] = selp.astype(bf)
    c["c_jmat"] = np.ascontiguousarray(np.eye(128, dtype=np.float32)[::-1]).astype(bf)
    ek = np.zeros((32, S), np.float32)
    ek[np.arange(S) // 64, np.arange(S)] = 1.0
    c["c_ek"] = ek.astype(bf)
    addm = np.zeros((128, 16, 32), np.float32)
    mulm = np.ones((128, 16, 32), np.float32)
    for tt in range(16):
        for p in range(128):
            qb = (tt * 128 + p) // 64
            for j in range(32):
                rel = qb - j
                forced = (j == 0) or (0 <= rel < 2)
                vis = rel >= 0
                if not vis:
                    addm[p, tt, j] = -1e30
                    mulm[p, tt, j] = 0.0
                elif forced:
                    addm[p, tt, j] = 1e9
                    mulm[p, tt, j] = 0.0
    c["c_addm"] = addm
    c["c_mulm"] = mulm
    ovl = np.zeros((127, 32), np.float32)
    for cc in range(127):
        for cell in (cc, cc + 1):
            ovl[cc, cell // 4] += 1.0
    c["c_ovl"] = ovl.astype(bf)
    kk = np.arange(128)[:, None]
    qq = np.arange(128)[None, :]
    c["c_we"] = np.where(qq >= kk, NEG, 0.0).astype(np.float32).astype(bf)
    c["c_cm"] = np.where(qq < kk, NEG, 0.0).astype(np.float32).astype(bf)
    return c


CONST_SPECS = {
    "c_oh1d": ([33, T1W], BF16), "c_selp": ([64, 4, 127], BF16), "c_jmat": ([128, 128], BF16),
    "c_ek": ([32, S], BF16), "c_addm": ([128, 16, 32], F32), "c_mulm": ([128, 16, 32], F32),
    "c_ovl": ([127, 32], BF16), "c_we": ([128, 128], BF16), "c_cm": ([128, 128], BF16),
}


def build(nseq=2, parts=("nsa", "fox", "mlp"), depth=2, debug=()):
    nc = bass.Bass("TRN2", target_bir_lowering=False)
    es = ExitStack()
    P = Prog(nc, es)

    def dram_in(name, shape, dt=F32):
        return nc.dram_tensor(name, list(shape), dt, kind="ExternalInput").ap()

    def dram_tmp(name, shape, dt):
        return nc.dram_tensor(name, list(shape), dt, kind=("ExternalOutput" if name in debug else "Internal")).ap()

    x_in = dram_in("x", [nseq, S, D])
    out = nc.dram_tensor("out", [nseq, S, D], F32, kind="ExternalOutput").ap()
    rel_bias = dram_in("rel_bias", [32, H])
    norm_mix = dram_in("norm_mix", [2, D])
    norm_mlp = dram_in("norm_mlp", [2, D])
    nsa_w_in = dram_in("nsa_w_in", [1, D, NSA_IN])
    nsa_pe_k = dram_in("nsa_pe_k", [1, 32, DH])
    nsa_wk1 = dram_in("nsa_wk1", [1, 32 * DH, DH])
    nsa_wk2 = dram_in("nsa_wk2", [1, DH, DH])
    nsa_pe_v = dram_in("nsa_pe_v", [1, 32, DH])
    nsa_wv1 = dram_in("nsa_wv1", [1, 32 * DH, DH])
    nsa_wv2 = dram_in("nsa_wv2", [1, DH, DH])
    nsa_w_out = dram_in("nsa_w_out", [1, D, D])
    fox_w_in = dram_in("fox_w_in", [1, D, FOX_IN])
    fox_b_f = dram_in("fox_b_f", [1, H])
    fox_w_out = dram_in("fox_w_out", [1, D, D])
    mlp_w1 = dram_in("mlp_w1", [2, D, DFF])
    mlp_w2 = dram_in("mlp_w2", [2, DFF, D])
    final_norm = dram_in("final_norm", [D])
    cst = {k: dram_in(k, sh, dt) for k, (sh, dt) in CONST_SPECS.items()}

    xs = dram_tmp("xs", [nseq, S, D], F32)
    b_xs = [[Buf() for _ in range(16)] for _ in range(nseq)]
    w1b = dram_tmp("w1b", [2, D, DFF], BF16)
    w2b = dram_tmp("w2b", [2, DFF, D], BF16)
    nwinb = dram_tmp("nwinb", [D, NSA_IN], BF16)
    nwoutb = dram_tmp("nwoutb", [D, D], BF16)
    fwinb = dram_tmp("fwinb", [D, FOX_IN], BF16)
    fwoutb = dram_tmp("fwoutb", [D, D], BF16)
    b_w1b = [Buf(), Buf()]
    b_w2b = [Buf(), Buf()]
    b_nwinb, b_nwoutb, b_fwinb, b_fwoutb = Buf(), Buf(), Buf(), Buf()
    b_nwinb_blk = [Buf() for _ in range(6)]
    t1_d = dram_tmp("t1_d", [H, T1W], BF16)
    b_t1d = Buf()
    qT_d = dram_tmp("qT_d", [D, S], BF16)
    kT_d = dram_tmp("kT_d", [D, S], BF16)
    v_d = dram_tmp("v_d", [S, D], BF16)
    gates_d = dram_tmp("gates_d", [S, 48], F32)
    cpart_d = dram_tmp("cpart_d", [6, H, S], BF16)
    b_qTd, b_kTd, b_vd, b_gd, b_cpd = Buf(), Buf(), Buf(), Buf(), Buf()

    NW = 51200
    arena_t = es.enter_context(nc.sbuf_tensor("arena", [128, NW], F32))
    A = Arena(arena_t, NW)
    banks = [es.enter_context(nc.psum_tensor("bank%d" % i, [128, 512], F32)) for i in range(8)]
    b_bank = [Buf("bank%d" % i) for i in range(8)]

    def bank_bf(i):
        return banks[i][:].bitcast(BF16)

    def mm(o, lhsT, rhs, start, stop, reads, writes, skip=False):
        P.op("pe", lambda e: e.matmul(o, lhsT, rhs, start=start, stop=stop, skip_group_check=skip), reads, writes)

    def tr(o, i, idn, reads, writes):
        P.op("pe", lambda e: e.transpose(out=o, in_=i, identity=idn), reads, writes)

    def act(o, i, func, reads, writes, **kw):
        P.op("act", lambda e: e.activation(out=o, in_=i, func=func, **kw), reads, writes)

    def tt(eng, o, a, b, op, reads, writes):
        P.op(eng, lambda e: e.tensor_tensor(out=o, in0=a, in1=b, op=op), reads, writes)

    def ts(eng, o, a, s1, s2, op0, op1, reads, writes):
        if s2 is None:
            P.op(eng, lambda e: e.tensor_scalar(out=o, in0=a, scalar1=s1, scalar2=None, op0=op0), reads, writes)
        else:
            P.op(eng, lambda e: e.tensor_scalar(out=o, in0=a, scalar1=s1, scalar2=s2, op0=op0, op1=op1), reads, writes)

    def cp(eng, o, i, reads, writes):
        P.op(eng, lambda e: e.tensor_copy(out=o, in_=i), reads, writes)

    ev_cnt = [0]

    def evac(o, i, reads, writes, scale=None):
        ev_cnt[0] += 1
        if ev_cnt[0] % 2 == 0:
            if scale is None:
                act(o, i, AF.Copy, reads, writes)
            else:
                act(o, i, AF.Copy, reads, writes, scale=float(scale))
        else:
            if scale is None:
                cp("dve", o, i, reads, writes)
            else:
                ts("dve", o, i, float(scale), None, ALU.mult, None, reads, writes)

    ident = A.alloc([128, 128], BF16)
    b_ident = Buf()
    P.op("pool", lambda e: e.memset(ident, 0.0), [], [b_ident])
    P.op("pool", lambda e: e.affine_select(out=ident, in_=ident, pattern=[[-1, 128]], compare_op=ALU.not_equal,
                                           fill=1.0, base=0, channel_multiplier=1), [b_ident], [b_ident])
    gcol = A.alloc([128, 4, 8], F32)
    b_gcol = Buf()
    for i, src in enumerate([norm_mix[0], norm_mix[1], norm_mlp[0], norm_mlp[1]]):
        P.dma("sp", gcol[:, i, :], src.rearrange("(k p) -> p k", p=128), [], [b_gcol], "gcol",
              allow_slow_non_contiguous=True)
    gfin = A.alloc([128, D], F32)
    b_gfin = Buf()
    P.dma("sp", gfin, final_norm.partition_broadcast(128), [], [b_gfin], "gfin")
    ones_f = A.alloc([128, 512], F32)
    b_ones = Buf()
    P.op("pool", lambda e: e.memset(ones_f, 1.0), [], [b_ones])

    NST = 4
    st_ss = [A.alloc([128, 1], F32) for _ in range(NST)]
    st_rs = [A.alloc([128, 1], F32) for _ in range(NST)]
    b_ss = [Buf() for _ in range(NST)]
    b_rs = [Buf() for _ in range(NST)]
    junk = [A.alloc([128, D], BF16) for _ in range(2)]
    b_junk = [Buf() for _ in range(2)]
    xn = [A.alloc([128, D], BF16) for _ in range(2)]
    b_xn = [Buf() for _ in range(2)]
    norm_i = [0]
    PT_BANKS = (4, 5)

    def rstd_of(xt_ap, b_xt):
        i = norm_i[0]
        norm_i[0] += 1
        k = i % NST
        j = i % 2
        act(junk[j], xt_ap, AF.Square, [b_xt], [b_junk[j], b_ss[k]], accum_out=st_ss[k])
        ts("dve", st_rs[k], st_ss[k], 1.0 / D, EPS, ALU.mult, ALU.add, [b_ss[k]], [b_rs[k]])
        act(st_rs[k], st_rs[k], AF.Sqrt, [b_rs[k]], [b_rs[k]])
        P.op("dve", lambda e: e.reciprocal(out=st_rs[k], in_=st_rs[k]), [b_rs[k]], [b_rs[k]])
        return st_rs[k], b_rs[k], j

    def norm_T(xt_ap, b_xt, gi, dst_ap, b_dst):
        rs, b_r, j = rstd_of(xt_ap, b_xt)
        ts("dve", xn[j], xt_ap, rs[:, 0:1], None, ALU.mult, None, [b_xt, b_r], [b_xn[j]])
        bk = PT_BANKS[j]
        pv = bank_bf(bk).rearrange("p (a b) -> p a b", b=128)
        for kc in range(8):
            tr(pv[:, kc, :], xn[j][:, kc * 128:(kc + 1) * 128], ident, [b_xn[j], b_ident], [b_bank[bk]])
        gb = gcol[:, gi, :].unsqueeze(2).broadcast_to([128, 8, 128])
        tt("dve", dst_ap, pv, gb, ALU.mult, [b_bank[bk], b_gcol], [b_dst])

    def cast_w(dst, src, rows, b, key, nsplit=4):
        step = rows // nsplit
        for r in range(nsplit):
            P.dma("pool", dst[r * step:(r + 1) * step, :], src[r * step:(r + 1) * step, :], [], [b], key)

    base_mark = A.mark()

    def run_units(units, PT, b_PT, pS_banks, pO_banks):
        flat = []
        for ui, u in enumerate(units):
            for si, st in enumerate(u["steps"]):
                flat.append((ui, si, st, u))
        nPT = len(PT)
        called = set()

        def call_pre(ui):
            if ui < len(units) and ui not in called:
                called.add(ui)
                if units[ui].get("pre") is not None:
                    units[ui]["pre"]()

        call_pre(0)

        def qk(idx):
            ui, si, st, u = flat[idx]
            if si == 0:
                call_pre(ui + 1)
            bk = pS_banks[idx % 2]
            M = st["M"]
            n = len(st["mms"])
            for mi, (oc0, oc1, lhsT, rhs, reads) in enumerate(st["mms"]):
                mm(banks[bk][0:M, oc0:oc1], lhsT, rhs, mi == 0, mi == n - 1, reads, [b_bank[bk]], skip=True)
            sl = idx % nPT
            act(PT[sl][0:M, st["c0"]:st["c1"]], banks[bk][0:M, st["c0"]:st["c1"]], AF.Exp, [b_bank[bk]], [b_PT[sl]])

        def pv(idx):
            ui, si, st, u = flat[idx]
            ob = pO_banks[ui % 2]
            M = st["M"]
            ncol = st["ncol"]
            sl = idx % nPT
            vr, vreads = st["v"]
            po = banks[ob][:, 0:4 * ncol].rearrange("p (j c) -> p j c", c=ncol)
            first = (si == 0)
            for j in range(4):
                if st["c0"] <= 128 * j and 128 * (j + 1) <= st["c1"]:
                    mm(po[:, j, :], PT[sl][0:M, 128 * j:128 * (j + 1)], vr, first, True, [b_PT[sl]] + vreads, [b_bank[ob]], skip=True)
                    first = False

        for idx in range(len(flat)):
            if idx == 0:
                qk(0)
            if idx + 1 < len(flat):
                qk(idx + 1)
            pv(idx)
            ui, si, st, u = flat[idx]
            if si == len(u["steps"]) - 1:
                u["epi"](pO_banks[ui % 2])

    def proj_phase(hT, b_hT, wsrc, b_wsrc, ncols_total, fm_list, tm_list, wring, b_wring, stg, b_stg, pbanks, gstg=None, b_gstg=None):
        nblk = (ncols_total + 511) // 512
        cnt = [0, 0]
        for blk in range(nblk):
            bc0 = blk * 512
            bw = min(512, ncols_total - bc0)
            wi = blk % len(wring)
            P.dma("sp", wring[wi][:, :, 0:bw], wsrc.rearrange("(kc p) n -> p kc n", p=128)[:, :, bc0:bc0 + bw],
                  [b_wsrc[blk] if isinstance(b_wsrc, list) else b_wsrc], [b_wring[wi]], "wring%d" % wi)
            for (c0, dst, b_dst, scale) in fm_list:
                if not (bc0 <= c0 < bc0 + bw):
                    continue
                lc = c0 - bc0
                for tc in range(4):
                    bk = pbanks[cnt[0] % 2]
                    si = cnt[0] % len(stg)
                    cnt[0] += 1
                    for kc in range(8):
                        mm(banks[bk][:, :], wring[wi][:, kc, lc:lc + 128], hT[:, kc, tc * 512:(tc + 1) * 512], kc == 0, kc == 7,
                           [b_wring[wi], b_hT[tc]], [b_bank[bk]])
                    evac(stg[si], banks[bk][:, :], [b_bank[bk]], [b_stg[si]], scale=scale)
                    P.dma("pool", dst[:, tc * 512:(tc + 1) * 512], stg[si], [b_stg[si]], [b_dst], "stg%d" % si)
            for (c0, ncols, dst, b_dst, kind) in tm_list:
                if not (bc0 <= c0 < bc0 + bw):
                    continue
                lc = c0 - bc0
                for t16 in range(16):
                    bk = pbanks[cnt[0] % 2]
                    cnt[0] += 1
                    for kc in range(8):
                        mm(banks[bk][:, 0:ncols], hT[:, kc, t16 * 128:(t16 + 1) * 128], wring[wi][:, kc, lc:lc + ncols], kc == 0, kc == 7,
                           [b_wring[wi], b_hT[t16 // 4]], [b_bank[bk]])
                    if kind == "sig":
                        gi = cnt[1] % len(gstg)
                        cnt[1] += 1
                        act(gstg[gi], banks[bk][:, 0:ncols], AF.Sigmoid, [b_bank[bk]], [b_gstg[gi]])
                        P.dma("pool", dst[t16 * 128:(t16 + 1) * 128, :], gstg[gi], [b_gstg[gi]], [b_dst], "gstg%d" % gi)
                    else:
                        si = cnt[0] % len(stg)
                        evac(stg[si][:, 0:ncols], banks[bk][:, 0:ncols], [b_bank[bk]], [b_stg[si]])
                        P.dma("pool", dst[t16 * 128:(t16 + 1) * 128, :], stg[si][:, 0:ncols], [b_stg[si]], [b_dst], "stg%d" % si)

    def load_hT(s, src_x, gi):
        hT = A.alloc([128, 8, S], BF16)
        b_hT = [Buf() for _ in range(4)]
        xt = [A.alloc([128, D], F32) for _ in range(3)]
        b_xt = [Buf() for _ in range(3)]
        for t16 in range(16):
            xi = t16 % 3
            P.dma("sp", xt[xi], src_x[s, t16 * 128:(t16 + 1) * 128, :], [b_xs[s][t16]], [b_xt[xi]], "ldx%d" % xi)
            norm_T(xt[xi], b_xt[xi], gi, hT[:, :, t16 * 128:(t16 + 1) * 128], b_hT[t16 // 4])
        return hT, b_hT

    def outproj_residual(s, qc, o_bf, b_obf, wout, b_wout, src_x, oT, b_oT, xres, b_xres, obanks):
        for j in range(4):
            bk = PT_BANKS[j % 2]
            pv = bank_bf(bk).rearrange("p (a b) -> p a b", b=128)
            for kc in range(8):
                tr(pv[:, kc, :], o_bf[:, j, kc * 128:(kc + 1) * 128], ident, [b_obf, b_ident], [b_bank[bk]])
            evac(oT[:, :, j * 128:(j + 1) * 128], pv, [b_bank[bk]], [b_oT])
        for j in range(4):
            t16 = qc * 4 + j
            xi = j % 2
            P.dma("sp", xres[xi], src_x[s, t16 * 128:(t16 + 1) * 128, :], [b_xs[s][t16]], [b_xres[xi]], "xres%d" % xi)
            for nh in range(2):
                bk = obanks[(j * 2 + nh) % 2]
                for kc in range(8):
                    mm(banks[bk][:, :], oT[:, kc, j * 128:(j + 1) * 128], wout[:, kc, nh * 512:(nh + 1) * 512], kc == 0, kc == 7,
                       [b_oT, b_wout], [b_bank[bk]])
                tt("dve", xres[xi][:, nh * 512:(nh + 1) * 512], banks[bk][:, :], xres[xi][:, nh * 512:(nh + 1) * 512], ALU.add,
                   [b_bank[bk], b_xres[xi]], [b_xres[xi]])
            P.dma("pool", xs[s, t16 * 128:(t16 + 1) * 128, :], xres[xi], [b_xres[xi]], [b_xs[s][t16]], "xst%d" % xi)

    def nsa_setup_tables():
        m = A.mark()
        rb = A.alloc([33, H], F32)
        rb31 = A.alloc([33, H], F32)
        tv = A.alloc([33, H], BF16)
        oh = A.alloc([33, T1W], BF16)
        t1s = A.alloc([H, T1W], BF16)
        b = Buf()
        P.op("dve", lambda e: e.memset(rb, NEG), [], [b])
        P.op("dve", lambda e: e.memset(rb31, 0.0), [b], [b])
        P.dma("sp", rb[0:32, :], rel_bias, [b], [b], "t1a")
        P.dma("sp", rb31[0:32, :], rel_bias[31, :].partition_broadcast(32), [b], [b], "t1a")
        P.dma("sp", oh, cst["c_oh1d"], [], [b], "t1a")
        tt("dve", tv, rb, rb31, ALU.subtract, [b], [b])
        for c4 in range(4):
            mm(banks[7][0:H, 0:400], tv, oh[:, c4 * 400:(c4 + 1) * 400], True, True, [b], [b_bank[7]])
            cp("dve", t1s[:, c4 * 400:(c4 + 1) * 400], banks[7][0:H, 0:400], [b_bank[7]], [b])
        P.dma("sp", t1_d, t1s, [b], [b_t1d], "t1a")
        P.barrier()
        A.reset(m)

    def nsa_layer(s, src_x):
        m0 = A.mark()
        hT, b_hT = load_hT(s, src_x, 0)
        wring = [A.alloc([128, 8, 512], BF16) for _ in range(3)]
        b_wring = [Buf() for _ in range(3)]
        stg = [A.alloc([128, 512], BF16) for _ in range(4)]
        b_stg = [Buf() for _ in range(4)]
        gstg = [A.alloc([128, 48], F32) for _ in range(2)]
        b_gstg = [Buf() for _ in range(2)]
        fm = []
        for i in range(8):
            fm.append((i * 128, qT_d[i * 128:(i + 1) * 128, :], b_qTd, 0.125))
        for i, c0 in enumerate([1024, 1152, 1280, 1408, 1536, 1664, 2048, 2176]):
            fm.append((c0, kT_d[i * 128:(i + 1) * 128, :], b_kTd, None))
        tm = [(1792, 256, v_d[:, 0:256], b_vd, "copy"), (2304, 256, v_d[:, 256:512], b_vd, "copy"),
              (2560, 48, gates_d, b_gd, "sig")]
        proj_phase(hT, b_hT, nwinb, b_nwinb_blk, NSA_IN, fm, tm, wring, b_wring, stg, b_stg, (6, 7), gstg, b_gstg)
        P.barrier()
        A.reset(m0)

        cd = A.alloc([128, H, 256], BF16)
        jmat = A.alloc([128, 128], BF16)
        addm = A.alloc([128, 16, 32], F32)
        mulm = A.alloc([128, 16, 32], F32)
        we = A.alloc([128, 128], BF16)
        b_c = Buf()
        P.dma("sp", cd, bass.AP(t1_d.tensor, T1Z - 127, [[1, 128], [T1W, H], [1, 256]]), [b_t1d], [b_c], "nc0")
        P.dma("sp", jmat, cst["c_jmat"], [], [b_c], "nc0")
        P.dma("sp", addm, cst["c_addm"], [], [b_c], "nc0")
        P.dma("sp", mulm, cst["c_mulm"], [], [b_c], "nc0")
        P.dma("sp", we, cst["c_we"], [], [b_c], "nc0")
        wout = A.alloc([128, 8, D], BF16)
        b_wout = Buf()
        P.dma("sp", wout, nwoutb.rearrange("(kc p) n -> p kc n", p=128), [b_nwoutb], [b_wout], "nc1")
        ksaug = A.alloc([128, G, S], BF16)
        kwaug = A.alloc([128, G, S], BF16)
        b_k = Buf()
        P.op("pool", lambda e: e.memset(ksaug, 0.0), [], [b_k])
        P.op("pool", lambda e: e.memset(kwaug, 0.0), [b_k], [b_k])
        for g in range(G):
            P.dma("sp", ksaug[0:64, g, :], kT_d[512 + g * 64:512 + (g + 1) * 64, :], [b_kTd, b_k], [b_k], "nc2")
            P.dma("sp", ksaug[64:96, g, :], cst["c_ek"], [b_k], [b_k], "nc2")
            P.dma("sp", kwaug[0:64, g, :], kT_d[768 + g * 64:768 + (g + 1) * 64, :], [b_kTd, b_k], [b_k], "nc2")
        vsx = A.alloc([128, 16, G, 65], BF16)
        vwx = A.alloc([128, 16, G, 65], BF16)
        b_v = Buf()
        P.op("pool", lambda e: e.memset(vsx, 1.0), [], [b_v])
        P.op("pool", lambda e: e.memset(vwx, 1.0), [b_v], [b_v])
        for kt in range(16):
            P.dma("sp", vsx[:, kt, :, 0:64], v_d[kt * 128:(kt + 1) * 128, 0:256].rearrange("p (g d) -> p g d", d=64), [b_vd, b_v], [b_v], "nc3")
            P.dma("sp", vwx[:, kt, :, 0:64], v_d[kt * 128:(kt + 1) * 128, 256:512].rearrange("p (g d) -> p g d", d=64), [b_vd, b_v], [b_v], "nc3")
        kcaug = A.alloc([128, G, 4, 128], BF16)
        vcx = A.alloc([128, G, 97], BF16)
        b_kcmp, b_vcx = Buf(), Buf()
        P.op("pool", lambda e: e.memset(kcaug, 0.0), [], [b_kcmp])
        P.op("pool", lambda e: e.memset(vcx, 1.0), [], [b_vcx])
        for g in range(G):
            P.dma("sp", vcx[0:127, g, 65:97], cst["c_ovl"], [b_vcx], [b_vcx], "nc4")
            P.dma("sp", kcaug[64:128, g, :, 0:127], cst["c_selp"], [b_kcmp], [b_kcmp], "nc4")

        mB = A.mark()
        kcT = A.alloc([64, G, S], BF16)
        vcT = A.alloc([64, G, S], BF16)
        b_kc = Buf()
        P.dma("sp", kcT, kT_d[0:256, :].rearrange("(g p) t -> p g t", p=64), [b_kTd], [b_kc], "nb0")
        P.dma("sp", vcT, kT_d[256:512, :].rearrange("(g p) t -> p g t", p=64), [b_kTd], [b_kc], "nb0")
        w1s = {}
        w2s = {}
        peT = {}
        b_cw = Buf()
        for nm, w1, w2, pe in (("k", nsa_wk1, nsa_wk2, nsa_pe_k), ("v", nsa_wv1, nsa_wv2, nsa_pe_v)):
            w1s[nm] = A.alloc([64, 32, DH], BF16)
            w2s[nm] = A.alloc([64, 64], BF16)
            peT[nm] = A.alloc([64, 34], BF16)
            P.op("dve", lambda e, nm=nm: e.memset(peT[nm], 0.0), [], [b_cw])
            P.dma("pool", w1s[nm], w1[0].rearrange("(l d) o -> d l o", d=DH), [], [b_cw], "nb1")
            P.dma("pool", w2s[nm], w2[0], [], [b_cw], "nb1")
            P.dma("pool", peT[nm][:, 0:32], pe[0].rearrange("l d -> d l"), [b_cw], [b_cw], "nb1", allow_slow_non_contiguous=True)
        cstv = A.alloc([64, 2], F32)
        b_cst = Buf()
        gu = [A.alloc([64, 128], F32) for _ in range(4)]
        gbf = A.alloc([64, 128], BF16)
        b_g = Buf()
        for ni, nm in enumerate(("k", "v")):
            for l in range(32):
                mm(banks[7][0:64, 0:2], w1s[nm][:, l, :], peT[nm][:, l:l + 2], l == 0, l == 31, [b_cw], [b_bank[7]])
            cp("dve", cstv[:, ni:ni + 1], banks[7][0:64, 0:1], [b_bank[7]], [b_cst])
        for g in range(G):
            for ni, (nm, srcT) in enumerate((("k", kcT), ("v", vcT))):
                bk = 6 + (g * 2 + ni) % 2
                v4 = srcT.rearrange("p g (c r) -> p g c r", r=16)
                for l in range(32):
                    mm(banks[bk][0:64, 0:127], w1s[nm][:, l, :], v4[:, g, (l // 16):(l // 16) + 127, l % 16], l == 0, l == 31,
                       [b_cw, b_kc], [b_bank[bk]])
                u, u2, t3, th = gu
                act(u[:, 0:127], banks[bk][0:64, 0:127], AF.Identity, [b_bank[bk], b_cst], [b_g], bias=cstv[:, ni:ni + 1])
                tt("dve", u2[:, 0:127], u[:, 0:127], u[:, 0:127], ALU.mult, [b_g], [b_g])
                ts("dve", u2[:, 0:127], u2[:, 0:127], 0.044715, 1.0, ALU.mult, ALU.add, [b_g], [b_g])
                tt("dve", t3[:, 0:127], u2[:, 0:127], u[:, 0:127], ALU.mult, [b_g], [b_g])
                act(th[:, 0:127], t3[:, 0:127], AF.Tanh, [b_g], [b_g], scale=0.7978845608028654)
                ts("dve", th[:, 0:127], th[:, 0:127], 1.0, 0.5, ALU.add, ALU.mult, [b_g], [b_g])
                tt("dve", gbf[:, 0:127], th[:, 0:127], u[:, 0:127], ALU.mult, [b_g], [b_g])
                if nm == "k":
                    mm(banks[bk][0:64, 128:255], w2s["k"], gbf[:, 0:127], True, True, [b_g, b_cw], [b_bank[bk]])
                    for q4 in range(4):
                        cp("dve", kcaug[0:64, g, q4, 0:127], banks[bk][0:64, 128:255], [b_bank[bk]], [b_kcmp])
                else:
                    mm(banks[bk][0:127, 256:320], gbf[:, 0:127], w2s["v"], True, True, [b_g, b_cw], [b_bank[bk]])
                    cp("dve", vcx[0:127, g, 0:64], banks[bk][0:127, 256:320], [b_bank[bk]], [b_vcx])
        P.barrier()
        A.reset(mB)

        Qc = [A.alloc([128, 512], BF16) for _ in range(3)]
        Qsw = [A.alloc([128, 512], BF16) for _ in range(3)]
        b_Qc = [Buf() for _ in range(3)]
        b_Qsw = [Buf() for _ in range(3)]
        for i in range(3):
            P.op("pool", lambda e, i=i: e.memset(Qsw[i], 0.0), [], [b_Qsw[i]])
        PT = [A.alloc([128, 512], BF16) for _ in range(3)]
        b_PT = [Buf() for _ in range(3)]
        o_acc = A.alloc([128, 4, D], F32)
        o_bf = A.alloc([128, 4, D], BF16)
        b_oacc, b_obf = Buf(), Buf()
        oT = A.alloc([128, 8, 512], BF16)
        b_oT = Buf()
        xres = [A.alloc([128, D], F32) for _ in range(2)]
        b_xres = [Buf() for _ in range(2)]
        gts = A.alloc([128, 4, 48], F32)
        b_gts = Buf()
        imp = A.alloc([128, 4, 32], F32)
        sc = A.alloc([128, 4, 32], F32)
        sc2 = A.alloc([128, 4, 32], F32)
        mx8 = A.alloc([128, 8], F32)
        mkb = A.alloc([128, 4, 96], BF16)
        b_imp, b_sc = Buf(), Buf()
        P.op("pool", lambda e: e.memset(mkb, 0.0), [], [b_sc])
        sm = [A.alloc([128, 4], F32) for _ in range(6)]
        b_sm = [Buf() for _ in range(6)]
        tmpo = [A.alloc([128, 4, 64], F32) for _ in range(2)]
        b_tmpo = [Buf() for _ in range(2)]
        tmpi = A.alloc([128, 4, 32], F32)
        b_tmpi = Buf()
        ucnt = [0]
        qcc = [0]
        qsc = [0]

        selTs = A.alloc([128, G, 512], BF16)
        b_selTs = [Buf() for _ in range(G)]
        stgo = [A.alloc([128, 4 * 97], F32) for _ in range(3)]
        b_stgo = [Buf() for _ in range(3)]
        for qc in range(4):
            q0 = qc * 512
            P.dma("sp", gts, gates_d[q0:q0 + 512, :].rearrange("(j p) c -> p j c", p=128), [b_gd, b_gts], [b_gts], "gts")
            use_sel = qc >= 2

            def epilogue(ob, h, br, ncol, first_branch, with_imp, hg):
                k = ucnt[0] % 6
                k2 = (ucnt[0] + 3) % 6
                si = ucnt[0] % 3
                ucnt[0] += 1
                cp("dve", stgo[si][:, 0:4 * ncol], banks[ob][:, 0:4 * ncol], [b_bank[ob]], [b_stgo[si]])
                po = stgo[si][:, 0:4 * ncol].rearrange("p (j c) -> p j c", c=ncol)
                b_po = b_stgo[si]
                ts("dve", sm[k], po[:, :, 64], 1e-30, None, ALU.max, None, [b_po], [b_sm[k]])
                P.op("dve", lambda e: e.reciprocal(out=sm[k], in_=sm[k]), [b_sm[k]], [b_sm[k]])
                if with_imp:
                    rb_ = sm[k][:, :].unsqueeze(2).broadcast_to([128, 4, 32])
                    if hg == 0:
                        tt("dve", imp, po[:, :, 65:97], rb_, ALU.mult, [b_po, b_sm[k]], [b_imp])
                    else:
                        tt("dve", tmpi, po[:, :, 65:97], rb_, ALU.mult, [b_po, b_sm[k]], [b_tmpi])
                        tt("pool", imp, imp, tmpi, ALU.add, [b_tmpi, b_imp], [b_imp])
                tt("dve", sm[k2], sm[k], gts[:, :, br * 16 + h], ALU.mult, [b_sm[k], b_gts], [b_sm[k2]])
                rg = sm[k2][:, :].unsqueeze(2).broadcast_to([128, 4, 64])
                osl = o_acc[:, :, h * 64:(h + 1) * 64]
                if first_branch:
                    tt("dve", osl, po[:, :, 0:64], rg, ALU.mult, [b_po, b_sm[k2]], [b_oacc])
                else:
                    ti = ucnt[0] % 2
                    tt("dve", tmpo[ti], po[:, :, 0:64], rg, ALU.mult, [b_po, b_sm[k2]], [b_tmpo[ti]])
                    tt("pool", osl, osl, tmpo[ti], ALU.add, [b_tmpo[ti], b_oacc], [b_oacc])

            def selection(g):
                bk_sel = PT_BANKS[g % 2]
                pvw = bank_bf(bk_sel)
                tt("dve", sc, imp, mulm[:, qc * 4:(qc + 1) * 4, :], ALU.mult, [b_imp, b_c], [b_sc])
                tt("dve", sc, sc, addm[:, qc * 4:(qc + 1) * 4, :], ALU.add, [b_sc, b_c], [b_sc])
                for j in range(4):
                    P.op("dve", lambda e, j=j: e.max(out=mx8, in_=sc[:, j, :]), [b_sc], [b_sc])
                    P.op("dve", lambda e, j=j: e.match_replace(out=sc2[:, j, :], in_to_replace=mx8, in_values=sc[:, j, :], imm_value=-3e38),
                         [b_sc], [b_sc])
                    P.op("dve", lambda e, j=j: e.max(out=mx8, in_=sc2[:, j, :]), [b_sc], [b_sc])
                    ts("dve", sc2[:, j, :], sc[:, j, :], mx8[:, 7:8], None, ALU.is_ge, None, [b_sc], [b_sc])
                ts("dve", mkb[:, :, 64:96], sc2, -NEG, NEG, ALU.mult, ALU.add, [b_sc], [b_sc])
                for j in range(4):
                    tr(pvw[0:96, j * 128:(j + 1) * 128], mkb[:, j, :], ident, [b_sc, b_ident], [b_bank[bk_sel]])
                cp("dve", selTs[64:96, g, :], pvw[64:96, 0:512], [b_bank[bk_sel]], [b_selTs[g]])

            def near_corr(mms, h, delta):
                if delta > 128:
                    return
                a0 = max(0, -delta)
                a1 = min(512, 256 - delta)
                mms.append((a0, a1, jmat, cd[:, h, delta + a0:delta + a1], [b_c]))

            units = []
            Mc = min(32 * (qc + 1), 127)
            for g in range(G):
                for hg in range(4):
                    h = g * 4 + hg
                    qi = qcc[0] % 3
                    qcc[0] += 1

                    def pre(h=h, qi=qi):
                        P.dma("sp", Qc[qi][0:64, :], qT_d[h * 64:(h + 1) * 64, q0:q0 + 512], [b_qTd, b_Qc[qi]], [b_Qc[qi]], "nQc%d" % qi)
                        P.dma("sp", Qc[qi][64:128, :], bass.AP(t1_d.tensor, h * T1W + T1Z + 481 - 1008, [[16, 64], [1, 512]]),
                              [b_t1d, b_Qc[qi]], [b_Qc[qi]], "nQc%d" % qi)
                    mms = [(0, 512, kcaug[:, g, qc, 0:Mc], Qc[qi], [b_kcmp, b_Qc[qi]])]
                    st = dict(M=Mc, c0=0, c1=512, mms=mms, v=(vcx[0:Mc, g, :], [b_vcx]), ncol=97)

                    def epi_c(ob, h=h, hg=hg, g=g):
                        epilogue(ob, h, 0, 97, True, True, hg)
                        if hg == 3 and use_sel:
                            selection(g)
                    units.append(dict(steps=[st], pre=pre, epi=epi_c))
                    steps = []
                    for kt in range(max(0, 4 * qc - 4), 4 * qc + 4):
                        k0 = kt * 128
                        delta = q0 - k0
                        c0 = max(0, -delta)
                        c1 = min(512, 640 - delta)
                        mms = [(c0, c1, kwaug[:, g, k0:k0 + 128], Qc[qi][:, c0:c1], [b_k, b_Qc[qi]])]
                        near_corr(mms, h, delta)
                        if delta >= 128:
                            f0 = 512 - delta
                            mms.append((f0, f0 + 128, ident, we, [b_c, b_ident]))
                        steps.append(dict(M=128, c0=c0, c1=c1, mms=mms, v=(vwx[:, kt, g, :], [b_v]), ncol=65))
                    units.append(dict(steps=steps, pre=None, epi=(lambda ob, h=h, hg=hg: epilogue(ob, h, 2, 65, False, False, hg))))
            for g in range(G):
                for hg in range(4):
                    h = g * 4 + hg
                    qi = qsc[0] % 3
                    qsc[0] += 1

                    def pre(h=h, qi=qi, g=g):
                        P.dma("sp", Qsw[qi][0:64, :], qT_d[h * 64:(h + 1) * 64, q0:q0 + 512], [b_qTd, b_Qsw[qi]], [b_Qsw[qi]], "nQs%d" % qi)
                        if use_sel:
                            cp("dve", Qsw[qi][64:96, :], selTs[64:96, g, :], [b_selTs[g], b_Qsw[qi]], [b_Qsw[qi]])
                    steps = []
                    for kt in range(4 * (qc + 1)):
                        k0 = kt * 128
                        delta = q0 - k0
                        c0 = max(0, -delta)
                        mms = [(c0, 512, ksaug[:, g, k0:k0 + 128], Qsw[qi][:, c0:512], [b_k, b_Qsw[qi]])]
                        near_corr(mms, h, delta)
                        steps.append(dict(M=128, c0=c0, c1=512, mms=mms, v=(vsx[:, kt, g, :], [b_v]), ncol=65))
                    units.append(dict(steps=steps, pre=pre, epi=(lambda ob, h=h, hg=hg: epilogue(ob, h, 1, 65, False, False, hg))))
            run_units(units, PT, b_PT, (0, 1), (2, 3))
            cp("dve", o_bf, o_acc, [b_oacc], [b_obf])
            outproj_residual(s, qc, o_bf, b_obf, wout, b_wout, src_x, oT, b_oT, xres, b_xres, (6, 7))
        P.barrier()
        A.reset(m0)

    def fox_layer(s, src_x):
        m0 = A.mark()
        hT, b_hT = load_hT(s, src_x, 1)
        wring = [A.alloc([128, 8, 512], BF16) for _ in range(3)]
        b_wring = [Buf() for _ in range(3)]
        stg = [A.alloc([128, 512], BF16) for _ in range(4)]
        b_stg = [Buf() for _ in range(4)]
        fm = []
        for i in range(8):
            fm.append((i * 128, qT_d[i * 128:(i + 1) * 128, :], b_qTd, 0.125))
        for i in range(8):
            fm.append((1024 + i * 128, kT_d[i * 128:(i + 1) * 128, :], b_kTd, None))
        tm = [(2048, 512, v_d[:, 0:512], b_vd, "copy"), (2560, 512, v_d[:, 512:1024], b_vd, "copy")]
        wf = A.alloc([128, 8, H], BF16)
        b_wf = Buf()
        P.dma("sp", wf, fwinb.rearrange("(kc p) n -> p kc n", p=128)[:, :, 3072:3088], [b_fwinb], [b_wf], "wf")
        fl = A.alloc([H, S], F32)
        b_fl = Buf()
        bfv = A.alloc([H, 1], F32)
        P.dma("sp", bfv, fox_b_f.rearrange("o h -> h o"), [], [b_fl], "bfv", allow_slow_non_contiguous=True)
        for tc in range(4):
            bk = 6 + tc % 2
            for kc in range(8):
                mm(banks[bk][0:H, :], wf[:, kc, :], hT[:, kc, tc * 512:(tc + 1) * 512], kc == 0, kc == 7, [b_wf, b_hT[tc]], [b_bank[bk]])
            act(fl[:, tc * 512:(tc + 1) * 512], banks[bk][0:H, :], AF.Identity, [b_bank[bk], b_fl], [b_fl], bias=bfv[:, 0:1])
        az = A.alloc([H, S], F32)
        mz = A.alloc([H, S], F32)
        onesr = A.alloc([H, S], F32)
        cpp = A.alloc([H, 6, S], BF16)
        P.op("pool", lambda e: e.memset(onesr, 1.0), [], [b_fl])
        act(az, fl, AF.Abs, [b_fl], [b_fl])
        act(az, az, AF.Exp, [b_fl], [b_fl], scale=-1.0)
        act(az, az, AF.Ln, [b_fl], [b_fl], bias=1.0)
        ts("dve", mz, fl, 0.0, None, ALU.min, None, [b_fl], [b_fl])
        tt("dve", mz, mz, az, ALU.subtract, [b_fl], [b_fl])
        P.op("dve", lambda e: e.tensor_tensor_scan(out=az, data0=onesr, data1=mz, initial=0.0, op0=ALU.mult, op1=ALU.add), [b_fl], [b_fl])
        cp("dve", cpp[:, 0, :], az, [b_fl], [b_fl])
        tt("dve", mz, az, cpp[:, 0, :], ALU.subtract, [b_fl], [b_fl])
        cp("dve", cpp[:, 1, :], mz, [b_fl], [b_fl])
        tt("dve", mz, mz, cpp[:, 1, :], ALU.subtract, [b_fl], [b_fl])
        cp("dve", cpp[:, 2, :], mz, [b_fl], [b_fl])
        ts("dve", cpp[:, 3:6, :], cpp[:, 0:3, :], -1.0, None, ALU.mult, None, [b_fl], [b_fl])
        P.dma("pool", cpart_d.rearrange("i h t -> h i t"), cpp, [b_fl], [b_cpd], "cpd")
        proj_phase(hT, b_hT, fwinb, b_fwinb, 3072, fm, tm, wring, b_wring, stg, b_stg, (6, 7))
        P.barrier()
        A.reset(m0)

        cm = A.alloc([128, 128], BF16)
        b_c = Buf()
        P.dma("sp", cm, cst["c_cm"], [], [b_c], "fc0")
        wout = A.alloc([128, 8, D], BF16)
        b_wout = Buf()
        P.dma("sp", wout, fwoutb.rearrange("(kc p) n -> p kc n", p=128), [b_fwoutb], [b_wout], "fc1")
        vfx = A.alloc([128, 16, H, 65], BF16)
        b_v = Buf()
        P.op("pool", lambda e: e.memset(vfx, 1.0), [], [b_v])
        for kt in range(16):
            P.dma("sp", vfx[:, kt, :, 0:64], v_d[kt * 128:(kt + 1) * 128, :].rearrange("p (h d) -> p h d", d=64), [b_vd, b_v], [b_v], "fc3")
        Kh = [A.alloc([128, S], BF16) for _ in range(2)]
        Qh = [A.alloc([128, 512], BF16) for _ in range(3)]
        b_Kh = [Buf() for _ in range(2)]
        b_Qh = [Buf() for _ in range(3)]
        for i in range(2):
            P.op("pool", lambda e, i=i: e.memset(Kh[i], 0.0), [], [b_Kh[i]])
            P.op("pool", lambda e, i=i: e.memset(Kh[i][64:67, :], 1.0), [b_Kh[i]], [b_Kh[i]])
        for i in range(3):
            P.op("pool", lambda e, i=i: e.memset(Qh[i], 0.0), [], [b_Qh[i]])
            P.op("pool", lambda e, i=i: e.memset(Qh[i][96:99, :], 1.0), [b_Qh[i]], [b_Qh[i]])
        PT = [A.alloc([128, 512], BF16) for _ in range(3)]
        b_PT = [Buf() for _ in range(3)]
        o_bf = A.alloc([128, 16, D], BF16)
        b_obf = Buf()
        oT = A.alloc([128, 8, 512], BF16)
        b_oT = Buf()
        xres = [A.alloc([128, D], F32) for _ in range(2)]
        b_xres = [Buf() for _ in range(2)]
        sm = [A.alloc([128, 4], F32) for _ in range(4)]
        b_sm = [Buf() for _ in range(4)]
        ucnt = [0]
        qcnt = [0]
        units = []
        for h in range(H):
            ki = h % 2
            for qc in range(4):
                q0 = qc * 512
                qi = qcnt[0] % 3
                qcnt[0] += 1

                def pre(h=h, ki=ki, qc=qc, q0=q0, qi=qi):
                    if qc == 0:
                        P.dma("sp", Kh[ki][0:64, :], kT_d[h * 64:(h + 1) * 64, :], [b_kTd, b_Kh[ki]], [b_Kh[ki]], "fK%d" % ki)
                        P.dma("sp", Kh[ki][96:99, :], cpart_d[3:6, h, :], [b_cpd, b_Kh[ki]], [b_Kh[ki]], "fK%d" % ki)
                    P.dma("sp", Qh[qi][0:64, :], qT_d[h * 64:(h + 1) * 64, q0:q0 + 512], [b_qTd, b_Qh[qi]], [b_Qh[qi]], "fQ%d" % qi)
                    P.dma("sp", Qh[qi][64:67, :], cpart_d[0:3, h, q0:q0 + 512], [b_cpd, b_Qh[qi]], [b_Qh[qi]], "fQ%d" % qi)
                steps = []
                for kt in range(4 * (qc + 1)):
                    k0 = kt * 128
                    delta = q0 - k0
                    c0 = max(0, -delta)
                    mms = [(c0, 512, Kh[ki][:, k0:k0 + 128], Qh[qi][:, c0:512], [b_Kh[ki], b_Qh[qi]])]
                    if delta <= 0:
                        mms.append((c0, c0 + 128, ident, cm, [b_c, b_ident]))
                    steps.append(dict(M=128, c0=c0, c1=512, mms=mms, v=(vfx[:, kt, h, :], [b_v]), ncol=65))

                def epi(ob, h=h, qc=qc):
                    k = ucnt[0] % 4
                    ucnt[0] += 1
                    po = banks[ob][:, 0:260].rearrange("p (j c) -> p j c", c=65)
                    P.op("dve", lambda e: e.reciprocal(out=sm[k], in_=po[:, :, 64]), [b_bank[ob]], [b_sm[k]])
                    rg = sm[k][:, :].unsqueeze(2).broadcast_to([128, 4, 64])
                    tt("dve", o_bf[:, qc * 4:(qc + 1) * 4, h * 64:(h + 1) * 64], po[:, :, 0:64], rg, ALU.mult, [b_bank[ob], b_sm[k]], [b_obf])
                units.append(dict(steps=steps, epi=epi, pre=pre))
        run_units(units, PT, b_PT, (0, 1), (2, 3))
        for qc in range(4):
            outproj_residual(s, qc, o_bf[:, qc * 4:(qc + 1) * 4, :], b_obf, wout, b_wout, src_x, oT, b_oT, xres, b_xres, (6, 7))
        P.barrier()
        A.reset(m0)

    def mlp_layer(l, src_x, last):
        m0 = A.mark()
        w2s = A.alloc([128, 32, D], BF16)
        b_w2s = Buf()
        for q4 in range(4):
            P.dma("sp", w2s[:, q4 * 8:(q4 + 1) * 8, :], w2b[l].rearrange("(fc p) n -> p fc n", p=128)[:, q4 * 8:(q4 + 1) * 8, :],
                  [b_w2b[l]], [b_w2s], "w2s")
        NW1 = 2
        w1s = [A.alloc([128, 8, 512], BF16) for _ in range(NW1)]
        b_w1s = [Buf() for _ in range(NW1)]
        aT = A.alloc([128, 32, 512], BF16)
        b_aT = [Buf() for _ in range(32)]
        hTc = [A.alloc([128, 8, 512], BF16) for _ in range(2)]
        b_hTc = [[Buf() for _ in range(4)] for _ in range(2)]
        xt = [A.alloc([128, D], F32) for _ in range(6)]
        b_xt = [Buf() for _ in range(6)]
        rtmp = [A.alloc([128, 512], F32) for _ in range(2)]
        b_rtmp = [Buf() for _ in range(2)]
        yo = [A.alloc([128, D], F32) for _ in range(2)]
        b_yo = [Buf() for _ in range(2)]
        w1cnt = p1cnt = p2cnt = xcnt = 0
        for s in range(nseq):
            for c in range(4):
                cb = (s * 4 + c) % 2
                xis = []
                for j in range(4):
                    t16 = c * 4 + j
                    xi = xcnt % 6
                    xcnt += 1
                    xis.append(xi)
                    P.dma("sp", xt[xi], src_x[s, t16 * 128:(t16 + 1) * 128, :], [b_xs[s][t16]], [b_xt[xi]], "mxt%d" % xi)
                    norm_T(xt[xi], b_xt[xi], 2 + l, hTc[cb][:, :, j * 128:(j + 1) * 128], b_hTc[cb][j])
                for blk in range(8):
                    wi = w1cnt % NW1
                    w1cnt += 1
                    P.dma("sp", w1s[wi], w1b[l].rearrange("(kc p) n -> p kc n", p=128)[:, :, blk * 512:(blk + 1) * 512],
                          [b_w1b[l]], [b_w1s[wi]], "w1s%d" % wi)
                    for f4 in range(4):
                        fc = blk * 4 + f4
                        pi = p1cnt % 2
                        p1cnt += 1
                        for kc in range(8):
                            mm(banks[pi][:, :], w1s[wi][:, kc, f4 * 128:(f4 + 1) * 128], hTc[cb][:, kc, :], kc == 0, kc == 7,
                               [b_w1s[wi]] + b_hTc[cb], [b_bank[pi]])
                        act(rtmp[pi], banks[pi][:, :], AF.Relu, [b_bank[pi]], [b_rtmp[pi]])
                        tt("dve", aT[:, fc, :], rtmp[pi], rtmp[pi], ALU.mult, [b_rtmp[pi]], [b_aT[fc]])
                for j in range(4):
                    t16 = c * 4 + j
                    xi = xis[j]
                    for nh in range(2):
                        pi = 2 + p2cnt % 2
                        p2cnt += 1
                        for fc in range(32):
                            mm(banks[pi][:, :], aT[:, fc, j * 128:(j + 1) * 128], w2s[:, fc, nh * 512:(nh + 1) * 512], fc == 0, fc == 31,
                               [b_aT[fc], b_w2s], [b_bank[pi]])
                        tt("dve", xt[xi][:, nh * 512:(nh + 1) * 512], banks[pi][:, :], xt[xi][:, nh * 512:(nh + 1) * 512], ALU.add,
                           [b_bank[pi], b_xt[xi]], [b_xt[xi]])
                    if not last:
                        P.dma("pool", xs[s, t16 * 128:(t16 + 1) * 128, :], xt[xi], [b_xt[xi]], [b_xs[s][t16]], "mxst%d" % xi)
                    else:
                        rs, b_r, jj = rstd_of(xt[xi], b_xt[xi])
                        yi = t16 % 2
                        P.op("dve", lambda e, xi=xi, yi=yi, rs=rs: e.scalar_tensor_tensor(
                            out=yo[yi], in0=xt[xi], scalar=rs[:, 0:1], in1=gfin, op0=ALU.mult, op1=ALU.mult),
                            [b_xt[xi], b_r, b_gfin], [b_yo[yi]])
                        P.dma("pool", out[s, t16 * 128:(t16 + 1) * 128, :], yo[yi], [b_yo[yi]], [], "yo%d" % yi, is_output=True)
        P.barrier()
        A.reset(m0)

    if "nsa" in parts:
        nsa_setup_tables()
        for blk in range(6):
            c0 = blk * 512
            c1 = min(NSA_IN, c0 + 512)
            P.dma("pool", nwinb[:, c0:c1], nsa_w_in[0][:, c0:c1], [], [b_nwinb_blk[blk]], "cw_a%d" % blk)
        cast_w(nwoutb, nsa_w_out[0], D, b_nwoutb, "cw_b", 2)
    if "mlp" in parts:
        cast_w(w1b[0], mlp_w1[0], D, b_w1b[0], "cw_c")
        cast_w(w2b[0], mlp_w2[0], DFF, b_w2b[0], "cw_d")
    if "fox" in parts:
        cast_w(fwinb, fox_w_in[0], D, b_fwinb, "cw_e")
        cast_w(fwoutb, fox_w_out[0], D, b_fwoutb, "cw_f", 2)
    if "mlp" in parts and depth > 1:
        cast_w(w1b[1], mlp_w1[1], D, b_w1b[1], "cw_g")
        cast_w(w2b[1], mlp_w2[1], DFF, b_w2b[1], "cw_h")

    cur = x_in
    for l in range(depth):
        if l == 0 and "nsa" in parts:
            for s in range(nseq):
                nsa_layer(s, cur)
            cur = xs
        if l == 1 and "fox" in parts:
            for s in range(nseq):
                fox_layer(s, cur)
            cur = xs
        if "mlp" in parts:
            mlp_layer(l, cur, last=(l == depth - 1))
            cur = xs
    P.emit()
    return nc, es


_CACHE = {}
IN_NAMES = ["rel_bias", "norm_mix", "norm_mlp", "nsa_w_in", "nsa_pe_k", "nsa_wk1", "nsa_wk2", "nsa_pe_v", "nsa_wv1", "nsa_wv2",
            "nsa_w_out", "fox_w_in", "fox_b_f", "fox_w_out", "mlp_w1", "mlp_w2", "final_norm"]


def kernel(**inputs):
    n = 8
    x = np.ascontiguousarray(np.asarray(inputs["x"], dtype=np.float32))
    nseq = x.shape[0] // n
    if "nc" not in _CACHE:
        _CACHE["nc"] = build(nseq=nseq)
    nc, _ = _CACHE["nc"]
    consts = host_consts()
    shared = {k: np.ascontiguousarray(np.asarray(inputs[k], dtype=np.float32)) for k in IN_NAMES}
    shared.update(consts)
    in_maps = []
    for c in range(n):
        m = dict(shared)
        m["x"] = x[c * nseq:(c + 1) * nseq]
        in_maps.append(m)
    res = run_bass_kernel_spmd(nc, in_maps, core_ids=list(range(n)))
    return np.concatenate([r["out"] for r in res.results], axis=0).astype(np.float32)
```

```python
import math
import numpy as np
import ml_dtypes
from contextlib import ExitStack
import concourse.bass as bass
import concourse.mybir as mybir
from concourse.bass_utils import run_bass_kernel_spmd

F32 = mybir.dt.float32
BF16 = mybir.dt.bfloat16
AF = mybir.ActivationFunctionType
ALU = mybir.AluOpType

S = 2048
D = 1024
DFF = 4096
H = 16
DH = 64
G = 4
NSA_IN = 2608
FOX_IN = 3088
EPS = 1e-6
NEG = -30000.0
T1W = 1600
T1Z = 600
SEM_ROLL = 30000


class Buf:
    __slots__ = ("name", "w", "r")

    def __init__(self, name=""):
        self.name = name
        self.w = {}
        self.r = {}


class Prog:
    ENG = ("pe", "act", "dve", "pool", "sp")

    def __init__(self, nc, es):
        self.nc = nc
        self.es = es
        self.ops = {e: [] for e in self.ENG}
        self.nsem = 0
        self.esem = {e: self._newsem() for e in ("pe", "act", "dve", "pool")}
        self.ecnt = {e: 0 for e in self.esem}
        self.waited = {e: {} for e in self.ENG}
        self.dsem = {}
        self.dcnt = {}
        self.allsems = {}
        self.rot = {}
        self.out_tokens = []

    def _newsem(self):
        self.nsem += 1
        return self.es.enter_context(self.nc.semaphore("sm%d" % self.nsem))

    def _deps(self, eng, reads, writes):
        need = {}
        pe_sem = self.esem["pe"]

        def add(sem, val, src):
            if src == "pe" and eng == "pe" and sem is pe_sem:
                return
            if need.get(sem, 0) < val:
                need[sem] = val

        for b in reads:
            for sem, (val, src) in b.w.items():
                add(sem, val, src)
        for b in writes:
            for sem, (val, src) in b.w.items():
                add(sem, val, src)
            for sem, (val, src) in b.r.items():
                add(sem, val, src)
        waits = []
        wd = self.waited[eng]
        for sem, val in need.items():
            if wd.get(sem, 0) < val:
                wd[sem] = val
                waits.append((sem, val))
        return waits

    def _commit(self, tok, reads, writes):
        sem, val, src = tok
        self.allsems[sem] = val
        for b in reads:
            if b.r.get(sem, (0, None))[0] < val:
                b.r[sem] = (val, src)
        for b in writes:
            if b.w.get(sem, (0, None))[0] < val:
                b.w[sem] = (val, src)

    def op(self, eng, fn, reads=(), writes=()):
        if self.ecnt[eng] >= SEM_ROLL:
            self.esem[eng] = self._newsem()
            self.ecnt[eng] = 0
        waits = self._deps(eng, reads, writes)
        self.ecnt[eng] += 1
        sem = self.esem[eng]
        self.ops[eng].append((waits, fn, sem, 1))
        self._commit((sem, self.ecnt[eng], eng), reads, writes)

    ROT_FAM = (("nc", 6), ("nb", 4), ("t1a", 2), ("gcol", 2), ("fc3", 4), ("w2s", 4), ("cw_", 4))

    def dma(self, q, out, in_, reads, writes, key, is_output=False, **kw):
        for fam, nrot in self.ROT_FAM:
            if key.startswith(fam):
                r = self.rot.get(fam, 0)
                self.rot[fam] = r + 1
                key = "%s~%d" % (fam, r % nrot)
                break
        if key in self.dsem and self.dcnt[key] >= 2000:
            del self.dsem[key]
        if key not in self.dsem:
            self.dsem[key] = self._newsem()
            self.dcnt[key] = 0
        waits = self._deps(q, reads, writes)
        sem = self.dsem[key]
        if self.dcnt[key] > 0 and self.waited[q].get(sem, 0) < 16 * self.dcnt[key]:
            self.waited[q][sem] = 16 * self.dcnt[key]
            waits.append((sem, 16 * self.dcnt[key]))
        self.dcnt[key] += 1
        self.ops[q].append((waits, (lambda e: e.dma_start(out=out, in_=in_, **kw)), sem, 16))
        tok = (sem, 16 * self.dcnt[key], q)
        self._commit(tok, reads, writes)
        if is_output:
            self.out_tokens.append(tok)

    def barrier(self):
        snap = dict(self.allsems)
        for eng in self.ENG:
            wd = self.waited[eng]
            waits = []
            for sem, val in snap.items():
                if wd.get(sem, 0) < val:
                    wd[sem] = val
                    waits.append((sem, val))
            if waits:
                self.ops[eng].append((waits, None, None, 0))

    def emit(self):
        nc = self.nc
        fin = {}
        for sem, val, _ in self.out_tokens:
            fin[sem] = max(fin.get(sem, 0), val)
        with nc.Block() as block:
            def run(eng_name):
                def body(e):
                    for waits, fn, sem, inc in self.ops[eng_name]:
                        for s, v in waits:
                            e.wait_ge(s, v)
                        if fn is not None:
                            fn(e).then_inc(sem, inc)
                    if eng_name == "sp":
                        for s, v in fin.items():
                            e.wait_ge(s, v)
                return body
            block.tensor(run("pe"))
            block.scalar(run("act"))
            block.vector(run("dve"))
            block.gpsimd(run("pool"))
            block.sync(run("sp"))


DTSIZE = {F32: 4, BF16: 2}


class Arena:
    def __init__(self, base_ap, nwords):
        self.base = base_ap
        self.n = nwords
        self.off = 0

    def mark(self):
        return self.off

    def reset(self, m):
        self.off = m

    def alloc(self, shape, dt):
        p = shape[0]
        free = list(shape[1:])
        nel = int(np.prod(free))
        words = (nel * DTSIZE[dt] + 3) // 4
        words = (words + 7) // 8 * 8
        assert self.off + words <= self.n, ("arena overflow", self.off, words, self.n)
        v = self.base[0:p, self.off:self.off + words]
        self.off += words
        if dt != F32:
            v = v.bitcast(dt)
        v = v[:, 0:nel]
        if len(free) == 2:
            v = v.rearrange("p (a b) -> p a b", b=free[1])
        elif len(free) == 3:
            v = v.rearrange("p (a b c) -> p a b c", b=free[1], c=free[2])
        return v


def rel_bucket_np(d):
    d = np.asarray(d)
    n = np.maximum(d, 0)
    nf = np.maximum(n, 1).astype(np.float32)
    large = 16 + (np.log(nf / np.float32(16)) / np.float32(math.log(8.0)) * np.float32(16)).astype(np.int32)
    large = np.minimum(large, 31)
    return np.where(n < 16, n, large)


def host_consts():
    bf = ml_dtypes.bfloat16
    c = {}
    d = np.arange(T1W) - T1Z
    b = np.where(d < 0, 32, np.where(d < 128, rel_bucket_np(d), 31))
    oh = np.zeros((33, T1W), np.float32)
    oh[b, np.arange(T1W)] = 1.0
    c["c_oh1d"] = oh.astype(bf)
    selp = np.zeros((64, 4, 127), np.float32)
    for j in range(4):
        for rp in range(64):
            cc = 32 * (j - 1) + 63 - rp
            if 0 <= cc < 127:
                selp[rp, j, cc] = 1.0
    c["c_selp"] = selp.astype(bf)
    c["c_jmat"] = np.ascontiguousarray(np.eye(128, dtype=np.float32)[::-1]).astype(bf)
    ek = np.zeros((32, S), np.float32)
    ek[np.arange(S) // 64, np.arange(S)] = 1.0
    c["c_ek"] = ek.astype(bf)
    addm = np.zeros((128, 16, 32), np.float32)
    mulm = np.ones((128, 16, 32), np.float32)
    for tt in range(16):
        for p in range(128):
            qb = (tt * 128 + p) // 64
            for j in range(32):
                rel = qb - j
                forced = (j == 0) or (0 <= rel < 2)
                vis = rel >= 0
                if not vis:
                    addm[p, tt, j] = -1e30
                    mulm[p, tt, j] = 0.0
                elif forced:
                    addm[p, tt, j] = 1e9
                    mulm[p, tt, j] = 0.0
    c["c_addm"] = addm
    c["c_mulm"] = mulm
    ovl = np.zeros((127, 32), np.float32)
    for cc in range(127):
        for cell in (cc, cc + 1):
            ovl[cc, cell // 4] += 1.0
    c["c_ovl"] = ovl.astype(bf)
    kk = np.arange(128)[:, None]
    qq = np.arange(128)[None, :]
    c["c_we"] = np.where(qq >= kk, NEG, 0.0).astype(np.float32).astype(bf)
    c["c_cm"] = np.where(qq < kk, NEG, 0.0).astype(np.float32).astype(bf)
    return c


CONST_SPECS = {
    "c_oh1d": ([33, T1W], BF16), "c_selp": ([64, 4, 127], BF16), "c_jmat": ([128, 128], BF16),
    "c_ek": ([32, S], BF16), "c_addm": ([128, 16, 32], F32), "c_mulm": ([128, 16, 32], F32),
    "c_ovl": ([127, 32], BF16), "c_we": ([128, 128], BF16), "c_cm": ([128, 128], BF16),
}


def build(nseq=2, parts=("nsa", "fox", "mlp"), depth=2, debug=()):
    nc = bass.Bass("TRN2", target_bir_lowering=False)
    es = ExitStack()
    P = Prog(nc, es)

    def dram_in(name, shape, dt=F32):
        return nc.dram_tensor(name, list(shape), dt, kind="ExternalInput").ap()

    def dram_tmp(name, shape, dt):
        return nc.dram_tensor(name, list(shape), dt, kind=("ExternalOutput" if name in debug else "Internal")).ap()

    x_in = dram_in("x", [nseq, S, D])
    out = nc.dram_tensor("out", [nseq, S, D], F32, kind="ExternalOutput").ap()
    rel_bias = dram_in("rel_bias", [32, H])
    norm_mix = dram_in("norm_mix", [2, D])
    norm_mlp = dram_in("norm_mlp", [2, D])
    nsa_w_in = dram_in("nsa_w_in", [1, D, NSA_IN])
    nsa_pe_k = dram_in("nsa_pe_k", [1, 32, DH])
    nsa_wk1 = dram_in("nsa_wk1", [1, 32 * DH, DH])
    nsa_wk2 = dram_in("nsa_wk2", [1, DH, DH])
    nsa_pe_v = dram_in("nsa_pe_v", [1, 32, DH])
    nsa_wv1 = dram_in("nsa_wv1", [1, 32 * DH, DH])
    nsa_wv2 = dram_in("nsa_wv2", [1, DH, DH])
    nsa_w_out = dram_in("nsa_w_out", [1, D, D])
    fox_w_in = dram_in("fox_w_in", [1, D, FOX_IN])
    fox_b_f = dram_in("fox_b_f", [1, H])
    fox_w_out = dram_in("fox_w_out", [1, D, D])
    mlp_w1 = dram_in("mlp_w1", [2, D, DFF])
    mlp_w2 = dram_in("mlp_w2", [2, DFF, D])
    final_norm = dram_in("final_norm", [D])
    cst = {k: dram_in(k, sh, dt) for k, (sh, dt) in CONST_SPECS.items()}

    xs = dram_tmp("xs", [nseq, S, D], F32)
    b_xs = [[Buf() for _ in range(16)] for _ in range(nseq)]
    w1b = dram_tmp("w1b", [2, D, DFF], BF16)
    w2b = dram_tmp("w2b", [2, DFF, D], BF16)
    nwinb = dram_tmp("nwinb", [D, NSA_IN], BF16)
    nwoutb = dram_tmp("nwoutb", [D, D], BF16)
    fwinb = dram_tmp("fwinb", [D, FOX_IN], BF16)
    fwoutb = dram_tmp("fwoutb", [D, D], BF16)
    b_w1b = [Buf(), Buf()]
    b_w2b = [Buf(), Buf()]
    b_nwinb, b_nwoutb, b_fwinb, b_fwoutb = Buf(), Buf(), Buf(), Buf()
    b_nwinb_blk = [Buf() for _ in range(6)]
    t1_d = dram_tmp("t1_d", [H, T1W], BF16)
    b_t1d = Buf()
    qT_d = dram_tmp("qT_d", [D, S], BF16)
    kT_d = dram_tmp("kT_d", [D, S], BF16)
    v_d = dram_tmp("v_d", [S, D], BF16)
    gates_d = dram_tmp("gates_d", [S, 48], F32)
    cpart_d = dram_tmp("cpart_d", [6, H, S], BF16)
    b_qTd, b_kTd, b_vd, b_gd, b_cpd = Buf(), Buf(), Buf(), Buf(), Buf()

    NW = 51200
    arena_t = es.enter_context(nc.sbuf_tensor("arena", [128, NW], F32))
    A = Arena(arena_t, NW)
    banks = [es.enter_context(nc.psum_tensor("bank%d" % i, [128, 512], F32)) for i in range(8)]
    b_bank = [Buf("bank%d" % i) for i in range(8)]

    def bank_bf(i):
        return banks[i][:].bitcast(BF16)

    def mm(o, lhsT, rhs, start, stop, reads, writes, skip=False):
        P.op("pe", lambda e: e.matmul(o, lhsT, rhs, start=start, stop=stop, skip_group_check=skip), reads, writes)

    def tr(o, i, idn, reads, writes):
        P.op("pe", lambda e: e.transpose(out=o, in_=i, identity=idn), reads, writes)

    def act(o, i, func, reads, writes, **kw):
        P.op("act", lambda e: e.activation(out=o, in_=i, func=func, **kw), reads, writes)

    def tt(eng, o, a, b, op, reads, writes):
        P.op(eng, lambda e: e.tensor_tensor(out=o, in0=a, in1=b, op=op), reads, writes)

    def ts(eng, o, a, s1, s2, op0, op1, reads, writes):
        if s2 is None:
            P.op(eng, lambda e: e.tensor_scalar(out=o, in0=a, scalar1=s1, scalar2=None, op0=op0), reads, writes)
        else:
            P.op(eng, lambda e: e.tensor_scalar(out=o, in0=a, scalar1=s1, scalar2=s2, op0=op0, op1=op1), reads, writes)

    def cp(eng, o, i, reads, writes):
        P.op(eng, lambda e: e.tensor_copy(out=o, in_=i), reads, writes)

    ev_cnt = [0]

    def evac(o, i, reads, writes, scale=None):
        ev_cnt[0] += 1
        if ev_cnt[0] % 2 == 0:
            if scale is None:
                act(o, i, AF.Copy, reads, writes)
            else:
                act(o, i, AF.Copy, reads, writes, scale=float(scale))
        else:
            if scale is None:
                cp("dve", o, i, reads, writes)
            else:
                ts("dve", o, i, float(scale), None, ALU.mult, None, reads, writes)

    ident = A.alloc([128, 128], BF16)
    b_ident = Buf()
    P.op("pool", lambda e: e.memset(ident, 0.0), [], [b_ident])
    P.op("pool", lambda e: e.affine_select(out=ident, in_=ident, pattern=[[-1, 128]], compare_op=ALU.not_equal,
                                           fill=1.0, base=0, channel_multiplier=1), [b_ident], [b_ident])
    gcol = A.alloc([128, 4, 8], F32)
    b_gcol = Buf()
    for i, src in enumerate([norm_mix[0], norm_mix[1], norm_mlp[0], norm_mlp[1]]):
        P.dma("sp", gcol[:, i, :], src.rearrange("(k p) -> p k", p=128), [], [b_gcol], "gcol",
              allow_slow_non_contiguous=True)
    gfin = A.alloc([128, D], F32)
    b_gfin = Buf()
    P.dma("sp", gfin, final_norm.partition_broadcast(128), [], [b_gfin], "gfin")
    ones_f = A.alloc([128, 512], F32)
    b_ones = Buf()
    P.op("pool", lambda e: e.memset(ones_f, 1.0), [], [b_ones])

    NST = 4
    st_ss = [A.alloc([128, 1], F32) for _ in range(NST)]
    st_rs = [A.alloc([128, 1], F32) for _ in range(NST)]
    b_ss = [Buf() for _ in range(NST)]
    b_rs = [Buf() for _ in range(NST)]
    junk = [A.alloc([128, D], BF16) for _ in range(2)]
    b_junk = [Buf() for _ in range(2)]
    xn = [A.alloc([128, D], BF16) for _ in range(2)]
    b_xn = [Buf() for _ in range(2)]
    norm_i = [0]
    PT_BANKS = (4, 5)

    def rstd_of(xt_ap, b_xt):
        i = norm_i[0]
        norm_i[0] += 1
        k = i % NST
        j = i % 2
        act(junk[j], xt_ap, AF.Square, [b_xt], [b_junk[j], b_ss[k]], accum_out=st_ss[k])
        ts("dve", st_rs[k], st_ss[k], 1.0 / D, EPS, ALU.mult, ALU.add, [b_ss[k]], [b_rs[k]])
        act(st_rs[k], st_rs[k], AF.Sqrt, [b_rs[k]], [b_rs[k]])
        P.op("dve", lambda e: e.reciprocal(out=st_rs[k], in_=st_rs[k]), [b_rs[k]], [b_rs[k]])
        return st_rs[k], b_rs[k], j

    def norm_T(xt_ap, b_xt, gi, dst_ap, b_dst):
        rs, b_r, j = rstd_of(xt_ap, b_xt)
        act(xn[j], xt_ap, AF.Identity, [b_xt, b_r], [b_xn[j]], scale=rs[:, 0:1])
        bk = PT_BANKS[j]
        pv = bank_bf(bk).rearrange("p (a b) -> p a b", b=128)
        for kc in range(8):
            tr(pv[:, kc, :], xn[j][:, kc * 128:(kc + 1) * 128], ident, [b_xn[j], b_ident], [b_bank[bk]])
        gb = gcol[:, gi, :].unsqueeze(2).broadcast_to([128, 8, 128])
        tt("dve", dst_ap, pv, gb, ALU.mult, [b_bank[bk], b_gcol], [b_dst])

    def cast_w(dst, src, rows, b, key, nsplit=4):
        step = rows // nsplit
        for r in range(nsplit):
            P.dma("pool", dst[r * step:(r + 1) * step, :], src[r * step:(r + 1) * step, :], [], [b], key)

    base_mark = A.mark()

    def run_units(units, PT, b_PT, pS_banks, pO_banks):
        flat = []
        for ui, u in enumerate(units):
            for si, st in enumerate(u["steps"]):
                flat.append((ui, si, st, u))
        nPT = len(PT)
        called = set()

        def call_pre(ui):
            if ui < len(units) and ui not in called:
                called.add(ui)
                if units[ui].get("pre") is not None:
                    units[ui]["pre"]()

        call_pre(0)

        def qk(idx):
            ui, si, st, u = flat[idx]
            if si == 0:
                call_pre(ui + 1)
            bk = pS_banks[idx % len(pS_banks)]
            M = st["M"]
            n = len(st["mms"])
            for mi, (oc0, oc1, lhsT, rhs, reads) in enumerate(st["mms"]):
                mm(banks[bk][0:M, oc0:oc1], lhsT, rhs, mi == 0, mi == n - 1, reads, [b_bank[bk]], skip=True)
            sl = idx % nPT
            act(PT[sl][0:M, st["c0"]:st["c1"]], banks[bk][0:M, st["c0"]:st["c1"]], AF.Exp, [b_bank[bk]], [b_PT[sl]])

        def pv(idx):
            ui, si, st, u = flat[idx]
            ob = pO_banks[ui % len(pO_banks)]
            M = st["M"]
            ncol = st["ncol"]
            sl = idx % nPT
            vr, vreads = st["v"]
            po = banks[ob][:, 0:4 * ncol].rearrange("p (j c) -> p j c", c=ncol)
            first = (si == 0)
            for j in range(4):
                if st["c0"] <= 128 * j and 128 * (j + 1) <= st["c1"]:
                    mm(po[:, j, :], PT[sl][0:M, 128 * j:128 * (j + 1)], vr, first, True, [b_PT[sl]] + vreads, [b_bank[ob]], skip=True)
                    first = False

        LA = 2
        for idx in range(len(flat)):
            if idx == 0:
                for k in range(min(LA, len(flat))):
                    qk(k)
            if idx + LA < len(flat):
                qk(idx + LA)
            pv(idx)
            ui, si, st, u = flat[idx]
            if si == len(u["steps"]) - 1:
                u["epi"](pO_banks[ui % len(pO_banks)])

    def proj_phase(hT, b_hT, wsrc, b_wsrc, ncols_total, fm_list, tm_list, wring, b_wring, stg, b_stg, pbanks, gstg=None, b_gstg=None, w32=None, mode=None):
        nblk = (ncols_total + 511) // 512
        cnt = [0, 0]
        def load_blk(blk):
            bc0 = blk * 512
            bw = min(512, ncols_total - bc0)
            wi = blk % len(wring)
            if w32 is not None:
                w32src, w32stage, b_w32stage = w32
                wj = blk % len(w32stage)
                P.dma("sp", w32stage[wj][:, :, 0:bw], w32src.rearrange("(kc p) n -> p kc n", p=128)[:, :, bc0:bc0 + bw],
                      [], [b_w32stage[wj]], "w32s%d" % wj)
                act(wring[wi][:, :, 0:bw], w32stage[wj][:, :, 0:bw], AF.Copy, [b_w32stage[wj]], [b_wring[wi]])
                P.dma("sp", wsrc.rearrange("(kc p) n -> p kc n", p=128)[:, :, bc0:bc0 + bw], wring[wi][:, :, 0:bw],
                      [b_wring[wi]], [b_wsrc[blk]], "w32w%d" % wi)
                return
            P.dma("sp", wring[wi][:, :, 0:bw], wsrc.rearrange("(kc p) n -> p kc n", p=128)[:, :, bc0:bc0 + bw],
                  [b_wsrc[blk] if isinstance(b_wsrc, list) else b_wsrc], [b_wring[wi]], "wring%d" % wi)

        if mode != "skip_prefetch":
            load_blk(0)
            if nblk > 1:
                load_blk(1)
        if mode == "prefetch_only":
            return
        for blk in range(nblk):
            bc0 = blk * 512
            bw = min(512, ncols_total - bc0)
            wi = blk % len(wring)
            if blk + 2 < nblk:
                load_blk(blk + 2)
            for (c0, dst, b_dst, scale) in fm_list:
                if not (bc0 <= c0 < bc0 + bw):
                    continue
                lc = c0 - bc0
                for tc in range(4):
                    bk = pbanks[cnt[0] % 2]
                    si = cnt[0] % len(stg)
                    cnt[0] += 1
                    for kc in range(8):
                        mm(banks[bk][:, :], wring[wi][:, kc, lc:lc + 128], hT[:, kc, tc * 512:(tc + 1) * 512], kc == 0, kc == 7,
                           [b_wring[wi], b_hT[tc]], [b_bank[bk]])
                    evac(stg[si], banks[bk][:, :], [b_bank[bk]], [b_stg[si]], scale=scale)
                    P.dma("sp", dst[:, tc * 512:(tc + 1) * 512], stg[si], [b_stg[si]], [b_dst], "stg%d" % si)
            for (c0, ncols, dst, b_dst, kind) in tm_list:
                if not (bc0 <= c0 < bc0 + bw):
                    continue
                lc = c0 - bc0
                for t16 in range(16):
                    bk = pbanks[cnt[0] % 2]
                    cnt[0] += 1
                    for kc in range(8):
                        mm(banks[bk][:, 0:ncols], hT[:, kc, t16 * 128:(t16 + 1) * 128], wring[wi][:, kc, lc:lc + ncols], kc == 0, kc == 7,
                           [b_wring[wi], b_hT[t16 // 4]], [b_bank[bk]])
                    if kind == "sig":
                        gi = cnt[1] % len(gstg)
                        cnt[1] += 1
                        act(gstg[gi], banks[bk][:, 0:ncols], AF.Sigmoid, [b_bank[bk]], [b_gstg[gi]])
                        P.dma("sp", dst[t16 * 128:(t16 + 1) * 128, :], gstg[gi], [b_gstg[gi]], [b_dst], "gstg%d" % gi)
                    else:
                        si = cnt[0] % len(stg)
                        evac(stg[si][:, 0:ncols], banks[bk][:, 0:ncols], [b_bank[bk]], [b_stg[si]])
                        P.dma("sp", dst[t16 * 128:(t16 + 1) * 128, :], stg[si][:, 0:ncols], [b_stg[si]], [b_dst], "stg%d" % si)

    def load_hT(s, src_x, gi):
        hT = A.alloc([128, 8, S], BF16)
        b_hT = [Buf() for _ in range(4)]
        xt = [A.alloc([128, D], F32) for _ in range(6)]
        b_xt = [Buf() for _ in range(6)]
        for t16 in range(16):
            xi = t16 % 6
            P.dma("sp", xt[xi], src_x[s, t16 * 128:(t16 + 1) * 128, :], [b_xs[s][t16]], [b_xt[xi]], "ldx%d" % xi)
            norm_T(xt[xi], b_xt[xi], gi, hT[:, :, t16 * 128:(t16 + 1) * 128], b_hT[t16 // 4])
        return hT, b_hT

    def outproj_residual(s, qc, o_bf, b_obf, wout, b_wout, src_x, oT, b_oT, xres, b_xres, obanks):
        for j in range(4):
            bk = PT_BANKS[j % 2]
            pv = bank_bf(bk).rearrange("p (a b) -> p a b", b=128)
            for kc in range(8):
                tr(pv[:, kc, :], o_bf[:, j, kc * 128:(kc + 1) * 128], ident, [b_obf, b_ident], [b_bank[bk]])
            evac(oT[:, :, j * 128:(j + 1) * 128], pv, [b_bank[bk]], [b_oT])
        for j in range(4):
            t16 = qc * 4 + j
            xi = j % 2
            P.dma("sp", xres[xi], src_x[s, t16 * 128:(t16 + 1) * 128, :], [b_xs[s][t16]], [b_xres[xi]], "xres%d" % xi)
            for nh in range(2):
                bk = obanks[(j * 2 + nh) % 2]
                for kc in range(8):
                    mm(banks[bk][:, :], oT[:, kc, j * 128:(j + 1) * 128], wout[:, kc, nh * 512:(nh + 1) * 512], kc == 0, kc == 7,
                       [b_oT, b_wout], [b_bank[bk]])
                tt("dve", xres[xi][:, nh * 512:(nh + 1) * 512], banks[bk][:, :], xres[xi][:, nh * 512:(nh + 1) * 512], ALU.add,
                   [b_bank[bk], b_xres[xi]], [b_xres[xi]])
            P.dma("pool", xs[s, t16 * 128:(t16 + 1) * 128, :], xres[xi], [b_xres[xi]], [b_xs[s][t16]], "xst%d" % xi)

    def nsa_setup_tables():
        m = A.mark()
        rb = A.alloc([33, H], F32)
        rb31 = A.alloc([33, H], F32)
        tv = A.alloc([33, H], BF16)
        oh = A.alloc([33, T1W], BF16)
        t1s = A.alloc([H, T1W], BF16)
        b = Buf()
        P.op("dve", lambda e: e.memset(rb, NEG), [], [b])
        P.op("dve", lambda e: e.memset(rb31, 0.0), [b], [b])
        P.dma("sp", rb[0:32, :], rel_bias, [b], [b], "t1a")
        P.dma("sp", rb31[0:32, :], rel_bias[31, :].partition_broadcast(32), [b], [b], "t1a")
        P.dma("sp", oh, cst["c_oh1d"], [], [b], "t1a")
        tt("dve", tv, rb, rb31, ALU.subtract, [b], [b])
        for c4 in range(4):
            mm(banks[7][0:H, 0:400], tv, oh[:, c4 * 400:(c4 + 1) * 400], True, True, [b], [b_bank[7]])
            cp("dve", t1s[:, c4 * 400:(c4 + 1) * 400], banks[7][0:H, 0:400], [b_bank[7]], [b])
        P.dma("sp", t1_d, t1s, [b], [b_t1d], "t1a")
        P.barrier()
        A.reset(m)

    def nsa_layer(s, src_x):
        m0 = A.mark()
        wring = [A.alloc([128, 8, 512], BF16) for _ in range(3)]
        b_wring = [Buf() for _ in range(3)]
        stg = [A.alloc([128, 512], BF16) for _ in range(4)]
        b_stg = [Buf() for _ in range(4)]
        gstg = [A.alloc([128, 48], F32) for _ in range(2)]
        b_gstg = [Buf() for _ in range(2)]
        w32 = None
        if s == 0:
            w32stage = [A.alloc([128, 8, 512], F32) for _ in range(2)]
            w32 = (nsa_w_in[0], w32stage, [Buf() for _ in range(2)])
        proj_phase(None, None, nwinb, b_nwinb_blk, NSA_IN, [], [], wring, b_wring, stg, b_stg, (6, 7), gstg, b_gstg, w32=w32, mode="prefetch_only")
        hT, b_hT = load_hT(s, src_x, 0)
        fm = []
        for i in range(8):
            fm.append((i * 128, qT_d[i * 128:(i + 1) * 128, :], b_qTd, 0.125))
        for i, c0 in enumerate([1024, 1152, 1280, 1408, 1536, 1664, 2048, 2176]):
            fm.append((c0, kT_d[i * 128:(i + 1) * 128, :], b_kTd, None))
        tm = [(1792, 256, v_d[:, 0:256], b_vd, "copy"), (2304, 256, v_d[:, 256:512], b_vd, "copy"),
              (2560, 48, gates_d, b_gd, "sig")]
        proj_phase(hT, b_hT, nwinb, b_nwinb_blk, NSA_IN, fm, tm, wring, b_wring, stg, b_stg, (6, 7), gstg, b_gstg, w32=w32, mode="skip_prefetch")
        P.barrier()
        A.reset(m0)

        cd = A.alloc([128, H, 256], BF16)
        jmat = A.alloc([128, 128], BF16)
        addm = A.alloc([128, 16, 32], F32)
        mulm = A.alloc([128, 16, 32], F32)
        we = A.alloc([128, 128], BF16)
        b_c = Buf()
        P.dma("sp", cd, bass.AP(t1_d.tensor, T1Z - 127, [[1, 128], [T1W, H], [1, 256]]), [b_t1d], [b_c], "nc0")
        P.dma("sp", jmat, cst["c_jmat"], [], [b_c], "nc0")
        P.dma("sp", addm, cst["c_addm"], [], [b_c], "nc0")
        P.dma("sp", mulm, cst["c_mulm"], [], [b_c], "nc0")
        P.dma("sp", we, cst["c_we"], [], [b_c], "nc0")
        wout = A.alloc([128, 8, D], BF16)
        b_wout = Buf()
        P.dma("sp", wout, nwoutb.rearrange("(kc p) n -> p kc n", p=128), [b_nwoutb], [b_wout], "nc1")
        ksaug = A.alloc([128, G, S], BF16)
        kwaug = A.alloc([128, G, S], BF16)
        b_k = Buf()
        P.op("pool", lambda e: e.memset(ksaug, 0.0), [], [b_k])
        P.op("pool", lambda e: e.memset(kwaug, 0.0), [b_k], [b_k])
        for g in range(G):
            P.dma("sp", ksaug[0:64, g, :], kT_d[512 + g * 64:512 + (g + 1) * 64, :], [b_kTd, b_k], [b_k], "nc2")
            P.dma("sp", ksaug[64:96, g, :], cst["c_ek"], [b_k], [b_k], "nc2")
            P.dma("sp", kwaug[0:64, g, :], kT_d[768 + g * 64:768 + (g + 1) * 64, :], [b_kTd, b_k], [b_k], "nc2")
        vsx = A.alloc([128, 16, G, 65], BF16)
        vwx = A.alloc([128, 16, G, 65], BF16)
        b_v = Buf()
        P.op("pool", lambda e: e.memset(vsx, 1.0), [], [b_v])
        P.op("pool", lambda e: e.memset(vwx, 1.0), [b_v], [b_v])
        for kt in range(16):
            P.dma("sp", vsx[:, kt, :, 0:64], v_d[kt * 128:(kt + 1) * 128, 0:256].rearrange("p (g d) -> p g d", d=64), [b_vd, b_v], [b_v], "nc3")
            P.dma("sp", vwx[:, kt, :, 0:64], v_d[kt * 128:(kt + 1) * 128, 256:512].rearrange("p (g d) -> p g d", d=64), [b_vd, b_v], [b_v], "nc3")
        kcaug = A.alloc([128, G, 4, 128], BF16)
        vcx = A.alloc([128, G, 97], BF16)
        b_kcmp, b_vcx = Buf(), Buf()
        P.op("pool", lambda e: e.memset(kcaug, 0.0), [], [b_kcmp])
        P.op("pool", lambda e: e.memset(vcx, 1.0), [], [b_vcx])
        for g in range(G):
            P.dma("sp", vcx[0:127, g, 65:97], cst["c_ovl"], [b_vcx], [b_vcx], "nc4")
            P.dma("sp", kcaug[64:128, g, :, 0:127], cst["c_selp"], [b_kcmp], [b_kcmp], "nc4")

        mB = A.mark()
        kcT = A.alloc([64, G, S], BF16)
        vcT = A.alloc([64, G, S], BF16)
        b_kc = Buf()
        P.dma("sp", kcT, kT_d[0:256, :].rearrange("(g p) t -> p g t", p=64), [b_kTd], [b_kc], "nb0")
        P.dma("sp", vcT, kT_d[256:512, :].rearrange("(g p) t -> p g t", p=64), [b_kTd], [b_kc], "nb0")
        w1s = {}
        w2s = {}
        peT = {}
        b_cw = Buf()
        for nm, w1, w2, pe in (("k", nsa_wk1, nsa_wk2, nsa_pe_k), ("v", nsa_wv1, nsa_wv2, nsa_pe_v)):
            w1s[nm] = A.alloc([64, 32, DH], BF16)
            w2s[nm] = A.alloc([64, 64], BF16)
            peT[nm] = A.alloc([64, 34], BF16)
            P.op("dve", lambda e, nm=nm: e.memset(peT[nm], 0.0), [], [b_cw])
            P.dma("pool", w1s[nm], w1[0].rearrange("(l d) o -> d l o", d=DH), [], [b_cw], "nb1")
            P.dma("pool", w2s[nm], w2[0], [], [b_cw], "nb1")
            P.dma("pool", peT[nm][:, 0:32], pe[0].rearrange("l d -> d l"), [b_cw], [b_cw], "nb1", allow_slow_non_contiguous=True)
        cstv = A.alloc([64, 2], F32)
        b_cst = Buf()
        gu = [A.alloc([64, 128], F32) for _ in range(4)]
        gbf = A.alloc([64, 128], BF16)
        b_g = Buf()
        for ni, nm in enumerate(("k", "v")):
            for l in range(32):
                mm(banks[7][0:64, 0:2], w1s[nm][:, l, :], peT[nm][:, l:l + 2], l == 0, l == 31, [b_cw], [b_bank[7]])
            cp("dve", cstv[:, ni:ni + 1], banks[7][0:64, 0:1], [b_bank[7]], [b_cst])
        for g in range(G):
            for ni, (nm, srcT) in enumerate((("k", kcT), ("v", vcT))):
                bk = 6 + (g * 2 + ni) % 2
                v4 = srcT.rearrange("p g (c r) -> p g c r", r=16)
                for l in range(32):
                    mm(banks[bk][0:64, 0:127], w1s[nm][:, l, :], v4[:, g, (l // 16):(l // 16) + 127, l % 16], l == 0, l == 31,
                       [b_cw, b_kc], [b_bank[bk]])
                u, u2, t3, th = gu
                act(u[:, 0:127], banks[bk][0:64, 0:127], AF.Identity, [b_bank[bk], b_cst], [b_g], bias=cstv[:, ni:ni + 1])
                tt("dve", u2[:, 0:127], u[:, 0:127], u[:, 0:127], ALU.mult, [b_g], [b_g])
                ts("dve", u2[:, 0:127], u2[:, 0:127], 0.044715, 1.0, ALU.mult, ALU.add, [b_g], [b_g])
                tt("dve", t3[:, 0:127], u2[:, 0:127], u[:, 0:127], ALU.mult, [b_g], [b_g])
                act(th[:, 0:127], t3[:, 0:127], AF.Tanh, [b_g], [b_g], scale=0.7978845608028654)
                ts("dve", th[:, 0:127], th[:, 0:127], 1.0, 0.5, ALU.add, ALU.mult, [b_g], [b_g])
                tt("dve", gbf[:, 0:127], th[:, 0:127], u[:, 0:127], ALU.mult, [b_g], [b_g])
                if nm == "k":
                    mm(banks[bk][0:64, 128:255], w2s["k"], gbf[:, 0:127], True, True, [b_g, b_cw], [b_bank[bk]])
                    for q4 in range(4):
                        cp("dve", kcaug[0:64, g, q4, 0:127], banks[bk][0:64, 128:255], [b_bank[bk]], [b_kcmp])
                else:
                    mm(banks[bk][0:127, 256:320], gbf[:, 0:127], w2s["v"], True, True, [b_g, b_cw], [b_bank[bk]])
                    cp("dve", vcx[0:127, g, 0:64], banks[bk][0:127, 256:320], [b_bank[bk]], [b_vcx])
        P.barrier()
        A.reset(mB)

        Qc = [A.alloc([128, 512], BF16) for _ in range(3)]
        Qsw = [A.alloc([128, 512], BF16) for _ in range(3)]
        b_Qc = [Buf() for _ in range(3)]
        b_Qsw = [Buf() for _ in range(3)]
        for i in range(3):
            P.op("pool", lambda e, i=i: e.memset(Qsw[i], 0.0), [], [b_Qsw[i]])
        PT = [A.alloc([128, 512], BF16) for _ in range(5)]
        b_PT = [Buf() for _ in range(5)]
        o_acc = A.alloc([128, 4, D], F32)
        o_bf = A.alloc([128, 4, D], BF16)
        b_oacc, b_obf = Buf(), Buf()
        oT = A.alloc([128, 8, 512], BF16)
        b_oT = Buf()
        xres = [A.alloc([128, D], F32) for _ in range(2)]
        b_xres = [Buf() for _ in range(2)]
        gts = A.alloc([128, 4, 48], F32)
        b_gts = Buf()
        imp = A.alloc([128, 4, 32], F32)
        sc = A.alloc([128, 4, 32], F32)
        sc2 = A.alloc([128, 4, 32], F32)
        mx8 = A.alloc([128, 8], F32)
        mkb = A.alloc([128, 4, 96], BF16)
        b_imp, b_sc = Buf(), Buf()
        P.op("pool", lambda e: e.memset(mkb, 0.0), [], [b_sc])
        sm = [A.alloc([128, 4], F32) for _ in range(6)]
        b_sm = [Buf() for _ in range(6)]
        tmpo = [A.alloc([128, 4, 64], F32) for _ in range(2)]
        b_tmpo = [Buf() for _ in range(2)]
        tmpi = A.alloc([128, 4, 32], F32)
        b_tmpi = Buf()
        ucnt = [0]
        qcc = [0]
        qsc = [0]

        selTs = A.alloc([128, G, 512], BF16)
        b_selTs = [Buf() for _ in range(G)]
        stgo = [A.alloc([128, 4 * 97], F32) for _ in range(3)]
        b_stgo = [Buf() for _ in range(3)]
        for qc in range(4):
            q0 = qc * 512
            P.dma("sp", gts, gates_d[q0:q0 + 512, :].rearrange("(j p) c -> p j c", p=128), [b_gd, b_gts], [b_gts], "gts")
            use_sel = qc >= 2

            def epilogue(ob, h, br, ncol, first_branch, with_imp, hg):
                k = ucnt[0] % 6
                k2 = (ucnt[0] + 3) % 6
                ucnt[0] += 1
                po = banks[ob][:, 0:4 * ncol].rearrange("p (j c) -> p j c", c=ncol)
                b_po = b_bank[ob]
                if br == 0 and qc == 0:
                    ts("dve", sm[k], po[:, :, 64], 1e-30, None, ALU.max, None, [b_po], [b_sm[k]])
                    P.op("dve", lambda e: e.reciprocal(out=sm[k], in_=sm[k]), [b_sm[k]], [b_sm[k]])
                else:
                    P.op("dve", lambda e: e.reciprocal(out=sm[k], in_=po[:, :, 64]), [b_po], [b_sm[k]])
                if with_imp and use_sel:
                    rb_ = sm[k][:, :].unsqueeze(2).broadcast_to([128, 4, 32])
                    if hg == 0:
                        tt("dve", imp, po[:, :, 65:97], rb_, ALU.mult, [b_po, b_sm[k]], [b_imp])
                    else:
                        tt("dve", tmpi, po[:, :, 65:97], rb_, ALU.mult, [b_po, b_sm[k]], [b_tmpi])
                        tt("pool", imp, imp, tmpi, ALU.add, [b_tmpi, b_imp], [b_imp])
                tt("dve", sm[k2], sm[k], gts[:, :, br * 16 + h], ALU.mult, [b_sm[k], b_gts], [b_sm[k2]])
                rg = sm[k2][:, :].unsqueeze(2).broadcast_to([128, 4, 64])
                osl = o_acc[:, :, h * 64:(h + 1) * 64]
                if first_branch:
                    tt("dve", osl, po[:, :, 0:64], rg, ALU.mult, [b_po, b_sm[k2]], [b_oacc])
                else:
                    ti = ucnt[0] % 2
                    tt("dve", tmpo[ti], po[:, :, 0:64], rg, ALU.mult, [b_po, b_sm[k2]], [b_tmpo[ti]])
                    tt("pool", osl, osl, tmpo[ti], ALU.add, [b_tmpo[ti], b_oacc], [b_oacc])

            def selection(g):
                bk_sel = PT_BANKS[g % 2]
                pvw = bank_bf(bk_sel)
                tt("dve", sc, imp, mulm[:, qc * 4:(qc + 1) * 4, :], ALU.mult, [b_imp, b_c], [b_sc])
                tt("dve", sc, sc, addm[:, qc * 4:(qc + 1) * 4, :], ALU.add, [b_sc, b_c], [b_sc])
                for j in range(4):
                    P.op("dve", lambda e, j=j: e.max(out=mx8, in_=sc[:, j, :]), [b_sc], [b_sc])
                    P.op("dve", lambda e, j=j: e.match_replace(out=sc2[:, j, :], in_to_replace=mx8, in_values=sc[:, j, :], imm_value=-3e38),
                         [b_sc], [b_sc])
                    P.op("dve", lambda e, j=j: e.max(out=mx8, in_=sc2[:, j, :]), [b_sc], [b_sc])
                    ts("dve", sc2[:, j, :], sc[:, j, :], mx8[:, 7:8], None, ALU.is_ge, None, [b_sc], [b_sc])
                ts("dve", mkb[:, :, 64:96], sc2, -NEG, NEG, ALU.mult, ALU.add, [b_sc], [b_sc])
                for j in range(4):
                    tr(pvw[0:96, j * 128:(j + 1) * 128], mkb[:, j, :], ident, [b_sc, b_ident], [b_bank[bk_sel]])
                cp("dve", selTs[64:96, g, :], pvw[64:96, 0:512], [b_bank[bk_sel]], [b_selTs[g]])

            def near_corr(mms, h, delta):
                if delta > 128:
                    return
                a0 = max(0, -delta)
                a1 = min(512, 256 - delta)
                mms.append((a0, a1, jmat, cd[:, h, delta + a0:delta + a1], [b_c]))

            units = []
            Mc = min(32 * (qc + 1), 127)
            for g in range(G):
                for hg in range(4):
                    h = g * 4 + hg
                    qi = qcc[0] % 3
                    qcc[0] += 1

                    def pre(h=h, qi=qi):
                        P.dma("sp", Qc[qi][0:64, :], qT_d[h * 64:(h + 1) * 64, q0:q0 + 512], [b_qTd, b_Qc[qi]], [b_Qc[qi]], "nQc%d" % qi)
                        P.dma("sp", Qc[qi][64:128, :], bass.AP(t1_d.tensor, h * T1W + T1Z + 481 - 1008, [[16, 64], [1, 512]]),
                              [b_t1d, b_Qc[qi]], [b_Qc[qi]], "nQcb%d" % qi)
                    mms = [(0, 512, kcaug[:, g, qc, 0:Mc], Qc[qi], [b_kcmp, b_Qc[qi]])]
                    st = dict(M=Mc, c0=0, c1=512, mms=mms, v=(vcx[0:Mc, g, :], [b_vcx]), ncol=97)

                    def epi_c(ob, h=h, hg=hg, g=g):
                        epilogue(ob, h, 0, 97, True, True, hg)
                        if hg == 3 and use_sel:
                            selection(g)
                    units.append(dict(steps=[st], pre=pre, epi=epi_c))
                    steps = []
                    for kt in range(max(0, 4 * qc - 4), 4 * qc + 4):
                        k0 = kt * 128
                        delta = q0 - k0
                        c0 = max(0, -delta)
                        c1 = min(512, 640 - delta)
                        mms = [(c0, c1, kwaug[:, g, k0:k0 + 128], Qc[qi][:, c0:c1], [b_k, b_Qc[qi]])]
                        near_corr(mms, h, delta)
                        if delta >= 128:
                            f0 = 512 - delta
                            mms.append((f0, f0 + 128, ident, we, [b_c, b_ident]))
                        steps.append(dict(M=128, c0=c0, c1=c1, mms=mms, v=(vwx[:, kt, g, :], [b_v]), ncol=65))
                    units.append(dict(steps=steps, pre=None, epi=(lambda ob, h=h, hg=hg: epilogue(ob, h, 2, 65, False, False, hg))))
            for g in range(G):
                for hg in range(4):
                    h = g * 4 + hg
                    qi = qsc[0] % 3
                    qsc[0] += 1

                    def pre(h=h, qi=qi, g=g):
                        P.dma("sp", Qsw[qi][0:64, :], qT_d[h * 64:(h + 1) * 64, q0:q0 + 512], [b_qTd, b_Qsw[qi]], [b_Qsw[qi]], "nQs%d" % qi)
                        if use_sel:
                            cp("dve", Qsw[qi][64:96, :], selTs[64:96, g, :], [b_selTs[g], b_Qsw[qi]], [b_Qsw[qi]])
                    steps = []
                    for kt in range(4 * (qc + 1)):
                        k0 = kt * 128
                        delta = q0 - k0
                        c0 = max(0, -delta)
                        mms = [(c0, 512, ksaug[:, g, k0:k0 + 128], Qsw[qi][:, c0:512], [b_k, b_Qsw[qi]])]
                        near_corr(mms, h, delta)
                        steps.append(dict(M=128, c0=c0, c1=512, mms=mms, v=(vsx[:, kt, g, :], [b_v]), ncol=65))
                    units.append(dict(steps=steps, pre=pre, epi=(lambda ob, h=h, hg=hg: epilogue(ob, h, 1, 65, False, False, hg))))
            run_units(units, PT, b_PT, (0, 1, 6), (2, 3, 7))
            cp("dve", o_bf, o_acc, [b_oacc], [b_obf])
            outproj_residual(s, qc, o_bf, b_obf, wout, b_wout, src_x, oT, b_oT, xres, b_xres, (6, 7))
        P.barrier()
        A.reset(m0)

    def fox_layer(s, src_x):
        m0 = A.mark()
        wring = [A.alloc([128, 8, 512], BF16) for _ in range(3)]
        b_wring = [Buf() for _ in range(3)]
        stg = [A.alloc([128, 512], BF16) for _ in range(4)]
        b_stg = [Buf() for _ in range(4)]
        proj_phase(None, None, fwinb, b_fwinb, 3072, [], [], wring, b_wring, stg, b_stg, (6, 7), mode="prefetch_only")
        hT, b_hT = load_hT(s, src_x, 1)
        fm = []
        for i in range(8):
            fm.append((i * 128, qT_d[i * 128:(i + 1) * 128, :], b_qTd, 0.125))
        for i in range(8):
            fm.append((1024 + i * 128, kT_d[i * 128:(i + 1) * 128, :], b_kTd, None))
        tm = [(2048, 512, v_d[:, 0:512], b_vd, "copy"), (2560, 512, v_d[:, 512:1024], b_vd, "copy")]
        wf = A.alloc([128, 8, H], BF16)
        b_wf = Buf()
        P.dma("sp", wf, fwinb.rearrange("(kc p) n -> p kc n", p=128)[:, :, 3072:3088], [b_fwinb], [b_wf], "wf")
        fl = A.alloc([H, S], F32)
        b_fl = Buf()
        bfv = A.alloc([H, 1], F32)
        P.dma("sp", bfv, fox_b_f.rearrange("o h -> h o"), [], [b_fl], "bfv", allow_slow_non_contiguous=True)
        for tc in range(4):
            bk = 6 + tc % 2
            for kc in range(8):
                mm(banks[bk][0:H, :], wf[:, kc, :], hT[:, kc, tc * 512:(tc + 1) * 512], kc == 0, kc == 7, [b_wf, b_hT[tc]], [b_bank[bk]])
            act(fl[:, tc * 512:(tc + 1) * 512], banks[bk][0:H, :], AF.Identity, [b_bank[bk], b_fl], [b_fl], bias=bfv[:, 0:1])
        az = A.alloc([H, S], F32)
        mz = A.alloc([H, S], F32)
        onesr = A.alloc([H, S], F32)
        cpp = A.alloc([H, 6, S], BF16)
        P.op("pool", lambda e: e.memset(onesr, 1.0), [], [b_fl])
        act(az, fl, AF.Abs, [b_fl], [b_fl])
        act(az, az, AF.Exp, [b_fl], [b_fl], scale=-1.0)
        act(az, az, AF.Ln, [b_fl], [b_fl], bias=1.0)
        ts("dve", mz, fl, 0.0, None, ALU.min, None, [b_fl], [b_fl])
        tt("dve", mz, mz, az, ALU.subtract, [b_fl], [b_fl])
        P.op("dve", lambda e: e.tensor_tensor_scan(out=az, data0=onesr, data1=mz, initial=0.0, op0=ALU.mult, op1=ALU.add), [b_fl], [b_fl])
        cp("dve", cpp[:, 0, :], az, [b_fl], [b_fl])
        tt("dve", mz, az, cpp[:, 0, :], ALU.subtract, [b_fl], [b_fl])
        cp("dve", cpp[:, 1, :], mz, [b_fl], [b_fl])
        tt("dve", mz, mz, cpp[:, 1, :], ALU.subtract, [b_fl], [b_fl])
        cp("dve", cpp[:, 2, :], mz, [b_fl], [b_fl])
        ts("dve", cpp[:, 3:6, :], cpp[:, 0:3, :], -1.0, None, ALU.mult, None, [b_fl], [b_fl])
        P.dma("pool", cpart_d.rearrange("i h t -> h i t"), cpp, [b_fl], [b_cpd], "cpd")
        proj_phase(hT, b_hT, fwinb, b_fwinb, 3072, fm, tm, wring, b_wring, stg, b_stg, (6, 7), mode="skip_prefetch")
        P.barrier()
        A.reset(m0)

        cm = A.alloc([128, 128], BF16)
        b_c = Buf()
        P.dma("sp", cm, cst["c_cm"], [], [b_c], "fc0")
        wout = A.alloc([128, 8, D], BF16)
        b_wout = Buf()
        P.dma("sp", wout, fwoutb.rearrange("(kc p) n -> p kc n", p=128), [b_fwoutb], [b_wout], "fc1")
        vfx = A.alloc([128, 16, H, 65], BF16)
        b_v = Buf()
        P.op("pool", lambda e: e.memset(vfx, 1.0), [], [b_v])
        for kt in range(16):
            P.dma("sp", vfx[:, kt, :, 0:64], v_d[kt * 128:(kt + 1) * 128, :].rearrange("p (h d) -> p h d", d=64), [b_vd, b_v], [b_v], "fc3")
        Kh = [A.alloc([128, S], BF16) for _ in range(2)]
        Qh = [A.alloc([128, 512], BF16) for _ in range(3)]
        b_Kh = [Buf() for _ in range(2)]
        b_Qh = [Buf() for _ in range(3)]
        for i in range(2):
            P.op("pool", lambda e, i=i: e.memset(Kh[i], 0.0), [], [b_Kh[i]])
            P.op("pool", lambda e, i=i: e.memset(Kh[i][64:67, :], 1.0), [b_Kh[i]], [b_Kh[i]])
        for i in range(3):
            P.op("pool", lambda e, i=i: e.memset(Qh[i], 0.0), [], [b_Qh[i]])
            P.op("pool", lambda e, i=i: e.memset(Qh[i][96:99, :], 1.0), [b_Qh[i]], [b_Qh[i]])
        PT = [A.alloc([128, 512], BF16) for _ in range(5)]
        b_PT = [Buf() for _ in range(5)]
        o_bf = A.alloc([128, 16, D], BF16)
        b_obf = Buf()
        oT = A.alloc([128, 8, 512], BF16)
        b_oT = Buf()
        xres = [A.alloc([128, D], F32) for _ in range(2)]
        b_xres = [Buf() for _ in range(2)]
        sm = [A.alloc([128, 4], F32) for _ in range(4)]
        b_sm = [Buf() for _ in range(4)]
        ucnt = [0]
        qcnt = [0]
        units = []
        for h in range(H):
            ki = h % 2
            for qc in range(4):
                q0 = qc * 512
                qi = qcnt[0] % 3
                qcnt[0] += 1

                def pre(h=h, ki=ki, qc=qc, q0=q0, qi=qi):
                    if qc == 0:
                        P.dma("sp", Kh[ki][0:64, :], kT_d[h * 64:(h + 1) * 64, :], [b_kTd, b_Kh[ki]], [b_Kh[ki]], "fK%d" % ki)
                        P.dma("sp", Kh[ki][96:99, :], cpart_d[3:6, h, :], [b_cpd, b_Kh[ki]], [b_Kh[ki]], "fKb%d" % ki)
                    P.dma("sp", Qh[qi][0:64, :], qT_d[h * 64:(h + 1) * 64, q0:q0 + 512], [b_qTd, b_Qh[qi]], [b_Qh[qi]], "fQ%d" % qi)
                    P.dma("sp", Qh[qi][64:67, :], cpart_d[0:3, h, q0:q0 + 512], [b_cpd, b_Qh[qi]], [b_Qh[qi]], "fQb%d" % qi)
                steps = []
                for kt in range(4 * (qc + 1)):
                    k0 = kt * 128
                    delta = q0 - k0
                    c0 = max(0, -delta)
                    mms = [(c0, 512, Kh[ki][:, k0:k0 + 128], Qh[qi][:, c0:512], [b_Kh[ki], b_Qh[qi]])]
                    if delta <= 0:
                        mms.append((c0, c0 + 128, ident, cm, [b_c, b_ident]))
                    steps.append(dict(M=128, c0=c0, c1=512, mms=mms, v=(vfx[:, kt, h, :], [b_v]), ncol=65))

                def epi(ob, h=h, qc=qc):
                    k = ucnt[0] % 4
                    ucnt[0] += 1
                    po = banks[ob][:, 0:260].rearrange("p (j c) -> p j c", c=65)
                    P.op("dve", lambda e: e.reciprocal(out=sm[k], in_=po[:, :, 64]), [b_bank[ob]], [b_sm[k]])
                    rg = sm[k][:, :].unsqueeze(2).broadcast_to([128, 4, 64])
                    tt("dve", o_bf[:, qc * 4:(qc + 1) * 4, h * 64:(h + 1) * 64], po[:, :, 0:64], rg, ALU.mult, [b_bank[ob], b_sm[k]], [b_obf])
                units.append(dict(steps=steps, epi=epi, pre=pre))
        run_units(units, PT, b_PT, (0, 1, 6), (2, 3, 7))
        for qc in range(4):
            outproj_residual(s, qc, o_bf[:, qc * 4:(qc + 1) * 4, :], b_obf, wout, b_wout, src_x, oT, b_oT, xres, b_xres, (6, 7))
        P.barrier()
        A.reset(m0)

    def mlp_layer(l, src_x, last):
        m0 = A.mark()
        w2s = A.alloc([128, 32, D], BF16)
        b_w2s = Buf()
        for q4 in range(4):
            P.dma("sp", w2s[:, q4 * 8:(q4 + 1) * 8, :], w2b[l].rearrange("(fc p) n -> p fc n", p=128)[:, q4 * 8:(q4 + 1) * 8, :],
                  [b_w2b[l]], [b_w2s], "w2s")
        NW1 = 2
        w1s = [A.alloc([128, 8, 512], BF16) for _ in range(NW1)]
        b_w1s = [Buf() for _ in range(NW1)]
        aT = A.alloc([128, 32, 512], BF16)
        b_aT = [Buf() for _ in range(32)]
        hTc = [A.alloc([128, 8, 512], BF16) for _ in range(2)]
        b_hTc = [[Buf() for _ in range(4)] for _ in range(2)]
        xt = [A.alloc([128, D], F32) for _ in range(6)]
        b_xt = [Buf() for _ in range(6)]
        rtmp = [A.alloc([128, 512], F32) for _ in range(2)]
        b_rtmp = [Buf() for _ in range(2)]
        yo = [A.alloc([128, D], F32) for _ in range(2)]
        b_yo = [Buf() for _ in range(2)]
        w1cnt = p1cnt = p2cnt = xcnt = 0
        for s in range(nseq):
            for c in range(4):
                cb = (s * 4 + c) % 2
                xis = []
                for j in range(4):
                    t16 = c * 4 + j
                    xi = xcnt % 6
                    xcnt += 1
                    xis.append(xi)
                    P.dma("sp", xt[xi], src_x[s, t16 * 128:(t16 + 1) * 128, :], [b_xs[s][t16]], [b_xt[xi]], "mxt%d" % xi)
                    norm_T(xt[xi], b_xt[xi], 2 + l, hTc[cb][:, :, j * 128:(j + 1) * 128], b_hTc[cb][j])
                for blk in range(8):
                    wi = w1cnt % NW1
                    w1cnt += 1
                    P.dma("sp", w1s[wi], w1b[l].rearrange("(kc p) n -> p kc n", p=128)[:, :, blk * 512:(blk + 1) * 512],
                          [b_w1b[l]], [b_w1s[wi]], "w1s%d" % wi)
                    for f4 in range(4):
                        fc = blk * 4 + f4
                        pi = p1cnt % 2
                        p1cnt += 1
                        for kc in range(8):
                            mm(banks[pi][:, :], w1s[wi][:, kc, f4 * 128:(f4 + 1) * 128], hTc[cb][:, kc, :], kc == 0, kc == 7,
                               [b_w1s[wi]] + b_hTc[cb], [b_bank[pi]])
                        act(rtmp[pi], banks[pi][:, :], AF.Relu, [b_bank[pi]], [b_rtmp[pi]])
                        tt("dve", aT[:, fc, :], rtmp[pi], rtmp[pi], ALU.mult, [b_rtmp[pi]], [b_aT[fc]])
                for j in range(4):
                    t16 = c * 4 + j
                    xi = xis[j]
                    for nh in range(2):
                        pi = 2 + p2cnt % 2
                        p2cnt += 1
                        for fc in range(32):
                            mm(banks[pi][:, :], aT[:, fc, j * 128:(j + 1) * 128], w2s[:, fc, nh * 512:(nh + 1) * 512], fc == 0, fc == 31,
                               [b_aT[fc], b_w2s], [b_bank[pi]])
                        tt("dve", xt[xi][:, nh * 512:(nh + 1) * 512], banks[pi][:, :], xt[xi][:, nh * 512:(nh + 1) * 512], ALU.add,
                           [b_bank[pi], b_xt[xi]], [b_xt[xi]])
                    if not last:
                        P.dma("pool", xs[s, t16 * 128:(t16 + 1) * 128, :], xt[xi], [b_xt[xi]], [b_xs[s][t16]], "mxst%d" % xi)
                    else:
                        rs, b_r, jj = rstd_of(xt[xi], b_xt[xi])
                        yi = t16 % 2
                        P.op("dve", lambda e, xi=xi, yi=yi, rs=rs: e.scalar_tensor_tensor(
                            out=yo[yi], in0=xt[xi], scalar=rs[:, 0:1], in1=gfin, op0=ALU.mult, op1=ALU.mult),
                            [b_xt[xi], b_r, b_gfin], [b_yo[yi]])
                        P.dma("pool", out[s, t16 * 128:(t16 + 1) * 128, :], yo[yi], [b_yo[yi]], [], "yo%d" % yi, is_output=True)
        P.barrier()
        A.reset(m0)

    if "nsa" in parts:
        nsa_setup_tables()
        cast_w(nwoutb, nsa_w_out[0], D, b_nwoutb, "cw_b", 2)
    if "mlp" in parts:
        cast_w(w1b[0], mlp_w1[0], D, b_w1b[0], "cw_c")
        cast_w(w2b[0], mlp_w2[0], DFF, b_w2b[0], "cw_d")
    if "fox" in parts:
        cast_w(fwinb, fox_w_in[0], D, b_fwinb, "cw_e")
        cast_w(fwoutb, fox_w_out[0], D, b_fwoutb, "cw_f", 2)
    if "mlp" in parts and depth > 1:
        cast_w(w1b[1], mlp_w1[1], D, b_w1b[1], "cw_g")
        cast_w(w2b[1], mlp_w2[1], DFF, b_w2b[1], "cw_h")

    cur = x_in
    for l in range(depth):
        if l == 0 and "nsa" in parts:
            for s in range(nseq):
                nsa_layer(s, cur)
            cur = xs
        if l == 1 and "fox" in parts:
            for s in range(nseq):
                fox_layer(s, cur)
            cur = xs
        if "mlp" in parts:
            mlp_layer(l, cur, last=(l == depth - 1))
            cur = xs
    P.emit()
    return nc, es


_CACHE = {}
IN_NAMES = ["rel_bias", "norm_mix", "norm_mlp", "nsa_w_in", "nsa_pe_k", "nsa_wk1", "nsa_wk2", "nsa_pe_v", "nsa_wv1", "nsa_wv2",
            "nsa_w_out", "fox_w_in", "fox_b_f", "fox_w_out", "mlp_w1", "mlp_w2", "final_norm"]


def kernel(**inputs):
    n = 8
    x = np.ascontiguousarray(np.asarray(inputs["x"], dtype=np.float32))
    nseq = x.shape[0] // n
    if "nc" not in _CACHE:
        _CACHE["nc"] = build(nseq=nseq)
    nc, _ = _CACHE["nc"]
    consts = host_consts()
    shared = {k: np.ascontiguousarray(np.asarray(inputs[k], dtype=np.float32)) for k in IN_NAMES}
    shared.update(consts)
    in_maps = []
    for c in range(n):
        m = dict(shared)
        m["x"] = x[c * nseq:(c + 1) * nseq]
        in_maps.append(m)
    res = run_bass_kernel_spmd(nc, in_maps, core_ids=list(range(n)))
    return np.concatenate([r["out"] for r in res.results], axis=0).astype(np.float32)
```

```python
import math
import numpy as np
import ml_dtypes
from contextlib import ExitStack
import concourse.bass as bass
import concourse.mybir as mybir
from concourse.bass_utils import run_bass_kernel_spmd

F32 = mybir.dt.float32
BF16 = mybir.dt.bfloat16
AF = mybir.ActivationFunctionType
ALU = mybir.AluOpType

S = 2048
D = 1024
DFF = 4096
H = 16
DH = 64
G = 4
NSA_IN = 2608
FOX_IN = 3088
EPS = 1e-6
NEG = -30000.0
T1W = 1600
T1Z = 600
SEM_ROLL = 30000


class Buf:
    __slots__ = ("name", "w", "r")

    def __init__(self, name=""):
        self.name = name
        self.w = {}
        self.r = {}


class Prog:
    ENG = ("pe", "act", "dve", "pool", "sp")

    def __init__(self, nc, es):
        self.nc = nc
        self.es = es
        self.ops = {e: [] for e in self.ENG}
        self.nsem = 0
        self.esem = {e: self._newsem() for e in ("pe", "act", "dve", "pool")}
        self.ecnt = {e: 0 for e in self.esem}
        self.waited = {e: {} for e in self.ENG}
        self.dsem = {}
        self.dcnt = {}
        self.allsems = {}
        self.rot = {}
        self.out_tokens = []

    def _newsem(self):
        self.nsem += 1
        return self.es.enter_context(self.nc.semaphore("sm%d" % self.nsem))

    def _deps(self, eng, reads, writes):
        need = {}
        pe_sem = self.esem["pe"]

        def add(sem, val, src):
            if src == "pe" and eng == "pe" and sem is pe_sem:
                return
            if need.get(sem, 0) < val:
                need[sem] = val

        for b in reads:
            for sem, (val, src) in b.w.items():
                add(sem, val, src)
        for b in writes:
            for sem, (val, src) in b.w.items():
                add(sem, val, src)
            for sem, (val, src) in b.r.items():
                add(sem, val, src)
        waits = []
        wd = self.waited[eng]
        for sem, val in need.items():
            if wd.get(sem, 0) < val:
                wd[sem] = val
                waits.append((sem, val))
        return waits

    def _commit(self, tok, reads, writes):
        sem, val, src = tok
        self.allsems[sem] = val
        for b in reads:
            if b.r.get(sem, (0, None))[0] < val:
                b.r[sem] = (val, src)
        for b in writes:
            if b.w.get(sem, (0, None))[0] < val:
                b.w[sem] = (val, src)

    def op(self, eng, fn, reads=(), writes=()):
        if self.ecnt[eng] >= SEM_ROLL:
            self.esem[eng] = self._newsem()
            self.ecnt[eng] = 0
        waits = self._deps(eng, reads, writes)
        self.ecnt[eng] += 1
        sem = self.esem[eng]
        self.ops[eng].append((waits, fn, sem, 1))
        self._commit((sem, self.ecnt[eng], eng), reads, writes)

    ROT_FAM = (("nc", 6), ("nb", 4), ("t1a", 2), ("gcol", 2), ("fc3", 4), ("w2s", 4), ("cw_", 4))

    def dma(self, q, out, in_, reads, writes, key, is_output=False, **kw):
        for fam, nrot in self.ROT_FAM:
            if key.startswith(fam):
                r = self.rot.get(fam, 0)
                self.rot[fam] = r + 1
                key = "%s~%d" % (fam, r % nrot)
                break
        if key in self.dsem and self.dcnt[key] >= 2000:
            del self.dsem[key]
        if key not in self.dsem:
            self.dsem[key] = self._newsem()
            self.dcnt[key] = 0
        waits = self._deps(q, reads, writes)
        sem = self.dsem[key]
        if self.dcnt[key] > 0 and self.waited[q].get(sem, 0) < 16 * self.dcnt[key]:
            self.waited[q][sem] = 16 * self.dcnt[key]
            waits.append((sem, 16 * self.dcnt[key]))
        self.dcnt[key] += 1
        self.ops[q].append((waits, (lambda e: e.dma_start(out=out, in_=in_, **kw)), sem, 16))
        tok = (sem, 16 * self.dcnt[key], q)
        self._commit(tok, reads, writes)
        if is_output:
            self.out_tokens.append(tok)

    def barrier(self):
        snap = dict(self.allsems)
        for eng in self.ENG:
            wd = self.waited[eng]
            waits = []
            for sem, val in snap.items():
                if wd.get(sem, 0) < val:
                    wd[sem] = val
                    waits.append((sem, val))
            if waits:
                self.ops[eng].append((waits, None, None, 0))

    def emit(self):
        nc = self.nc
        fin = {}
        for sem, val, _ in self.out_tokens:
            fin[sem] = max(fin.get(sem, 0), val)
        with nc.Block() as block:
            def run(eng_name):
                def body(e):
                    for waits, fn, sem, inc in self.ops[eng_name]:
                        for s, v in waits:
                            e.wait_ge(s, v)
                        if fn is not None:
                            fn(e).then_inc(sem, inc)
                    if eng_name == "sp":
                        for s, v in fin.items():
                            e.wait_ge(s, v)
                return body
            block.tensor(run("pe"))
            block.scalar(run("act"))
            block.vector(run("dve"))
            block.gpsimd(run("pool"))
            block.sync(run("sp"))


DTSIZE = {F32: 4, BF16: 2}


class Arena:
    def __init__(self, base_ap, nwords):
        self.base = base_ap
        self.n = nwords
        self.off = 0

    def mark(self):
        return self.off

    def reset(self, m):
        self.off = m

    def alloc(self, shape, dt):
        p = shape[0]
        free = list(shape[1:])
        nel = int(np.prod(free))
        words = (nel * DTSIZE[dt] + 3) // 4
        words = (words + 7) // 8 * 8
        assert self.off + words <= self.n, ("arena overflow", self.off, words, self.n)
        v = self.base[0:p, self.off:self.off + words]
        self.off += words
        if dt != F32:
            v = v.bitcast(dt)
        v = v[:, 0:nel]
        if len(free) == 2:
            v = v.rearrange("p (a b) -> p a b", b=free[1])
        elif len(free) == 3:
            v = v.rearrange("p (a b c) -> p a b c", b=free[1], c=free[2])
        return v


def rel_bucket_np(d):
    d = np.asarray(d)
    n = np.maximum(d, 0)
    nf = np.maximum(n, 1).astype(np.float32)
    large = 16 + (np.log(nf / np.float32(16)) / np.float32(math.log(8.0)) * np.float32(16)).astype(np.int32)
    large = np.minimum(large, 31)
    return np.where(n < 16, n, large)


def host_consts():
    bf = ml_dtypes.bfloat16
    c = {}
    d = np.arange(T1W) - T1Z
    b = np.where(d < 0, 32, np.where(d < 128, rel_bucket_np(d), 31))
    oh = np.zeros((33, T1W), np.float32)
    oh[b, np.arange(T1W)] = 1.0
    c["c_oh1d"] = oh.astype(bf)
    selp = np.zeros((64, 4, 127), np.float32)
    for j in range(4):
        for rp in range(64):
            cc = 32 * (j - 1) + 63 - rp
            if 0 <= cc < 127:
                selp[rp, j, cc] = 1.0
    c["c_selp"] = selp.astype(bf)
    c["c_jmat"] = np.ascontiguousarray(np.eye(128, dtype=np.float32)[::-1]).astype(bf)
    ek = np.zeros((32, S), np.float32)
    ek[np.arange(S) // 64, np.arange(S)] = 1.0
    c["c_ek"] = ek.astype(bf)
    addm = np.zeros((128, 16, 32), np.float32)
    mulm = np.ones((128, 16, 32), np.float32)
    for tt in range(16):
        for p in range(128):
            qb = (tt * 128 + p) // 64
            for j in range(32):
                rel = qb - j
                forced = (j == 0) or (0 <= rel < 2)
                vis = rel >= 0
                if not vis:
                    addm[p, tt, j] = -1e30
                    mulm[p, tt, j] = 0.0
                elif forced:
                    addm[p, tt, j] = 1e9
                    mulm[p, tt, j] = 0.0
    c["c_addm"] = addm
    c["c_mulm"] = mulm
    ovl = np.zeros((127, 32), np.float32)
    for cc in range(127):
        for cell in (cc, cc + 1):
            ovl[cc, cell // 4] += 1.0
    c["c_ovl"] = ovl.astype(bf)
    kk = np.arange(128)[:, None]
    qq = np.arange(128)[None, :]
    c["c_we"] = np.where(qq >= kk, NEG, 0.0).astype(np.float32).astype(bf)
    c["c_cm"] = np.where(qq < kk, NEG, 0.0).astype(np.float32).astype(bf)
    return c


CONST_SPECS = {
    "c_oh1d": ([33, T1W], BF16), "c_selp": ([64, 4, 127], BF16), "c_jmat": ([128, 128], BF16),
    "c_ek": ([32, S], BF16), "c_addm": ([128, 16, 32], F32), "c_mulm": ([128, 16, 32], F32),
    "c_ovl": ([127, 32], BF16), "c_we": ([128, 128], BF16), "c_cm": ([128, 128], BF16),
}


def build(nseq=2, parts=("nsa", "fox", "mlp"), depth=2, debug=()):
    nc = bass.Bass("TRN2", target_bir_lowering=False)
    es = ExitStack()
    P = Prog(nc, es)

    def dram_in(name, shape, dt=F32):
        return nc.dram_tensor(name, list(shape), dt, kind="ExternalInput").ap()

    def dram_tmp(name, shape, dt):
        return nc.dram_tensor(name, list(shape), dt, kind=("ExternalOutput" if name in debug else "Internal")).ap()

    x_in = dram_in("x", [nseq, S, D])
    out = nc.dram_tensor("out", [nseq, S, D], F32, kind="ExternalOutput").ap()
    rel_bias = dram_in("rel_bias", [32, H])
    norm_mix = dram_in("norm_mix", [2, D])
    norm_mlp = dram_in("norm_mlp", [2, D])
    nsa_w_in = dram_in("nsa_w_in", [1, D, NSA_IN])
    nsa_pe_k = dram_in("nsa_pe_k", [1, 32, DH])
    nsa_wk1 = dram_in("nsa_wk1", [1, 32 * DH, DH])
    nsa_wk2 = dram_in("nsa_wk2", [1, DH, DH])
    nsa_pe_v = dram_in("nsa_pe_v", [1, 32, DH])
    nsa_wv1 = dram_in("nsa_wv1", [1, 32 * DH, DH])
    nsa_wv2 = dram_in("nsa_wv2", [1, DH, DH])
    nsa_w_out = dram_in("nsa_w_out", [1, D, D])
    fox_w_in = dram_in("fox_w_in", [1, D, FOX_IN])
    fox_b_f = dram_in("fox_b_f", [1, H])
    fox_w_out = dram_in("fox_w_out", [1, D, D])
    mlp_w1 = dram_in("mlp_w1", [2, D, DFF])
    mlp_w2 = dram_in("mlp_w2", [2, DFF, D])
    final_norm = dram_in("final_norm", [D])
    cst = {k: dram_in(k, sh, dt) for k, (sh, dt) in CONST_SPECS.items()}

    xs = dram_tmp("xs", [nseq, S, D], F32)
    b_xs = [[Buf() for _ in range(16)] for _ in range(nseq)]
    w1b = dram_tmp("w1b", [2, D, DFF], BF16)
    w2b = dram_tmp("w2b", [2, DFF, D], BF16)
    nwinb = dram_tmp("nwinb", [D, NSA_IN], BF16)
    nwoutb = dram_tmp("nwoutb", [D, D], BF16)
    fwinb = dram_tmp("fwinb", [D, FOX_IN], BF16)
    fwoutb = dram_tmp("fwoutb", [D, D], BF16)
    b_w1b = [Buf(), Buf()]
    b_w2b = [Buf(), Buf()]
    b_nwinb, b_nwoutb, b_fwinb, b_fwoutb = Buf(), Buf(), Buf(), Buf()
    b_nwinb_blk = [Buf() for _ in range(6)]
    t1_d = dram_tmp("t1_d", [H, T1W], BF16)
    b_t1d = Buf()
    qT_d = dram_tmp("qT_d", [D, S], BF16)
    kT_d = dram_tmp("kT_d", [D, S], BF16)
    v_d = dram_tmp("v_d", [S, D], BF16)
    gates_d = dram_tmp("gates_d", [S, 48], F32)
    cpart_d = dram_tmp("cpart_d", [6, H, S], BF16)
    b_qTd, b_kTd, b_vd, b_gd, b_cpd = Buf(), Buf(), Buf(), Buf(), Buf()

    NW = 51200
    arena_t = es.enter_context(nc.sbuf_tensor("arena", [128, NW], F32))
    A = Arena(arena_t, NW)
    banks = [es.enter_context(nc.psum_tensor("bank%d" % i, [128, 512], F32)) for i in range(8)]
    b_bank = [Buf("bank%d" % i) for i in range(8)]

    def bank_bf(i):
        return banks[i][:].bitcast(BF16)

    def mm(o, lhsT, rhs, start, stop, reads, writes, skip=False):
        P.op("pe", lambda e: e.matmul(o, lhsT, rhs, start=start, stop=stop, skip_group_check=skip), reads, writes)

    def tr(o, i, idn, reads, writes):
        P.op("pe", lambda e: e.transpose(out=o, in_=i, identity=idn), reads, writes)

    def act(o, i, func, reads, writes, **kw):
        P.op("act", lambda e: e.activation(out=o, in_=i, func=func, **kw), reads, writes)

    def tt(eng, o, a, b, op, reads, writes):
        P.op(eng, lambda e: e.tensor_tensor(out=o, in0=a, in1=b, op=op), reads, writes)

    def ts(eng, o, a, s1, s2, op0, op1, reads, writes):
        if s2 is None:
            P.op(eng, lambda e: e.tensor_scalar(out=o, in0=a, scalar1=s1, scalar2=None, op0=op0), reads, writes)
        else:
            P.op(eng, lambda e: e.tensor_scalar(out=o, in0=a, scalar1=s1, scalar2=s2, op0=op0, op1=op1), reads, writes)

    def cp(eng, o, i, reads, writes):
        P.op(eng, lambda e: e.tensor_copy(out=o, in_=i), reads, writes)

    ev_cnt = [0]

    def evac(o, i, reads, writes, scale=None):
        ev_cnt[0] += 1
        if ev_cnt[0] % 2 == 0:
            if scale is None:
                act(o, i, AF.Copy, reads, writes)
            else:
                act(o, i, AF.Copy, reads, writes, scale=float(scale))
        else:
            if scale is None:
                cp("dve", o, i, reads, writes)
            else:
                ts("dve", o, i, float(scale), None, ALU.mult, None, reads, writes)

    ident = A.alloc([128, 128], BF16)
    b_ident = Buf()
    P.op("pool", lambda e: e.memset(ident, 0.0), [], [b_ident])
    P.op("pool", lambda e: e.affine_select(out=ident, in_=ident, pattern=[[-1, 128]], compare_op=ALU.not_equal,
                                           fill=1.0, base=0, channel_multiplier=1), [b_ident], [b_ident])
    gcolT = A.alloc([128, 32], F32)
    gcol = gcolT.rearrange("p (i k) -> p i k", k=8)
    grow = A.alloc([32, 128], F32)
    b_gcol = Buf()
    for i, src in enumerate([norm_mix[0], norm_mix[1], norm_mlp[0], norm_mlp[1]]):
        P.dma("sp", grow[i * 8:(i + 1) * 8, :], src.rearrange("(k p) -> k p", p=128), [], [b_gcol], "gcol")
    for j in range(4):
        P.op("dve", lambda e, j=j: e.transpose(out=gcolT[32 * j:32 * (j + 1), 0:32], in_=grow[0:32, 32 * j:32 * (j + 1)]), [b_gcol], [b_gcol])
    gfin = A.alloc([128, D], F32)
    b_gfin = Buf()
    P.dma("sp", gfin, final_norm.partition_broadcast(128), [], [b_gfin], "gfin")
    ones_f = A.alloc([128, 512], F32)
    b_ones = Buf()
    P.op("pool", lambda e: e.memset(ones_f, 1.0), [], [b_ones])

    NST = 4
    st_ss = [A.alloc([128, 1], F32) for _ in range(NST)]
    st_rs = [A.alloc([128, 1], F32) for _ in range(NST)]
    b_ss = [Buf() for _ in range(NST)]
    b_rs = [Buf() for _ in range(NST)]
    junk = [A.alloc([128, D], BF16) for _ in range(2)]
    b_junk = [Buf() for _ in range(2)]
    xn = [A.alloc([128, D], BF16) for _ in range(2)]
    b_xn = [Buf() for _ in range(2)]
    norm_i = [0]
    PT_BANKS = (4, 5)

    def rstd_of(xt_ap, b_xt):
        i = norm_i[0]
        norm_i[0] += 1
        k = i % NST
        j = i % 2
        act(junk[j], xt_ap, AF.Square, [b_xt], [b_junk[j], b_ss[k]], accum_out=st_ss[k])
        ts("dve", st_rs[k], st_ss[k], 1.0 / D, EPS, ALU.mult, ALU.add, [b_ss[k]], [b_rs[k]])
        act(st_rs[k], st_rs[k], AF.Sqrt, [b_rs[k]], [b_rs[k]])
        P.op("dve", lambda e: e.reciprocal(out=st_rs[k], in_=st_rs[k]), [b_rs[k]], [b_rs[k]])
        return st_rs[k], b_rs[k], j

    def norm_T(xt_ap, b_xt, gi, dst_ap, b_dst):
        rs, b_r, j = rstd_of(xt_ap, b_xt)
        act(xn[j], xt_ap, AF.Identity, [b_xt, b_r], [b_xn[j]], scale=rs[:, 0:1])
        bk = PT_BANKS[j]
        pv = bank_bf(bk).rearrange("p (a b) -> p a b", b=128)
        for kc in range(8):
            tr(pv[:, kc, :], xn[j][:, kc * 128:(kc + 1) * 128], ident, [b_xn[j], b_ident], [b_bank[bk]])
        gb = gcol[:, gi, :].unsqueeze(2).broadcast_to([128, 8, 128])
        tt("dve", dst_ap, pv, gb, ALU.mult, [b_bank[bk], b_gcol], [b_dst])

    def cast_w(dst, src, rows, b, key, nsplit=4):
        step = rows // nsplit
        for r in range(nsplit):
            P.dma("pool", dst[r * step:(r + 1) * step, :], src[r * step:(r + 1) * step, :], [], [b], key)

    base_mark = A.mark()

    def run_units(units, PT, b_PT, pS_banks, pO_banks):
        flat = []
        for ui, u in enumerate(units):
            for si, st in enumerate(u["steps"]):
                flat.append((ui, si, st, u))
        nPT = len(PT)
        called = set()

        def call_pre(ui):
            if ui < len(units) and ui not in called:
                called.add(ui)
                if units[ui].get("pre") is not None:
                    units[ui]["pre"]()

        call_pre(0)

        def qk(idx):
            ui, si, st, u = flat[idx]
            if si == 0:
                call_pre(ui + 1)
            bk = pS_banks[idx % len(pS_banks)]
            M = st["M"]
            n = len(st["mms"])
            for mi, (oc0, oc1, lhsT, rhs, reads) in enumerate(st["mms"]):
                mm(banks[bk][0:M, oc0:oc1], lhsT, rhs, mi == 0, mi == n - 1, reads, [b_bank[bk]], skip=True)
            sl = idx % nPT
            act(PT[sl][0:M, st["c0"]:st["c1"]], banks[bk][0:M, st["c0"]:st["c1"]], AF.Exp, [b_bank[bk]], [b_PT[sl]])

        def pv(idx):
            ui, si, st, u = flat[idx]
            ob = pO_banks[ui % len(pO_banks)]
            M = st["M"]
            ncol = st["ncol"]
            sl = idx % nPT
            vr, vreads = st["v"]
            po = banks[ob][:, 0:4 * ncol].rearrange("p (j c) -> p j c", c=ncol)
            first = (si == 0)
            for j in range(4):
                if st["c0"] <= 128 * j and 128 * (j + 1) <= st["c1"]:
                    mm(po[:, j, :], PT[sl][0:M, 128 * j:128 * (j + 1)], vr, first, True, [b_PT[sl]] + vreads, [b_bank[ob]], skip=True)
                    first = False

        LA = 2
        for idx in range(len(flat)):
            if idx == 0:
                for k in range(min(LA, len(flat))):
                    qk(k)
            if idx + LA < len(flat):
                qk(idx + LA)
            pv(idx)
            ui, si, st, u = flat[idx]
            if si == len(u["steps"]) - 1:
                u["epi"](pO_banks[ui % len(pO_banks)])

    def proj_phase(hT, b_hT, wsrc, b_wsrc, ncols_total, fm_list, tm_list, wring, b_wring, stg, b_stg, pbanks, gstg=None, b_gstg=None, w32=None, mode=None, order=None, after_blk=None):
        nblk = (ncols_total + 511) // 512
        cnt = [0, 0]
        if order is None:
            order = list(range(nblk))

        def load_blk(pos, part=None):
            blk = order[pos]
            bc0 = blk * 512
            bw = min(512, ncols_total - bc0)
            wi = pos % len(wring)
            if w32 is not None:
                w32src, w32stage, b_w32stage = w32
                wj = pos % len(w32stage)
                if part != "cast":
                    P.dma("sp", w32stage[wj][:, :, 0:bw], w32src.rearrange("(kc p) n -> p kc n", p=128)[:, :, bc0:bc0 + bw],
                          [], [b_w32stage[wj]], "w32s%d" % wj)
                if part == "dma":
                    return
                act(wring[wi][:, :, 0:bw], w32stage[wj][:, :, 0:bw], AF.Copy, [b_w32stage[wj]], [b_wring[wi]])
                P.dma("act", wsrc.rearrange("(kc p) n -> p kc n", p=128)[:, :, bc0:bc0 + bw], wring[wi][:, :, 0:bw],
                      [b_wring[wi]], [b_wsrc[blk]], "w32w%d" % wi)
                return
            P.dma("sp", wring[wi][:, :, 0:bw], wsrc.rearrange("(kc p) n -> p kc n", p=128)[:, :, bc0:bc0 + bw],
                  [b_wsrc[blk] if isinstance(b_wsrc, list) else b_wsrc], [b_wring[wi]], "wring%d" % wi)

        if mode in ("prefetch_dma", "prefetch_cast"):
            if w32 is not None:
                load_blk(0, mode[9:])
                load_blk(1, mode[9:])
            elif mode == "prefetch_dma":
                load_blk(0)
                load_blk(1)
            return
        if mode != "skip_prefetch":
            load_blk(0)
            if nblk > 1:
                load_blk(1)
        if mode == "prefetch_only":
            return
        for pos in range(nblk):
            blk = order[pos]
            bc0 = blk * 512
            bw = min(512, ncols_total - bc0)
            wi = pos % len(wring)
            if pos + 2 < nblk:
                load_blk(pos + 2)
            if after_blk is not None and pos > 0 and order[pos - 1] in after_blk:
                after_blk[order[pos - 1]]()
            for (c0, dst, b_dst, scale) in fm_list:
                if not (bc0 <= c0 < bc0 + bw):
                    continue
                lc = c0 - bc0
                for tc in range(4):
                    bk = pbanks[cnt[0] % 2]
                    si = cnt[0] % len(stg)
                    cnt[0] += 1
                    for kc in range(8):
                        mm(banks[bk][:, :], wring[wi][:, kc, lc:lc + 128], hT[:, kc, tc * 512:(tc + 1) * 512], kc == 0, kc == 7,
                           [b_wring[wi], b_hT[tc]], [b_bank[bk]])
                    evac(stg[si], banks[bk][:, :], [b_bank[bk]], [b_stg[si]], scale=scale)
                    P.dma("sp", dst[:, tc * 512:(tc + 1) * 512], stg[si], [b_stg[si]], [b_dst], "stg%d" % si)
            for (c0, ncols, dst, b_dst, kind) in tm_list:
                if not (bc0 <= c0 < bc0 + bw):
                    continue
                lc = c0 - bc0
                for t16 in range(16):
                    bk = pbanks[cnt[0] % 2]
                    cnt[0] += 1
                    for kc in range(8):
                        mm(banks[bk][:, 0:ncols], hT[:, kc, t16 * 128:(t16 + 1) * 128], wring[wi][:, kc, lc:lc + ncols], kc == 0, kc == 7,
                           [b_wring[wi], b_hT[t16 // 4]], [b_bank[bk]])
                    if kind == "sig":
                        gi = cnt[1] % len(gstg)
                        cnt[1] += 1
                        act(gstg[gi], banks[bk][:, 0:ncols], AF.Sigmoid, [b_bank[bk]], [b_gstg[gi]])
                        P.dma("sp", dst[t16 * 128:(t16 + 1) * 128, :], gstg[gi], [b_gstg[gi]], [b_dst], "gstg%d" % gi)
                    else:
                        si = cnt[0] % len(stg)
                        evac(stg[si][:, 0:ncols], banks[bk][:, 0:ncols], [b_bank[bk]], [b_stg[si]])
                        P.dma("sp", dst[t16 * 128:(t16 + 1) * 128, :], stg[si][:, 0:ncols], [b_stg[si]], [b_dst], "stg%d" % si)

    def load_hT(s, src_x, gi, after4=None):
        hT = A.alloc([128, 8, S], BF16)
        b_hT = [Buf() for _ in range(4)]
        xt = [A.alloc([128, D], F32) for _ in range(6)]
        b_xt = [Buf() for _ in range(6)]
        for t16 in range(16):
            xi = t16 % 6
            P.dma("sp", xt[xi], src_x[s, t16 * 128:(t16 + 1) * 128, :], [b_xs[s][t16]], [b_xt[xi]], "ldx%d" % xi)
            norm_T(xt[xi], b_xt[xi], gi, hT[:, :, t16 * 128:(t16 + 1) * 128], b_hT[t16 // 4])
            if t16 == 3 and after4 is not None:
                after4()
        return hT, b_hT

    def outproj_residual(s, qc, o_bf, b_obf, wout, b_wout, src_x, oT, b_oT, xres, b_xres, obanks):
        for j in range(4):
            bk = PT_BANKS[j % 2]
            pv = bank_bf(bk).rearrange("p (a b) -> p a b", b=128)
            for kc in range(8):
                tr(pv[:, kc, :], o_bf[:, j, kc * 128:(kc + 1) * 128], ident, [b_obf, b_ident], [b_bank[bk]])
            evac(oT[:, :, j * 128:(j + 1) * 128], pv, [b_bank[bk]], [b_oT])
        for j in range(4):
            t16 = qc * 4 + j
            xi = j % 2
            P.dma("sp", xres[xi], src_x[s, t16 * 128:(t16 + 1) * 128, :], [b_xs[s][t16]], [b_xres[xi]], "xres%d" % xi)
            for nh in range(2):
                bk = obanks[(j * 2 + nh) % 2]
                for kc in range(8):
                    mm(banks[bk][:, :], oT[:, kc, j * 128:(j + 1) * 128], wout[:, kc, nh * 512:(nh + 1) * 512], kc == 0, kc == 7,
                       [b_oT, b_wout], [b_bank[bk]])
                tt("dve", xres[xi][:, nh * 512:(nh + 1) * 512], banks[bk][:, :], xres[xi][:, nh * 512:(nh + 1) * 512], ALU.add,
                   [b_bank[bk], b_xres[xi]], [b_xres[xi]])
            P.dma("sp", xs[s, t16 * 128:(t16 + 1) * 128, :], xres[xi], [b_xres[xi]], [b_xs[s][t16]], "xst%d" % xi)

    def nsa_setup_tables():
        m = A.mark()
        rb = A.alloc([33, H], F32)
        rb31 = A.alloc([33, H], F32)
        tv = A.alloc([33, H], BF16)
        oh = A.alloc([33, T1W], BF16)
        t1s = A.alloc([H, T1W], BF16)
        b = Buf()
        P.op("dve", lambda e: e.memset(rb, NEG), [], [b])
        P.op("dve", lambda e: e.memset(rb31, 0.0), [b], [b])
        P.dma("sp", rb[0:32, :], rel_bias, [b], [b], "t1a")
        P.dma("sp", rb31[0:32, :], rel_bias[31, :].partition_broadcast(32), [b], [b], "t1a")
        P.dma("sp", oh, cst["c_oh1d"], [], [b], "t1a")
        tt("dve", tv, rb, rb31, ALU.subtract, [b], [b])
        for c4 in range(4):
            mm(banks[7][0:H, 0:400], tv, oh[:, c4 * 400:(c4 + 1) * 400], True, True, [b], [b_bank[7]])
            cp("dve", t1s[:, c4 * 400:(c4 + 1) * 400], banks[7][0:H, 0:400], [b_bank[7]], [b])
        P.dma("sp", t1_d, t1s, [b], [b_t1d], "t1a")
        P.barrier()
        A.reset(m)

    def nsa_layer(s, src_x):
        m00 = A.mark()
        kcaug = A.alloc([128, G, 4, 128], BF16)
        vcx = A.alloc([128, G, 97], BF16)
        b_kcmp, b_vcx = Buf(), Buf()
        P.op("dve", lambda e: e.memset(kcaug, 0.0), [], [b_kcmp])
        P.op("dve", lambda e: e.memset(vcx, 1.0), [], [b_vcx])
        m0 = A.mark()
        wring = [A.alloc([128, 8, 512], BF16) for _ in range(3)]
        b_wring = [Buf() for _ in range(3)]
        stg = [A.alloc([128, 512], BF16) for _ in range(4)]
        b_stg = [Buf() for _ in range(4)]
        gstg = [A.alloc([128, 48], F32) for _ in range(2)]
        b_gstg = [Buf() for _ in range(2)]
        w32 = None
        if s == 0:
            w32stage = [A.alloc([128, 8, 512], F32) for _ in range(2)]
            w32 = (nsa_w_in[0], w32stage, [Buf() for _ in range(2)])
        proj_phase(None, None, nwinb, b_nwinb_blk, NSA_IN, [], [], wring, b_wring, stg, b_stg, (6, 7), gstg, b_gstg, w32=w32, mode="prefetch_dma", order=[2, 0, 1, 3, 4, 5])

        def after4():
            proj_phase(None, None, nwinb, b_nwinb_blk, NSA_IN, [], [], wring, b_wring, stg, b_stg, (6, 7), gstg, b_gstg, w32=w32, mode="prefetch_cast",
                       order=[2, 0, 1, 3, 4, 5])
            for g in range(G):
                P.dma("sp", vcx[0:127, g, 65:97], cst["c_ovl"], [b_vcx], [b_vcx], "nc4")
                P.dma("sp", kcaug[64:128, g, :, 0:127], cst["c_selp"], [b_kcmp], [b_kcmp], "nc4")
        hT, b_hT = load_hT(s, src_x, 0, after4=after4)
        fm = []
        for i in range(8):
            fm.append((i * 128, qT_d[i * 128:(i + 1) * 128, :], b_qTd, 0.125))
        for i, c0 in enumerate([1024, 1152, 1280, 1408, 1536, 1664, 2048, 2176]):
            fm.append((c0, kT_d[i * 128:(i + 1) * 128, :], b_kTd, None))
        tm = [(1792, 256, v_d[:, 0:256], b_vd, "copy"), (2304, 256, v_d[:, 256:512], b_vd, "copy"),
              (2560, 48, gates_d, b_gd, "sig")]

        def compress():
            kcT = A.alloc([64, G, S], BF16)
            vcT = A.alloc([64, G, S], BF16)
            b_kc = Buf()
            P.dma("sp", kcT, kT_d[0:256, :].rearrange("(g p) t -> p g t", p=64), [b_kTd], [b_kc], "nb0")
            P.dma("sp", vcT, kT_d[256:512, :].rearrange("(g p) t -> p g t", p=64), [b_kTd], [b_kc], "nb0")
            w1s, w2s, peT, b_cw = CW["w1s"], CW["w2s"], CW["peT"], CW["b"]
            cstv = A.alloc([64, 2], F32)
            b_cst = Buf()
            gu = [A.alloc([64, 512], F32) for _ in range(4)]
            gbf = A.alloc([64, 512], BF16)
            b_g = Buf()
            for ni, nm in enumerate(("k", "v")):
                for l in range(32):
                    mm(banks[0][0:64, 0:2], w1s[nm][:, l, :], peT[nm][:, l:l + 2], l == 0, l == 31, [b_cw], [b_bank[0]])
                cp("dve", cstv[:, ni:ni + 1], banks[0][0:64, 0:1], [b_bank[0]], [b_cst])
            for ni, (nm, srcT) in enumerate((("k", kcT), ("v", vcT))):
                bk = 1 + ni
                v4 = srcT.rearrange("p g (c r) -> p g c r", r=16)
                for g in range(G):
                    for l in range(32):
                        mm(banks[bk][0:64, g * 128:g * 128 + 127], w1s[nm][:, l, :], v4[:, g, (l // 16):(l // 16) + 127, l % 16], l == 0, l == 31,
                           [b_cw, b_kc], [b_bank[bk]])
                u, u2, t3, th = gu
                act(u, banks[bk][0:64, :], AF.Identity, [b_bank[bk], b_cst], [b_g], bias=cstv[:, ni:ni + 1])
                tt("dve", u2, u, u, ALU.mult, [b_g], [b_g])
                ts("dve", u2, u2, 0.044715, 1.0, ALU.mult, ALU.add, [b_g], [b_g])
                tt("dve", t3, u2, u, ALU.mult, [b_g], [b_g])
                act(th, t3, AF.Tanh, [b_g], [b_g], scale=0.7978845608028654)
                ts("dve", th, th, 1.0, 0.5, ALU.add, ALU.mult, [b_g], [b_g])
                tt("dve", gbf, th, u, ALU.mult, [b_g], [b_g])
                if nm == "k":
                    for g in range(G):
                        mm(banks[3][0:64, g * 128:g * 128 + 127], w2s["k"], gbf[:, g * 128:g * 128 + 127], True, True, [b_g, b_cw], [b_bank[3]], skip=True)
                    for g in range(G):
                        cp("dve", kcaug[0:64, g, :, 0:127], banks[3][0:64, g * 128:g * 128 + 127].unsqueeze(1).broadcast_to([64, 4, 127]),
                           [b_bank[3]], [b_kcmp])
                else:
                    for g in range(G):
                        mm(banks[0][0:127, g * 64:(g + 1) * 64], gbf[:, g * 128:g * 128 + 127], w2s["v"], True, True, [b_g, b_cw], [b_bank[0]], skip=True)
                    cp("dve", vcx[0:127, :, 0:64], banks[0][0:127, 0:256].rearrange("p (g d) -> p g d", d=64), [b_bank[0]], [b_vcx])

        proj_phase(hT, b_hT, nwinb, b_nwinb_blk, NSA_IN, fm, tm, wring, b_wring, stg, b_stg, (6, 7), gstg, b_gstg, w32=w32, mode="skip_prefetch",
                   order=[2, 0, 1, 3, 4, 5], after_blk={2: compress})
        P.barrier()
        A.reset(m0)

        deferred_casts(s)
        cd = A.alloc([128, H, 256], BF16)
        jmat = A.alloc([128, 128], BF16)
        addm = A.alloc([128, 16, 32], F32)
        mulm = A.alloc([128, 16, 32], F32)
        we = A.alloc([128, 128], BF16)
        wout = A.alloc([128, 8, D], BF16)
        ksaug = A.alloc([128, G, S], BF16)
        kwaug = A.alloc([128, G, S], BF16)
        vsx = A.alloc([128, 16, G, 65], BF16)
        vwx = A.alloc([128, 16, G, 65], BF16)
        b_c, b_c2, b_wout = Buf(), Buf(), Buf()
        b_kw, b_ks, b_kpad = Buf(), Buf(), Buf()
        b_vw, b_vs, b_vpad = Buf(), Buf(), Buf()
        P.op("dve", lambda e: e.memset(kwaug[64:128, :, :], 0.0), [], [b_kpad])
        P.op("dve", lambda e: e.memset(vwx[:, :, :, 64:65], 1.0), [], [b_vpad])
        P.op("dve", lambda e: e.memset(ksaug[96:128, :, :], 0.0), [], [b_kpad])
        P.op("dve", lambda e: e.memset(vsx[:, :, :, 64:65], 1.0), [], [b_vpad])
        P.dma("sp", cd, bass.AP(t1_d.tensor, T1Z - 127, [[1, 128], [T1W, H], [1, 256]]), [b_t1d], [b_c], "nc0")
        P.dma("sp", jmat, cst["c_jmat"], [], [b_c], "nc0")
        P.dma("sp", we, cst["c_we"], [], [b_c], "nc0")
        for g in range(G):
            P.dma("sp", kwaug[0:64, g, :], kT_d[768 + g * 64:768 + (g + 1) * 64, :], [b_kTd], [b_kw], "nc2")
        for kt in range(16):
            P.dma("sp", vwx[:, kt, :, 0:64], v_d[kt * 128:(kt + 1) * 128, 256:512].rearrange("p (g d) -> p g d", d=64), [b_vd], [b_vw], "nc3")
        P.dma("sp", addm, cst["c_addm"], [], [b_c2], "nc0")
        P.dma("sp", mulm, cst["c_mulm"], [], [b_c2], "nc0")
        for g in range(G):
            P.dma("sp", ksaug[0:64, g, :], kT_d[512 + g * 64:512 + (g + 1) * 64, :], [b_kTd], [b_ks], "nc2")
            P.dma("sp", ksaug[64:96, g, :], cst["c_ek"], [], [b_ks], "nc2")
        for kt in range(16):
            P.dma("sp", vsx[:, kt, :, 0:64], v_d[kt * 128:(kt + 1) * 128, 0:256].rearrange("p (g d) -> p g d", d=64), [b_vd], [b_vs], "nc3")
        P.dma("sp", wout, nwoutb.rearrange("(kc p) n -> p kc n", p=128), [b_nwoutb], [b_wout], "nc1")
        Qc = [A.alloc([128, 512], BF16) for _ in range(3)]
        Qsw = [A.alloc([128, 512], BF16) for _ in range(3)]
        b_Qc = [Buf() for _ in range(3)]
        b_Qsw = [Buf() for _ in range(3)]
        for i in range(3):
            P.op("dve", lambda e, i=i: e.memset(Qsw[i][64:128, :], 0.0), [], [b_Qsw[i]])
        PT = [A.alloc([128, 512], BF16) for _ in range(5)]
        b_PT = [Buf() for _ in range(5)]
        o_acc = A.alloc([128, 4, D], F32)
        o_bf = A.alloc([128, 4, D], BF16)
        b_oacc, b_obf = Buf(), Buf()
        oT = A.alloc([128, 8, 512], BF16)
        b_oT = Buf()
        xres = [A.alloc([128, D], F32) for _ in range(2)]
        b_xres = [Buf() for _ in range(2)]
        gts = A.alloc([128, 4, 48], F32)
        b_gts = Buf()
        imp = A.alloc([128, 4, 32], F32)
        sc = A.alloc([128, 4, 32], F32)
        sc2 = A.alloc([128, 4, 32], F32)
        mx8 = A.alloc([128, 8], F32)
        mkb = A.alloc([128, 4, 96], BF16)
        b_imp, b_sc = Buf(), Buf()
        P.op("dve", lambda e: e.memset(mkb, 0.0), [], [b_sc])
        sm = [A.alloc([128, 4], F32) for _ in range(6)]
        b_sm = [Buf() for _ in range(6)]
        tmpo = [A.alloc([128, 4, 64], F32) for _ in range(2)]
        b_tmpo = [Buf() for _ in range(2)]
        tmpi = A.alloc([128, 4, 32], F32)
        b_tmpi = Buf()
        ucnt = [0]
        qcc = [0]
        qsc = [0]

        selTs = A.alloc([128, G, 512], BF16)
        b_selTs = [Buf() for _ in range(G)]
        stgo = [A.alloc([128, 4 * 97], F32) for _ in range(3)]
        b_stgo = [Buf() for _ in range(3)]
        for qc in range(4):
            q0 = qc * 512
            P.dma("sp", gts, gates_d[q0:q0 + 512, :].rearrange("(j p) c -> p j c", p=128), [b_gd, b_gts], [b_gts], "gts")
            use_sel = qc >= 2

            def epilogue(ob, h, br, ncol, first_branch, with_imp, hg):
                k = ucnt[0] % 6
                k2 = (ucnt[0] + 3) % 6
                ucnt[0] += 1
                po = banks[ob][:, 0:4 * ncol].rearrange("p (j c) -> p j c", c=ncol)
                b_po = b_bank[ob]
                if br == 0 and qc == 0:
                    ts("dve", sm[k], po[:, :, 64], 1e-30, None, ALU.max, None, [b_po], [b_sm[k]])
                    P.op("dve", lambda e: e.reciprocal(out=sm[k], in_=sm[k]), [b_sm[k]], [b_sm[k]])
                else:
                    P.op("dve", lambda e: e.reciprocal(out=sm[k], in_=po[:, :, 64]), [b_po], [b_sm[k]])
                if with_imp and use_sel:
                    rb_ = sm[k][:, :].unsqueeze(2).broadcast_to([128, 4, 32])
                    if hg == 0:
                        tt("dve", imp, po[:, :, 65:97], rb_, ALU.mult, [b_po, b_sm[k]], [b_imp])
                    else:
                        tt("dve", tmpi, po[:, :, 65:97], rb_, ALU.mult, [b_po, b_sm[k]], [b_tmpi])
                        tt("dve", imp, imp, tmpi, ALU.add, [b_tmpi, b_imp], [b_imp])
                tt("dve", sm[k2], sm[k], gts[:, :, br * 16 + h], ALU.mult, [b_sm[k], b_gts], [b_sm[k2]])
                rg = sm[k2][:, :].unsqueeze(2).broadcast_to([128, 4, 64])
                osl = o_acc[:, :, h * 64:(h + 1) * 64]
                if first_branch:
                    tt("dve", osl, po[:, :, 0:64], rg, ALU.mult, [b_po, b_sm[k2]], [b_oacc])
                else:
                    ti = ucnt[0] % 2
                    tt("dve", tmpo[ti], po[:, :, 0:64], rg, ALU.mult, [b_po, b_sm[k2]], [b_tmpo[ti]])
                    tt("dve", osl, osl, tmpo[ti], ALU.add, [b_tmpo[ti], b_oacc], [b_oacc])

            def selection(g):
                bk_sel = PT_BANKS[g % 2]
                pvw = bank_bf(bk_sel)
                tt("dve", sc, imp, mulm[:, qc * 4:(qc + 1) * 4, :], ALU.mult, [b_imp, b_c2], [b_sc])
                tt("dve", sc, sc, addm[:, qc * 4:(qc + 1) * 4, :], ALU.add, [b_sc, b_c2], [b_sc])
                for j in range(4):
                    P.op("dve", lambda e, j=j: e.max(out=mx8, in_=sc[:, j, :]), [b_sc], [b_sc])
                    P.op("dve", lambda e, j=j: e.match_replace(out=sc2[:, j, :], in_to_replace=mx8, in_values=sc[:, j, :], imm_value=-3e38),
                         [b_sc], [b_sc])
                    P.op("dve", lambda e, j=j: e.max(out=mx8, in_=sc2[:, j, :]), [b_sc], [b_sc])
                    ts("dve", sc2[:, j, :], sc[:, j, :], mx8[:, 7:8], None, ALU.is_ge, None, [b_sc], [b_sc])
                ts("dve", mkb[:, :, 64:96], sc2, -NEG, NEG, ALU.mult, ALU.add, [b_sc], [b_sc])
                for j in range(4):
                    tr(pvw[0:96, j * 128:(j + 1) * 128], mkb[:, j, :], ident, [b_sc, b_ident], [b_bank[bk_sel]])
                cp("dve", selTs[64:96, g, :], pvw[64:96, 0:512], [b_bank[bk_sel]], [b_selTs[g]])

            def near_corr(mms, h, delta):
                if delta > 128:
                    return
                a0 = max(0, -delta)
                a1 = min(512, 256 - delta)
                mms.append((a0, a1, jmat, cd[:, h, delta + a0:delta + a1], [b_c]))

            units = []
            Mc = min(32 * (qc + 1), 127)
            for g in range(G):
                for hg in range(4):
                    h = g * 4 + hg
                    qi = qcc[0] % 3
                    qcc[0] += 1

                    def pre(h=h, qi=qi):
                        P.dma("sp", Qc[qi][0:64, :], qT_d[h * 64:(h + 1) * 64, q0:q0 + 512], [b_qTd, b_Qc[qi]], [b_Qc[qi]], "nQc%d" % qi)
                        P.dma("sp", Qc[qi][64:128, :], bass.AP(t1_d.tensor, h * T1W + T1Z + 481 - 1008, [[16, 64], [1, 512]]),
                              [b_t1d, b_Qc[qi]], [b_Qc[qi]], "nQcb%d" % qi)
                    mms = [(0, 512, kcaug[:, g, qc, 0:Mc], Qc[qi], [b_kcmp, b_Qc[qi]])]
                    st = dict(M=Mc, c0=0, c1=512, mms=mms, v=(vcx[0:Mc, g, :], [b_vcx]), ncol=97)

                    def epi_c(ob, h=h, hg=hg, g=g):
                        epilogue(ob, h, 0, 97, True, True, hg)
                        if hg == 3 and use_sel:
                            selection(g)
                    units.append(dict(steps=[st], pre=pre, epi=epi_c))
                    steps = []
                    for kt in range(max(0, 4 * qc - 4), 4 * qc + 4):
                        k0 = kt * 128
                        delta = q0 - k0
                        c0 = max(0, -delta)
                        c1 = min(512, 640 - delta)
                        mms = [(c0, c1, kwaug[:, g, k0:k0 + 128], Qc[qi][:, c0:c1], [b_kw, b_kpad, b_Qc[qi]])]
                        near_corr(mms, h, delta)
                        if delta >= 128:
                            f0 = 512 - delta
                            mms.append((f0, f0 + 128, ident, we, [b_c, b_ident]))
                        steps.append(dict(M=128, c0=c0, c1=c1, mms=mms, v=(vwx[:, kt, g, :], [b_vw, b_vpad]), ncol=65))
                    units.append(dict(steps=steps, pre=None, epi=(lambda ob, h=h, hg=hg: epilogue(ob, h, 2, 65, False, False, hg))))
            for g in range(G):
                for hg in range(4):
                    h = g * 4 + hg
                    qi = qsc[0] % 3
                    qsc[0] += 1

                    def pre(h=h, qi=qi, g=g):
                        P.dma("sp", Qsw[qi][0:64, :], qT_d[h * 64:(h + 1) * 64, q0:q0 + 512], [b_qTd, b_Qsw[qi]], [b_Qsw[qi]], "nQs%d" % qi)
                        if use_sel:
                            cp("dve", Qsw[qi][64:96, :], selTs[64:96, g, :], [b_selTs[g], b_Qsw[qi]], [b_Qsw[qi]])
                    steps = []
                    for kt in range(4 * (qc + 1)):
                        k0 = kt * 128
                        delta = q0 - k0
                        c0 = max(0, -delta)
                        mms = [(c0, 512, ksaug[:, g, k0:k0 + 128], Qsw[qi][:, c0:512], [b_ks, b_kpad, b_Qsw[qi]])]
                        near_corr(mms, h, delta)
                        steps.append(dict(M=128, c0=c0, c1=512, mms=mms, v=(vsx[:, kt, g, :], [b_vs, b_vpad]), ncol=65))
                    units.append(dict(steps=steps, pre=pre, epi=(lambda ob, h=h, hg=hg: epilogue(ob, h, 1, 65, False, False, hg))))
            run_units(units, PT, b_PT, (0, 1, 6), (2, 3, 7))
            cp("dve", o_bf, o_acc, [b_oacc], [b_obf])
            outproj_residual(s, qc, o_bf, b_obf, wout, b_wout, src_x, oT, b_oT, xres, b_xres, (6, 7))
        P.barrier()
        A.reset(m00)

    def fox_layer(s, src_x):
        m0 = A.mark()
        wring = [A.alloc([128, 8, 512], BF16) for _ in range(3)]
        b_wring = [Buf() for _ in range(3)]
        stg = [A.alloc([128, 512], BF16) for _ in range(4)]
        b_stg = [Buf() for _ in range(4)]
        proj_phase(None, None, fwinb, b_fwinb, 3072, [], [], wring, b_wring, stg, b_stg, (6, 7), mode="prefetch_only")
        hT, b_hT = load_hT(s, src_x, 1)
        fm = []
        for i in range(8):
            fm.append((i * 128, qT_d[i * 128:(i + 1) * 128, :], b_qTd, 0.125))
        for i in range(8):
            fm.append((1024 + i * 128, kT_d[i * 128:(i + 1) * 128, :], b_kTd, None))
        tm = [(2048, 512, v_d[:, 0:512], b_vd, "copy"), (2560, 512, v_d[:, 512:1024], b_vd, "copy")]
        wf = A.alloc([128, 8, H], BF16)
        b_wf = Buf()
        P.dma("sp", wf, fwinb.rearrange("(kc p) n -> p kc n", p=128)[:, :, 3072:3088], [b_fwinb], [b_wf], "wf")
        fl = A.alloc([H, S], F32)
        b_fl = Buf()
        bfv = A.alloc([H, 1], F32)
        P.dma("sp", bfv, fox_b_f.rearrange("o h -> h o"), [], [b_fl], "bfv", allow_slow_non_contiguous=True)
        for tc in range(4):
            bk = 6 + tc % 2
            for kc in range(8):
                mm(banks[bk][0:H, :], wf[:, kc, :], hT[:, kc, tc * 512:(tc + 1) * 512], kc == 0, kc == 7, [b_wf, b_hT[tc]], [b_bank[bk]])
            act(fl[:, tc * 512:(tc + 1) * 512], banks[bk][0:H, :], AF.Identity, [b_bank[bk], b_fl], [b_fl], bias=bfv[:, 0:1])
        az = A.alloc([H, S], F32)
        mz = A.alloc([H, S], F32)
        onesr = A.alloc([H, S], F32)
        cpp = A.alloc([H, 6, S], BF16)
        P.op("pool", lambda e: e.memset(onesr, 1.0), [], [b_fl])
        act(az, fl, AF.Abs, [b_fl], [b_fl])
        act(az, az, AF.Exp, [b_fl], [b_fl], scale=-1.0)
        act(az, az, AF.Ln, [b_fl], [b_fl], bias=1.0)
        ts("dve", mz, fl, 0.0, None, ALU.min, None, [b_fl], [b_fl])
        tt("dve", mz, mz, az, ALU.subtract, [b_fl], [b_fl])
        P.op("dve", lambda e: e.tensor_tensor_scan(out=az, data0=onesr, data1=mz, initial=0.0, op0=ALU.mult, op1=ALU.add), [b_fl], [b_fl])
        cp("dve", cpp[:, 0, :], az, [b_fl], [b_fl])
        tt("dve", mz, az, cpp[:, 0, :], ALU.subtract, [b_fl], [b_fl])
        cp("dve", cpp[:, 1, :], mz, [b_fl], [b_fl])
        tt("dve", mz, mz, cpp[:, 1, :], ALU.subtract, [b_fl], [b_fl])
        cp("dve", cpp[:, 2, :], mz, [b_fl], [b_fl])
        ts("dve", cpp[:, 3:6, :], cpp[:, 0:3, :], -1.0, None, ALU.mult, None, [b_fl], [b_fl])
        P.dma("pool", cpart_d.rearrange("i h t -> h i t"), cpp, [b_fl], [b_cpd], "cpd")
        proj_phase(hT, b_hT, fwinb, b_fwinb, 3072, fm, tm, wring, b_wring, stg, b_stg, (6, 7), mode="skip_prefetch")
        P.barrier()
        A.reset(m0)

        cm = A.alloc([128, 128], BF16)
        b_c = Buf()
        P.dma("sp", cm, cst["c_cm"], [], [b_c], "fc0")
        wout = A.alloc([128, 8, D], BF16)
        b_wout = Buf()
        vfx = A.alloc([128, 16, H, 65], BF16)
        b_vk = [Buf() for _ in range(16)]
        b_vpad = Buf()
        P.op("dve", lambda e: e.memset(vfx[:, :, :, 64:65], 1.0), [], [b_vpad])
        for kt in range(16):
            P.dma("sp", vfx[:, kt, :, 0:64], v_d[kt * 128:(kt + 1) * 128, :].rearrange("p (h d) -> p h d", d=64), [b_vd], [b_vk[kt]], "fc3")
        P.dma("sp", wout, fwoutb.rearrange("(kc p) n -> p kc n", p=128), [b_fwoutb], [b_wout], "fc1")
        Kh = [A.alloc([128, S], BF16) for _ in range(2)]
        Qh = [A.alloc([128, 512], BF16) for _ in range(3)]
        b_Kh = [Buf() for _ in range(2)]
        b_Qh = [Buf() for _ in range(3)]
        for i in range(2):
            P.op("dve", lambda e, i=i: e.memset(Kh[i][64:128, :], 0.0), [], [b_Kh[i]])
            P.op("dve", lambda e, i=i: e.memset(Kh[i][64:67, :], 1.0), [b_Kh[i]], [b_Kh[i]])
        for i in range(3):
            P.op("dve", lambda e, i=i: e.memset(Qh[i][64:128, :], 0.0), [], [b_Qh[i]])
            P.op("dve", lambda e, i=i: e.memset(Qh[i][96:99, :], 1.0), [b_Qh[i]], [b_Qh[i]])
        PT = [A.alloc([128, 512], BF16) for _ in range(5)]
        b_PT = [Buf() for _ in range(5)]
        o_bf = A.alloc([128, 16, D], BF16)
        b_obf = Buf()
        oT = A.alloc([128, 8, 512], BF16)
        b_oT = Buf()
        xres = [A.alloc([128, D], F32) for _ in range(2)]
        b_xres = [Buf() for _ in range(2)]
        sm = [A.alloc([128, 4], F32) for _ in range(4)]
        b_sm = [Buf() for _ in range(4)]
        ucnt = [0]
        qcnt = [0]
        units = []
        for h in range(H):
            ki = h % 2
            for qc in range(4):
                q0 = qc * 512
                qi = qcnt[0] % 3
                qcnt[0] += 1

                def pre(h=h, ki=ki, qc=qc, q0=q0, qi=qi):
                    if qc == 0:
                        P.dma("sp", Kh[ki][0:64, :], kT_d[h * 64:(h + 1) * 64, :], [b_kTd, b_Kh[ki]], [b_Kh[ki]], "fK%d" % ki)
                        P.dma("sp", Kh[ki][96:99, :], cpart_d[3:6, h, :], [b_cpd, b_Kh[ki]], [b_Kh[ki]], "fKb%d" % ki)
                    P.dma("sp", Qh[qi][0:64, :], qT_d[h * 64:(h + 1) * 64, q0:q0 + 512], [b_qTd, b_Qh[qi]], [b_Qh[qi]], "fQ%d" % qi)
                    P.dma("sp", Qh[qi][64:67, :], cpart_d[0:3, h, q0:q0 + 512], [b_cpd, b_Qh[qi]], [b_Qh[qi]], "fQb%d" % qi)
                steps = []
                for kt in range(4 * (qc + 1)):
                    k0 = kt * 128
                    delta = q0 - k0
                    c0 = max(0, -delta)
                    mms = [(c0, 512, Kh[ki][:, k0:k0 + 128], Qh[qi][:, c0:512], [b_Kh[ki], b_Qh[qi]])]
                    if delta <= 0:
                        mms.append((c0, c0 + 128, ident, cm, [b_c, b_ident]))
                    steps.append(dict(M=128, c0=c0, c1=512, mms=mms, v=(vfx[:, kt, h, :], [b_vk[kt], b_vpad]), ncol=65))

                def epi(ob, h=h, qc=qc):
                    k = ucnt[0] % 4
                    ucnt[0] += 1
                    po = banks[ob][:, 0:260].rearrange("p (j c) -> p j c", c=65)
                    P.op("dve", lambda e: e.reciprocal(out=sm[k], in_=po[:, :, 64]), [b_bank[ob]], [b_sm[k]])
                    rg = sm[k][:, :].unsqueeze(2).broadcast_to([128, 4, 64])
                    tt("dve", o_bf[:, qc * 4:(qc + 1) * 4, h * 64:(h + 1) * 64], po[:, :, 0:64], rg, ALU.mult, [b_bank[ob], b_sm[k]], [b_obf])
                units.append(dict(steps=steps, epi=epi, pre=pre))
        run_units(units, PT, b_PT, (0, 1, 6), (2, 3, 7))
        for qc in range(4):
            outproj_residual(s, qc, o_bf[:, qc * 4:(qc + 1) * 4, :], b_obf, wout, b_wout, src_x, oT, b_oT, xres, b_xres, (6, 7))
        P.barrier()
        A.reset(m0)

    def mlp_layer(l, src_x, last):
        m0 = A.mark()
        w2s = A.alloc([128, 32, D], BF16)
        b_w2s = Buf()
        for q4 in range(4):
            P.dma("sp", w2s[:, q4 * 8:(q4 + 1) * 8, :], w2b[l].rearrange("(fc p) n -> p fc n", p=128)[:, q4 * 8:(q4 + 1) * 8, :],
                  [b_w2b[l]], [b_w2s], "w2s")
        NW1 = 2
        w1s = [A.alloc([128, 8, 512], BF16) for _ in range(NW1)]
        b_w1s = [Buf() for _ in range(NW1)]
        aT = A.alloc([128, 32, 512], BF16)
        b_aT = [Buf() for _ in range(32)]
        hTc = [A.alloc([128, 8, 512], BF16) for _ in range(2)]
        b_hTc = [[Buf() for _ in range(4)] for _ in range(2)]
        xt = [A.alloc([128, D], F32) for _ in range(8)]
        b_xt = [Buf() for _ in range(8)]
        rtmp = [A.alloc([128, 512], F32) for _ in range(2)]
        b_rtmp = [Buf() for _ in range(2)]
        yo = [A.alloc([128, D], F32) for _ in range(2)]
        b_yo = [Buf() for _ in range(2)]
        w1cnt = p1cnt = p2cnt = 0
        chunks = [(s_, c_) for s_ in range(nseq) for c_ in range(4)]
        xi_of = {}

        def prep_tile(k, j):
            s_, c_ = chunks[k]
            cb_ = k % 2
            t16 = c_ * 4 + j
            xi = (k * 4 + j) % 8
            xi_of[(k, j)] = xi
            P.dma("sp", xt[xi], src_x[s_, t16 * 128:(t16 + 1) * 128, :], [b_xs[s_][t16]], [b_xt[xi]], "mxt%d" % xi)
            norm_T(xt[xi], b_xt[xi], 2 + l, hTc[cb_][:, :, j * 128:(j + 1) * 128], b_hTc[cb_][j])

        for j in range(4):
            prep_tile(0, j)
        for k, (s, c) in enumerate(chunks):
            cb = k % 2
            for blk in range(8):
                wi = w1cnt % NW1
                w1cnt += 1
                P.dma("sp", w1s[wi], w1b[l].rearrange("(kc p) n -> p kc n", p=128)[:, :, blk * 512:(blk + 1) * 512],
                      [b_w1b[l]], [b_w1s[wi]], "w1s%d" % wi)
                for f4 in range(4):
                    fc = blk * 4 + f4
                    pi = p1cnt % 2
                    p1cnt += 1
                    for kc in range(8):
                        mm(banks[pi][:, :], w1s[wi][:, kc, f4 * 128:(f4 + 1) * 128], hTc[cb][:, kc, :], kc == 0, kc == 7,
                           [b_w1s[wi]] + b_hTc[cb], [b_bank[pi]])
                    act(rtmp[pi], banks[pi][:, :], AF.Relu, [b_bank[pi]], [b_rtmp[pi]])
                    tt("dve", aT[:, fc, :], rtmp[pi], rtmp[pi], ALU.mult, [b_rtmp[pi]], [b_aT[fc]])
                if 1 <= blk <= 4 and k + 1 < len(chunks):
                    prep_tile(k + 1, blk - 1)
            for j in range(4):
                t16 = c * 4 + j
                xi = xi_of[(k, j)]
                for nh in range(2):
                    pi = 2 + p2cnt % 2
                    p2cnt += 1
                    for fc in range(32):
                        mm(banks[pi][:, :], aT[:, fc, j * 128:(j + 1) * 128], w2s[:, fc, nh * 512:(nh + 1) * 512], fc == 0, fc == 31,
                           [b_aT[fc], b_w2s], [b_bank[pi]])
                    tt("dve", xt[xi][:, nh * 512:(nh + 1) * 512], banks[pi][:, :], xt[xi][:, nh * 512:(nh + 1) * 512], ALU.add,
                       [b_bank[pi], b_xt[xi]], [b_xt[xi]])
                if not last:
                    P.dma("pool", xs[s, t16 * 128:(t16 + 1) * 128, :], xt[xi], [b_xt[xi]], [b_xs[s][t16]], "mxst%d" % xi)
                else:
                    rs, b_r, jj = rstd_of(xt[xi], b_xt[xi])
                    yi = t16 % 2
                    P.op("dve", lambda e, xi=xi, yi=yi, rs=rs: e.scalar_tensor_tensor(
                        out=yo[yi], in0=xt[xi], scalar=rs[:, 0:1], in1=gfin, op0=ALU.mult, op1=ALU.mult),
                        [b_xt[xi], b_r, b_gfin], [b_yo[yi]])
                    P.dma("pool", out[s, t16 * 128:(t16 + 1) * 128, :], yo[yi], [b_yo[yi]], [], "yo%d" % yi, is_output=True)
        P.barrier()
        A.reset(m0)

    CW = {"w1s": {}, "w2s": {}, "peT": {}, "b": Buf()}
    if "nsa" in parts:
        for nm, w1, w2, pe in (("k", nsa_wk1, nsa_wk2, nsa_pe_k), ("v", nsa_wv1, nsa_wv2, nsa_pe_v)):
            CW["w1s"][nm] = A.alloc([64, 32, DH], BF16)
            CW["w2s"][nm] = A.alloc([64, 64], BF16)
            CW["peT"][nm] = A.alloc([64, 34], BF16)
            P.op("dve", lambda e, nm=nm: e.memset(CW["peT"][nm], 0.0), [], [CW["b"]])
            P.dma("pool", CW["w1s"][nm], w1[0].rearrange("(l d) o -> d l o", d=DH), [], [CW["b"]], "nb1")
            P.dma("pool", CW["w2s"][nm], w2[0], [], [CW["b"]], "nb1")
            P.dma("pool", CW["peT"][nm][:, 0:32], pe[0].rearrange("l d -> d l"), [CW["b"]], [CW["b"]], "nb1", allow_slow_non_contiguous=True)
        nsa_setup_tables()
        cast_w(nwoutb, nsa_w_out[0], D, b_nwoutb, "cw_b", 2)
    def deferred_casts(s_):
        if s_ == 0:
            if "mlp" in parts:
                cast_w(w1b[0], mlp_w1[0], D, b_w1b[0], "cw_c")
                cast_w(w2b[0], mlp_w2[0], DFF, b_w2b[0], "cw_d")
        elif s_ == nseq - 1 or nseq == 1:
            pass
        if s_ == min(1, nseq - 1):
            if "fox" in parts:
                cast_w(fwinb, fox_w_in[0], D, b_fwinb, "cw_e")
                cast_w(fwoutb, fox_w_out[0], D, b_fwoutb, "cw_f", 2)
            if "mlp" in parts and depth > 1:
                cast_w(w1b[1], mlp_w1[1], D, b_w1b[1], "cw_g")
                cast_w(w2b[1], mlp_w2[1], DFF, b_w2b[1], "cw_h")

    if "nsa" not in parts:
        for s_ in range(nseq):
            deferred_casts(s_)
    cur = x_in
    for l in range(depth):
        if l == 0 and "nsa" in parts:
            for s in range(nseq):
                nsa_layer(s, cur)
            cur = xs
        if l == 1 and "fox" in parts:
            for s in range(nseq):
                fox_layer(s, cur)
            cur = xs
        if "mlp" in parts:
            mlp_layer(l, cur, last=(l == depth - 1))
            cur = xs
    P.emit()
    return nc, es


_CACHE = {}
IN_NAMES = ["rel_bias", "norm_mix", "norm_mlp", "nsa_w_in", "nsa_pe_k", "nsa_wk1", "nsa_wk2", "nsa_pe_v", "nsa_wv1", "nsa_wv2",
            "nsa_w_out", "fox_w_in", "fox_b_f", "fox_w_out", "mlp_w1", "mlp_w2", "final_norm"]


def kernel(**inputs):
    n = 8
    x = np.ascontiguousarray(np.asarray(inputs["x"], dtype=np.float32))
    nseq = x.shape[0] // n
    if "nc" not in _CACHE:
        _CACHE["nc"] = build(nseq=nseq)
    nc, _ = _CACHE["nc"]
    consts = host_consts()
    shared = {k: np.ascontiguousarray(np.asarray(inputs[k], dtype=np.float32)) for k in IN_NAMES}
    shared.update(consts)
    in_maps = []
    for c in range(n):
        m = dict(shared)
        m["x"] = x[c * nseq:(c + 1) * nseq]
        in_maps.append(m)
    res = run_bass_kernel_spmd(nc, in_maps, core_ids=list(range(n)))
    return np.concatenate([r["out"] for r in res.results], axis=0).astype(np.float32)
```

```python
import math
import numpy as np
import ml_dtypes
from contextlib import ExitStack
import concourse.bass as bass
import concourse.mybir as mybir
from concourse.bass_utils import run_bass_kernel_spmd

F32 = mybir.dt.float32
BF16 = mybir.dt.bfloat16
AF = mybir.ActivationFunctionType
ALU = mybir.AluOpType

S = 2048
D = 1024
DFF = 4096
H = 16
DH = 64
G = 4
NSA_IN = 2608
FOX_IN = 3088
EPS = 1e-6
NEG = -30000.0
T1W = 1600
T1Z = 600
SEM_ROLL = 30000


class Buf:
    __slots__ = ("name", "w", "r")

    def __init__(self, name=""):
        self.name = name
        self.w = {}
        self.r = {}


class Prog:
    ENG = ("pe", "act", "dve", "pool", "sp")

    def __init__(self, nc, es):
        self.nc = nc
        self.es = es
        self.ops = {e: [] for e in self.ENG}
        self.nsem = 0
        self.esem = {e: self._newsem() for e in ("pe", "act", "dve", "pool")}
        self.ecnt = {e: 0 for e in self.esem}
        self.waited = {e: {} for e in self.ENG}
        self.dsem = {}
        self.dcnt = {}
        self.allsems = {}
        self.rot = {}
        self.out_tokens = []

    def _newsem(self):
        self.nsem += 1
        return self.es.enter_context(self.nc.semaphore("sm%d" % self.nsem))

    def _deps(self, eng, reads, writes):
        need = {}
        pe_sem = self.esem["pe"]

        def add(sem, val, src):
            if src == "pe" and eng == "pe" and sem is pe_sem:
                return
            if need.get(sem, 0) < val:
                need[sem] = val

        for b in reads:
            for sem, (val, src) in b.w.items():
                add(sem, val, src)
        for b in writes:
            for sem, (val, src) in b.w.items():
                add(sem, val, src)
            for sem, (val, src) in b.r.items():
                add(sem, val, src)
        waits = []
        wd = self.waited[eng]
        for sem, val in need.items():
            if wd.get(sem, 0) < val:
                wd[sem] = val
                waits.append((sem, val))
        return waits

    def _commit(self, tok, reads, writes):
        sem, val, src = tok
        self.allsems[sem] = val
        for b in reads:
            if b.r.get(sem, (0, None))[0] < val:
                b.r[sem] = (val, src)
        for b in writes:
            if b.w.get(sem, (0, None))[0] < val:
                b.w[sem] = (val, src)

    def op(self, eng, fn, reads=(), writes=()):
        if self.ecnt[eng] >= SEM_ROLL:
            self.esem[eng] = self._newsem()
            self.ecnt[eng] = 0
        waits = self._deps(eng, reads, writes)
        self.ecnt[eng] += 1
        sem = self.esem[eng]
        self.ops[eng].append((waits, fn, sem, 1))
        self._commit((sem, self.ecnt[eng], eng), reads, writes)

    ROT_FAM = (("nc", 6), ("nb", 4), ("t1a", 2), ("gcol", 2), ("fc3", 4), ("w2s", 4), ("cw_", 4))

    def dma(self, q, out, in_, reads, writes, key, is_output=False, **kw):
        for fam, nrot in self.ROT_FAM:
            if key.startswith(fam):
                r = self.rot.get(fam, 0)
                self.rot[fam] = r + 1
                key = "%s~%d" % (fam, r % nrot)
                break
        if key in self.dsem and self.dcnt[key] >= 2000:
            del self.dsem[key]
        if key not in self.dsem:
            self.dsem[key] = self._newsem()
            self.dcnt[key] = 0
        waits = self._deps(q, reads, writes)
        sem = self.dsem[key]
        if self.dcnt[key] > 0 and self.waited[q].get(sem, 0) < 16 * self.dcnt[key]:
            self.waited[q][sem] = 16 * self.dcnt[key]
            waits.append((sem, 16 * self.dcnt[key]))
        self.dcnt[key] += 1
        self.ops[q].append((waits, (lambda e: e.dma_start(out=out, in_=in_, **kw)), sem, 16))
        tok = (sem, 16 * self.dcnt[key], q)
        self._commit(tok, reads, writes)
        if is_output:
            self.out_tokens.append(tok)

    def barrier(self):
        snap = dict(self.allsems)
        for eng in self.ENG:
            wd = self.waited[eng]
            waits = []
            for sem, val in snap.items():
                if wd.get(sem, 0) < val:
                    wd[sem] = val
                    waits.append((sem, val))
            if waits:
                self.ops[eng].append((waits, None, None, 0))

    def emit(self):
        nc = self.nc
        fin = {}
        for sem, val, _ in self.out_tokens:
            fin[sem] = max(fin.get(sem, 0), val)
        with nc.Block() as block:
            def run(eng_name):
                def body(e):
                    for waits, fn, sem, inc in self.ops[eng_name]:
                        for s, v in waits:
                            e.wait_ge(s, v)
                        if fn is not None:
                            fn(e).then_inc(sem, inc)
                    if eng_name == "sp":
                        for s, v in fin.items():
                            e.wait_ge(s, v)
                return body
            block.tensor(run("pe"))
            block.scalar(run("act"))
            block.vector(run("dve"))
            block.gpsimd(run("pool"))
            block.sync(run("sp"))


DTSIZE = {F32: 4, BF16: 2}


class Arena:
    def __init__(self, base_ap, nwords):
        self.base = base_ap
        self.n = nwords
        self.off = 0

    def mark(self):
        return self.off

    def reset(self, m):
        self.off = m

    def alloc(self, shape, dt):
        p = shape[0]
        free = list(shape[1:])
        nel = int(np.prod(free))
        words = (nel * DTSIZE[dt] + 3) // 4
        words = (words + 7) // 8 * 8
        assert self.off + words <= self.n, ("arena overflow", self.off, words, self.n)
        v = self.base[0:p, self.off:self.off + words]
        self.off += words
        if dt != F32:
            v = v.bitcast(dt)
        v = v[:, 0:nel]
        if len(free) == 2:
            v = v.rearrange("p (a b) -> p a b", b=free[1])
        elif len(free) == 3:
            v = v.rearrange("p (a b c) -> p a b c", b=free[1], c=free[2])
        return v


def rel_bucket_np(d):
    d = np.asarray(d)
    n = np.maximum(d, 0)
    nf = np.maximum(n, 1).astype(np.float32)
    large = 16 + (np.log(nf / np.float32(16)) / np.float32(math.log(8.0)) * np.float32(16)).astype(np.int32)
    large = np.minimum(large, 31)
    return np.where(n < 16, n, large)


def host_consts():
    bf = ml_dtypes.bfloat16
    c = {}
    d = np.arange(T1W) - T1Z
    b = np.where(d < 0, 32, np.where(d < 128, rel_bucket_np(d), 31))
    oh = np.zeros((33, T1W), np.float32)
    oh[b, np.arange(T1W)] = 1.0
    c["c_oh1d"] = oh.astype(bf)
    selp = np.zeros((64, 4, 127), np.float32)
    for j in range(4):
        for rp in range(64):
            cc = 32 * (j - 1) + 63 - rp
            if 0 <= cc < 127:
                selp[rp, j, cc] = 1.0
    c["c_selp"] = selp.astype(bf)
    c["c_jmat"] = np.ascontiguousarray(np.eye(128, dtype=np.float32)[::-1]).astype(bf)
    ek = np.zeros((32, S), np.float32)
    ek[np.arange(S) // 64, np.arange(S)] = 1.0
    c["c_ek"] = ek.astype(bf)
    addm = np.zeros((128, 16, 32), np.float32)
    mulm = np.ones((128, 16, 32), np.float32)
    for tt in range(16):
        for p in range(128):
            qb = (tt * 128 + p) // 64
            for j in range(32):
                rel = qb - j
                forced = (j == 0) or (0 <= rel < 2)
                vis = rel >= 0
                if not vis:
                    addm[p, tt, j] = -1e30
                    mulm[p, tt, j] = 0.0
                elif forced:
                    addm[p, tt, j] = 1e9
                    mulm[p, tt, j] = 0.0
    c["c_addm"] = addm
    c["c_mulm"] = mulm
    ovl = np.zeros((127, 32), np.float32)
    for cc in range(127):
        for cell in (cc, cc + 1):
            ovl[cc, cell // 4] += 1.0
    c["c_ovl"] = ovl.astype(bf)
    kk = np.arange(128)[:, None]
    qq = np.arange(128)[None, :]
    c["c_we"] = np.where(qq >= kk, NEG, 0.0).astype(np.float32).astype(bf)
    c["c_cm"] = np.where(qq < kk, NEG, 0.0).astype(np.float32).astype(bf)
    return c


CONST_SPECS = {
    "c_oh1d": ([33, T1W], BF16), "c_selp": ([64, 4, 127], BF16), "c_jmat": ([128, 128], BF16),
    "c_ek": ([32, S], BF16), "c_addm": ([128, 16, 32], F32), "c_mulm": ([128, 16, 32], F32),
    "c_ovl": ([127, 32], BF16), "c_we": ([128, 128], BF16), "c_cm": ([128, 128], BF16),
}


def build(nseq=2, parts=("nsa", "fox", "mlp"), depth=2, debug=()):
    nc = bass.Bass("TRN2", target_bir_lowering=False)
    es = ExitStack()
    P = Prog(nc, es)

    def dram_in(name, shape, dt=F32):
        return nc.dram_tensor(name, list(shape), dt, kind="ExternalInput").ap()

    def dram_tmp(name, shape, dt):
        return nc.dram_tensor(name, list(shape), dt, kind=("ExternalOutput" if name in debug else "Internal")).ap()

    x_in = dram_in("x", [nseq, S, D])
    out = nc.dram_tensor("out", [nseq, S, D], F32, kind="ExternalOutput").ap()
    rel_bias = dram_in("rel_bias", [32, H])
    norm_mix = dram_in("norm_mix", [2, D])
    norm_mlp = dram_in("norm_mlp", [2, D])
    nsa_w_in = dram_in("nsa_w_in", [1, D, NSA_IN])
    nsa_pe_k = dram_in("nsa_pe_k", [1, 32, DH])
    nsa_wk1 = dram_in("nsa_wk1", [1, 32 * DH, DH])
    nsa_wk2 = dram_in("nsa_wk2", [1, DH, DH])
    nsa_pe_v = dram_in("nsa_pe_v", [1, 32, DH])
    nsa_wv1 = dram_in("nsa_wv1", [1, 32 * DH, DH])
    nsa_wv2 = dram_in("nsa_wv2", [1, DH, DH])
    nsa_w_out = dram_in("nsa_w_out", [1, D, D])
    fox_w_in = dram_in("fox_w_in", [1, D, FOX_IN])
    fox_b_f = dram_in("fox_b_f", [1, H])
    fox_w_out = dram_in("fox_w_out", [1, D, D])
    mlp_w1 = dram_in("mlp_w1", [2, D, DFF])
    mlp_w2 = dram_in("mlp_w2", [2, DFF, D])
    final_norm = dram_in("final_norm", [D])
    cst = {k: dram_in(k, sh, dt) for k, (sh, dt) in CONST_SPECS.items()}

    xs = dram_tmp("xs", [nseq, S, D], F32)
    b_xs = [[Buf() for _ in range(16)] for _ in range(nseq)]
    w1b = dram_tmp("w1b", [2, D, DFF], BF16)
    w2b = dram_tmp("w2b", [2, DFF, D], BF16)
    nwinb = dram_tmp("nwinb", [D, NSA_IN], BF16)
    nwoutb = dram_tmp("nwoutb", [D, D], BF16)
    fwinb = dram_tmp("fwinb", [D, FOX_IN], BF16)
    fwoutb = dram_tmp("fwoutb", [D, D], BF16)
    b_w1b = [Buf(), Buf()]
    b_w2b = [Buf(), Buf()]
    b_nwinb, b_nwoutb, b_fwinb, b_fwoutb = Buf(), Buf(), Buf(), Buf()
    b_nwinb_blk = [Buf() for _ in range(6)]
    t1_d = dram_tmp("t1_d", [H, T1W], BF16)
    b_t1d = Buf()
    qT_d = dram_tmp("qT_d", [D, S], BF16)
    kT_d = dram_tmp("kT_d", [D, S], BF16)
    v_d = dram_tmp("v_d", [S, D], BF16)
    gates_d = dram_tmp("gates_d", [S, 48], F32)
    cpart_d = dram_tmp("cpart_d", [6, H, S], BF16)
    b_qTd, b_kTd, b_vd, b_gd, b_cpd = Buf(), Buf(), Buf(), Buf(), Buf()

    NW = 51200
    arena_t = es.enter_context(nc.sbuf_tensor("arena", [128, NW], F32))
    A = Arena(arena_t, NW)
    banks = [es.enter_context(nc.psum_tensor("bank%d" % i, [128, 512], F32)) for i in range(8)]
    b_bank = [Buf("bank%d" % i) for i in range(8)]

    def bank_bf(i):
        return banks[i][:].bitcast(BF16)

    def mm(o, lhsT, rhs, start, stop, reads, writes, skip=False):
        P.op("pe", lambda e: e.matmul(o, lhsT, rhs, start=start, stop=stop, skip_group_check=skip), reads, writes)

    def tr(o, i, idn, reads, writes):
        P.op("pe", lambda e: e.transpose(out=o, in_=i, identity=idn), reads, writes)

    def act(o, i, func, reads, writes, **kw):
        P.op("act", lambda e: e.activation(out=o, in_=i, func=func, **kw), reads, writes)

    def tt(eng, o, a, b, op, reads, writes):
        P.op(eng, lambda e: e.tensor_tensor(out=o, in0=a, in1=b, op=op), reads, writes)

    def ts(eng, o, a, s1, s2, op0, op1, reads, writes):
        if s2 is None:
            P.op(eng, lambda e: e.tensor_scalar(out=o, in0=a, scalar1=s1, scalar2=None, op0=op0), reads, writes)
        else:
            P.op(eng, lambda e: e.tensor_scalar(out=o, in0=a, scalar1=s1, scalar2=s2, op0=op0, op1=op1), reads, writes)

    def cp(eng, o, i, reads, writes):
        P.op(eng, lambda e: e.tensor_copy(out=o, in_=i), reads, writes)

    ev_cnt = [0]

    def evac(o, i, reads, writes, scale=None):
        ev_cnt[0] += 1
        if ev_cnt[0] % 2 == 0:
            if scale is None:
                act(o, i, AF.Copy, reads, writes)
            else:
                act(o, i, AF.Copy, reads, writes, scale=float(scale))
        else:
            if scale is None:
                cp("dve", o, i, reads, writes)
            else:
                ts("dve", o, i, float(scale), None, ALU.mult, None, reads, writes)

    ident = A.alloc([128, 128], BF16)
    b_ident = Buf()
    P.op("pool", lambda e: e.memset(ident, 0.0), [], [b_ident])
    P.op("pool", lambda e: e.affine_select(out=ident, in_=ident, pattern=[[-1, 128]], compare_op=ALU.not_equal,
                                           fill=1.0, base=0, channel_multiplier=1), [b_ident], [b_ident])
    gcolT = A.alloc([128, 32], F32)
    gcol = gcolT.rearrange("p (i k) -> p i k", k=8)
    grow = A.alloc([32, 128], F32)
    b_gcol = Buf()
    for i, src in enumerate([norm_mix[0], norm_mix[1], norm_mlp[0], norm_mlp[1]]):
        P.dma("sp", grow[i * 8:(i + 1) * 8, :], src.rearrange("(k p) -> k p", p=128), [], [b_gcol], "gcol")
    for j in range(4):
        P.op("dve", lambda e, j=j: e.transpose(out=gcolT[32 * j:32 * (j + 1), 0:32], in_=grow[0:32, 32 * j:32 * (j + 1)]), [b_gcol], [b_gcol])
    gfin = A.alloc([128, D], F32)
    b_gfin = Buf()
    P.dma("sp", gfin, final_norm.partition_broadcast(128), [], [b_gfin], "gfin")
    ones_f = A.alloc([128, 512], F32)
    b_ones = Buf()
    P.op("pool", lambda e: e.memset(ones_f, 1.0), [], [b_ones])

    NST = 4
    st_ss = [A.alloc([128, 1], F32) for _ in range(NST)]
    st_rs = [A.alloc([128, 1], F32) for _ in range(NST)]
    b_ss = [Buf() for _ in range(NST)]
    b_rs = [Buf() for _ in range(NST)]
    junk = [A.alloc([128, D], BF16) for _ in range(2)]
    b_junk = [Buf() for _ in range(2)]
    xn = [A.alloc([128, D], BF16) for _ in range(2)]
    b_xn = [Buf() for _ in range(2)]
    norm_i = [0]
    PT_BANKS = (4, 5)

    def rstd_of(xt_ap, b_xt):
        i = norm_i[0]
        norm_i[0] += 1
        k = i % NST
        j = i % 2
        act(junk[j], xt_ap, AF.Square, [b_xt], [b_junk[j], b_ss[k]], accum_out=st_ss[k])
        ts("dve", st_rs[k], st_ss[k], 1.0 / D, EPS, ALU.mult, ALU.add, [b_ss[k]], [b_rs[k]])
        act(st_rs[k], st_rs[k], AF.Sqrt, [b_rs[k]], [b_rs[k]])
        P.op("dve", lambda e: e.reciprocal(out=st_rs[k], in_=st_rs[k]), [b_rs[k]], [b_rs[k]])
        return st_rs[k], b_rs[k], j

    def norm_T(xt_ap, b_xt, gi, dst_ap, b_dst):
        rs, b_r, j = rstd_of(xt_ap, b_xt)
        act(xn[j], xt_ap, AF.Identity, [b_xt, b_r], [b_xn[j]], scale=rs[:, 0:1])
        bk = PT_BANKS[j]
        pv = bank_bf(bk).rearrange("p (a b) -> p a b", b=128)
        for kc in range(8):
            tr(pv[:, kc, :], xn[j][:, kc * 128:(kc + 1) * 128], ident, [b_xn[j], b_ident], [b_bank[bk]])
        gb = gcol[:, gi, :].unsqueeze(2).broadcast_to([128, 8, 128])
        tt("dve", dst_ap, pv, gb, ALU.mult, [b_bank[bk], b_gcol], [b_dst])

    def cast_w(dst, src, rows, b, key, nsplit=4):
        step = rows // nsplit
        for r in range(nsplit):
            P.dma("pool", dst[r * step:(r + 1) * step, :], src[r * step:(r + 1) * step, :], [], [b], key)

    base_mark = A.mark()

    def run_units(units, PT, b_PT, pS_banks, pO_banks):
        flat = []
        for ui, u in enumerate(units):
            for si, st in enumerate(u["steps"]):
                flat.append((ui, si, st, u))
        nPT = len(PT)
        called = set()

        def call_pre(ui):
            if ui < len(units) and ui not in called:
                called.add(ui)
                if units[ui].get("pre") is not None:
                    units[ui]["pre"]()

        call_pre(0)

        def qk(idx):
            ui, si, st, u = flat[idx]
            if si == 0:
                call_pre(ui + 1)
            bk = pS_banks[idx % len(pS_banks)]
            M = st["M"]
            n = len(st["mms"])
            for mi, (oc0, oc1, lhsT, rhs, reads) in enumerate(st["mms"]):
                mm(banks[bk][0:M, oc0:oc1], lhsT, rhs, mi == 0, mi == n - 1, reads, [b_bank[bk]], skip=True)
            sl = idx % nPT
            act(PT[sl][0:M, st["c0"]:st["c1"]], banks[bk][0:M, st["c0"]:st["c1"]], AF.Exp, [b_bank[bk]], [b_PT[sl]])

        def pv(idx):
            ui, si, st, u = flat[idx]
            ob = pO_banks[ui % len(pO_banks)]
            M = st["M"]
            ncol = st["ncol"]
            sl = idx % nPT
            vr, vreads = st["v"]
            po = banks[ob][:, 0:4 * ncol].rearrange("p (j c) -> p j c", c=ncol)
            first = (si == 0)
            for j in range(4):
                if st["c0"] <= 128 * j and 128 * (j + 1) <= st["c1"]:
                    mm(po[:, j, :], PT[sl][0:M, 128 * j:128 * (j + 1)], vr, first, True, [b_PT[sl]] + vreads, [b_bank[ob]], skip=True)
                    first = False

        LA = 2
        for idx in range(len(flat)):
            if idx == 0:
                for k in range(min(LA, len(flat))):
                    qk(k)
            if idx + LA < len(flat):
                qk(idx + LA)
            pv(idx)
            ui, si, st, u = flat[idx]
            if si == len(u["steps"]) - 1:
                u["epi"](pO_banks[ui % len(pO_banks)])

    def proj_phase(hT, b_hT, wsrc, b_wsrc, ncols_total, fm_list, tm_list, wring, b_wring, stg, b_stg, pbanks, gstg=None, b_gstg=None, w32=None, mode=None, order=None, after_blk=None):
        nblk = (ncols_total + 511) // 512
        cnt = [0, 0]
        if order is None:
            order = list(range(nblk))

        def load_blk(pos, part=None):
            blk = order[pos]
            bc0 = blk * 512
            bw = min(512, ncols_total - bc0)
            wi = pos % len(wring)
            if w32 is not None:
                w32src, w32stage, b_w32stage = w32
                wj = pos % len(w32stage)
                if part != "cast":
                    P.dma("sp", w32stage[wj][:, :, 0:bw], w32src.rearrange("(kc p) n -> p kc n", p=128)[:, :, bc0:bc0 + bw],
                          [], [b_w32stage[wj]], "w32s%d" % wj)
                if part == "dma":
                    return
                act(wring[wi][:, :, 0:bw], w32stage[wj][:, :, 0:bw], AF.Copy, [b_w32stage[wj]], [b_wring[wi]])
                P.dma("act", wsrc.rearrange("(kc p) n -> p kc n", p=128)[:, :, bc0:bc0 + bw], wring[wi][:, :, 0:bw],
                      [b_wring[wi]], [b_wsrc[blk]], "w32w%d" % wi)
                return
            P.dma("sp", wring[wi][:, :, 0:bw], wsrc.rearrange("(kc p) n -> p kc n", p=128)[:, :, bc0:bc0 + bw],
                  [b_wsrc[blk] if isinstance(b_wsrc, list) else b_wsrc], [b_wring[wi]], "wring%d" % wi)

        if mode in ("prefetch_dma", "prefetch_cast"):
            if w32 is not None:
                load_blk(0, mode[9:])
                load_blk(1, mode[9:])
            elif mode == "prefetch_dma":
                load_blk(0)
                load_blk(1)
            return
        if mode != "skip_prefetch":
            load_blk(0)
            if nblk > 1:
                load_blk(1)
        if mode == "prefetch_only":
            return
        for pos in range(nblk):
            blk = order[pos]
            bc0 = blk * 512
            bw = min(512, ncols_total - bc0)
            wi = pos % len(wring)
            if pos + 2 < nblk:
                load_blk(pos + 2)
            if after_blk is not None and pos > 0 and order[pos - 1] in after_blk:
                after_blk[order[pos - 1]]()
            for (c0, dst, b_dst, scale) in fm_list:
                if not (bc0 <= c0 < bc0 + bw):
                    continue
                lc = c0 - bc0
                for tc in range(4):
                    bk = pbanks[cnt[0] % 2]
                    si = cnt[0] % len(stg)
                    cnt[0] += 1
                    for kc in range(8):
                        mm(banks[bk][:, :], wring[wi][:, kc, lc:lc + 128], hT[:, kc, tc * 512:(tc + 1) * 512], kc == 0, kc == 7,
                           [b_wring[wi], b_hT[tc]], [b_bank[bk]])
                    evac(stg[si], banks[bk][:, :], [b_bank[bk]], [b_stg[si]], scale=scale)
                    P.dma("sp", dst[:, tc * 512:(tc + 1) * 512], stg[si], [b_stg[si]], [b_dst], "stg%d" % si)
            for (c0, ncols, dst, b_dst, kind) in tm_list:
                if not (bc0 <= c0 < bc0 + bw):
                    continue
                lc = c0 - bc0
                for t16 in range(16):
                    bk = pbanks[cnt[0] % 2]
                    cnt[0] += 1
                    for kc in range(8):
                        mm(banks[bk][:, 0:ncols], hT[:, kc, t16 * 128:(t16 + 1) * 128], wring[wi][:, kc, lc:lc + ncols], kc == 0, kc == 7,
                           [b_wring[wi], b_hT[t16 // 4]], [b_bank[bk]])
                    if kind == "sig":
                        gi = cnt[1] % len(gstg)
                        cnt[1] += 1
                        act(gstg[gi], banks[bk][:, 0:ncols], AF.Sigmoid, [b_bank[bk]], [b_gstg[gi]])
                        P.dma("sp", dst[t16 * 128:(t16 + 1) * 128, :], gstg[gi], [b_gstg[gi]], [b_dst], "gstg%d" % gi)
                    else:
                        si = cnt[0] % len(stg)
                        evac(stg[si][:, 0:ncols], banks[bk][:, 0:ncols], [b_bank[bk]], [b_stg[si]])
                        P.dma("sp", dst[t16 * 128:(t16 + 1) * 128, :], stg[si][:, 0:ncols], [b_stg[si]], [b_dst], "stg%d" % si)

    def load_hT(s, src_x, gi, after4=None):
        hT = A.alloc([128, 8, S], BF16)
        b_hT = [Buf() for _ in range(4)]
        xt = [A.alloc([128, D], F32) for _ in range(6)]
        b_xt = [Buf() for _ in range(6)]
        for t16 in range(16):
            xi = t16 % 6
            P.dma("sp", xt[xi], src_x[s, t16 * 128:(t16 + 1) * 128, :], [b_xs[s][t16]], [b_xt[xi]], "ldx%d" % xi)
            norm_T(xt[xi], b_xt[xi], gi, hT[:, :, t16 * 128:(t16 + 1) * 128], b_hT[t16 // 4])
            if t16 == 3 and after4 is not None:
                after4()
        return hT, b_hT

    def outproj_residual(s, qc, o_bf, b_obf, wout, b_wout, src_x, oT, b_oT, xres, b_xres, obanks):
        for j in range(4):
            bk = PT_BANKS[j % 2]
            pv = bank_bf(bk).rearrange("p (a b) -> p a b", b=128)
            for kc in range(8):
                tr(pv[:, kc, :], o_bf[:, j, kc * 128:(kc + 1) * 128], ident, [b_obf, b_ident], [b_bank[bk]])
            evac(oT[:, :, j * 128:(j + 1) * 128], pv, [b_bank[bk]], [b_oT])
        for j in range(4):
            t16 = qc * 4 + j
            xi = j % 2
            P.dma("sp", xres[xi], src_x[s, t16 * 128:(t16 + 1) * 128, :], [b_xs[s][t16]], [b_xres[xi]], "xres%d" % xi)
            for nh in range(2):
                bk = obanks[(j * 2 + nh) % 2]
                for kc in range(8):
                    mm(banks[bk][:, :], oT[:, kc, j * 128:(j + 1) * 128], wout[:, kc, nh * 512:(nh + 1) * 512], kc == 0, kc == 7,
                       [b_oT, b_wout], [b_bank[bk]])
                tt("dve", xres[xi][:, nh * 512:(nh + 1) * 512], banks[bk][:, :], xres[xi][:, nh * 512:(nh + 1) * 512], ALU.add,
                   [b_bank[bk], b_xres[xi]], [b_xres[xi]])
            P.dma("sp", xs[s, t16 * 128:(t16 + 1) * 128, :], xres[xi], [b_xres[xi]], [b_xs[s][t16]], "xst%d" % xi)

    def nsa_setup_tables():
        m = A.mark()
        rb = A.alloc([33, H], F32)
        rb31 = A.alloc([33, H], F32)
        tv = A.alloc([33, H], BF16)
        oh = A.alloc([33, T1W], BF16)
        t1s = A.alloc([H, T1W], BF16)
        b = Buf()
        P.op("dve", lambda e: e.memset(rb, NEG), [], [b])
        P.op("dve", lambda e: e.memset(rb31, 0.0), [b], [b])
        P.dma("sp", rb[0:32, :], rel_bias, [b], [b], "t1a")
        P.dma("sp", rb31[0:32, :], rel_bias[31, :].partition_broadcast(32), [b], [b], "t1a")
        P.dma("sp", oh, cst["c_oh1d"], [], [b], "t1a")
        tt("dve", tv, rb, rb31, ALU.subtract, [b], [b])
        for c4 in range(4):
            mm(banks[7][0:H, 0:400], tv, oh[:, c4 * 400:(c4 + 1) * 400], True, True, [b], [b_bank[7]])
            cp("dve", t1s[:, c4 * 400:(c4 + 1) * 400], banks[7][0:H, 0:400], [b_bank[7]], [b])
        P.dma("sp", t1_d, t1s, [b], [b_t1d], "t1a")
        P.barrier()
        A.reset(m)

    def nsa_layer(s, src_x):
        m00 = A.mark()
        kcaug = A.alloc([128, G, 4, 128], BF16)
        vcx = A.alloc([128, G, 97], BF16)
        b_kcmp, b_vcx = Buf(), Buf()
        P.op("dve", lambda e: e.memset(kcaug, 0.0), [], [b_kcmp])
        P.op("dve", lambda e: e.memset(vcx, 1.0), [], [b_vcx])
        m0 = A.mark()
        wring = [A.alloc([128, 8, 512], BF16) for _ in range(3)]
        b_wring = [Buf() for _ in range(3)]
        stg = [A.alloc([128, 512], BF16) for _ in range(4)]
        b_stg = [Buf() for _ in range(4)]
        gstg = [A.alloc([128, 48], F32) for _ in range(2)]
        b_gstg = [Buf() for _ in range(2)]
        w32 = None
        if s == 0:
            w32stage = [A.alloc([128, 8, 512], F32) for _ in range(2)]
            w32 = (nsa_w_in[0], w32stage, [Buf() for _ in range(2)])
        proj_phase(None, None, nwinb, b_nwinb_blk, NSA_IN, [], [], wring, b_wring, stg, b_stg, (6, 7), gstg, b_gstg, w32=w32, mode="prefetch_dma", order=[2, 0, 1, 3, 4, 5])

        def after4():
            proj_phase(None, None, nwinb, b_nwinb_blk, NSA_IN, [], [], wring, b_wring, stg, b_stg, (6, 7), gstg, b_gstg, w32=w32, mode="prefetch_cast",
                       order=[2, 0, 1, 3, 4, 5])
            for g in range(G):
                P.dma("sp", vcx[0:127, g, 65:97], cst["c_ovl"], [b_vcx], [b_vcx], "nc4")
                P.dma("sp", kcaug[64:128, g, :, 0:127], cst["c_selp"], [b_kcmp], [b_kcmp], "nc4")
        hT, b_hT = load_hT(s, src_x, 0, after4=after4)
        fm = []
        for i in range(8):
            fm.append((i * 128, qT_d[i * 128:(i + 1) * 128, :], b_qTd, 0.125))
        for i, c0 in enumerate([1024, 1152, 1280, 1408, 1536, 1664, 2048, 2176]):
            fm.append((c0, kT_d[i * 128:(i + 1) * 128, :], b_kTd, None))
        tm = [(1792, 256, v_d[:, 0:256], b_vd, "copy"), (2304, 256, v_d[:, 256:512], b_vd, "copy"),
              (2560, 48, gates_d, b_gd, "sig")]

        def compress():
            kcT = A.alloc([64, G, S], BF16)
            vcT = A.alloc([64, G, S], BF16)
            b_kc = Buf()
            P.dma("sp", kcT, kT_d[0:256, :].rearrange("(g p) t -> p g t", p=64), [b_kTd], [b_kc], "nb0")
            P.dma("sp", vcT, kT_d[256:512, :].rearrange("(g p) t -> p g t", p=64), [b_kTd], [b_kc], "nb0")
            w1s, w2s, peT, b_cw = CW["w1s"], CW["w2s"], CW["peT"], CW["b"]
            cstv = A.alloc([64, 2], F32)
            b_cst = Buf()
            gu = [A.alloc([64, 512], F32) for _ in range(4)]
            gbf = A.alloc([64, 512], BF16)
            b_g = Buf()
            for ni, nm in enumerate(("k", "v")):
                for l in range(32):
                    mm(banks[0][0:64, 0:2], w1s[nm][:, l, :], peT[nm][:, l:l + 2], l == 0, l == 31, [b_cw], [b_bank[0]])
                cp("dve", cstv[:, ni:ni + 1], banks[0][0:64, 0:1], [b_bank[0]], [b_cst])
            for ni, (nm, srcT) in enumerate((("k", kcT), ("v", vcT))):
                bk = 1 + ni
                v4 = srcT.rearrange("p g (c r) -> p g c r", r=16)
                for g in range(G):
                    for l in range(32):
                        mm(banks[bk][0:64, g * 128:g * 128 + 127], w1s[nm][:, l, :], v4[:, g, (l // 16):(l // 16) + 127, l % 16], l == 0, l == 31,
                           [b_cw, b_kc], [b_bank[bk]])
                u, u2, t3, th = gu
                act(u, banks[bk][0:64, :], AF.Identity, [b_bank[bk], b_cst], [b_g], bias=cstv[:, ni:ni + 1])
                tt("dve", u2, u, u, ALU.mult, [b_g], [b_g])
                ts("dve", u2, u2, 0.044715, 1.0, ALU.mult, ALU.add, [b_g], [b_g])
                tt("dve", t3, u2, u, ALU.mult, [b_g], [b_g])
                act(th, t3, AF.Tanh, [b_g], [b_g], scale=0.7978845608028654)
                ts("dve", th, th, 1.0, 0.5, ALU.add, ALU.mult, [b_g], [b_g])
                tt("dve", gbf, th, u, ALU.mult, [b_g], [b_g])
                if nm == "k":
                    for g in range(G):
                        mm(banks[3][0:64, g * 128:g * 128 + 127], w2s["k"], gbf[:, g * 128:g * 128 + 127], True, True, [b_g, b_cw], [b_bank[3]], skip=True)
                    for g in range(G):
                        cp("dve", kcaug[0:64, g, :, 0:127], banks[3][0:64, g * 128:g * 128 + 127].unsqueeze(1).broadcast_to([64, 4, 127]),
                           [b_bank[3]], [b_kcmp])
                else:
                    for g in range(G):
                        mm(banks[0][0:127, g * 64:(g + 1) * 64], gbf[:, g * 128:g * 128 + 127], w2s["v"], True, True, [b_g, b_cw], [b_bank[0]], skip=True)
                    cp("dve", vcx[0:127, :, 0:64], banks[0][0:127, 0:256].rearrange("p (g d) -> p g d", d=64), [b_bank[0]], [b_vcx])

        proj_phase(hT, b_hT, nwinb, b_nwinb_blk, NSA_IN, fm, tm, wring, b_wring, stg, b_stg, (6, 7), gstg, b_gstg, w32=w32, mode="skip_prefetch",
                   order=[2, 0, 1, 3, 4, 5], after_blk={2: compress})
        P.barrier()
        A.reset(m0)

        deferred_casts(s)
        cd = A.alloc([128, H, 256], BF16)
        jmat = A.alloc([128, 128], BF16)
        addm = A.alloc([128, 16, 32], F32)
        mulm = A.alloc([128, 16, 32], F32)
        we = A.alloc([128, 128], BF16)
        wout = A.alloc([128, 8, D], BF16)
        ksaug = A.alloc([128, G, S], BF16)
        kwaug = A.alloc([128, G, S], BF16)
        vsx = A.alloc([128, 16, G, 65], BF16)
        vwx = A.alloc([128, 16, G, 65], BF16)
        b_c, b_c2, b_wout = Buf(), Buf(), Buf()
        b_kw, b_ks, b_kpad = Buf(), Buf(), Buf()
        b_vw, b_vs, b_vpad = Buf(), Buf(), Buf()
        P.op("dve", lambda e: e.memset(kwaug[64:128, :, :], 0.0), [], [b_kpad])
        P.op("dve", lambda e: e.memset(vwx[:, :, :, 64:65], 1.0), [], [b_vpad])
        P.op("dve", lambda e: e.memset(ksaug[96:128, :, :], 0.0), [], [b_kpad])
        P.op("dve", lambda e: e.memset(vsx[:, :, :, 64:65], 1.0), [], [b_vpad])
        P.dma("sp", cd, bass.AP(t1_d.tensor, T1Z - 127, [[1, 128], [T1W, H], [1, 256]]), [b_t1d], [b_c], "nc0")
        P.dma("sp", jmat, cst["c_jmat"], [], [b_c], "nc0")
        P.dma("sp", we, cst["c_we"], [], [b_c], "nc0")
        for g in range(G):
            P.dma("sp", kwaug[0:64, g, :], kT_d[768 + g * 64:768 + (g + 1) * 64, :], [b_kTd], [b_kw], "nc2")
        for kt in range(16):
            P.dma("sp", vwx[:, kt, :, 0:64], v_d[kt * 128:(kt + 1) * 128, 256:512].rearrange("p (g d) -> p g d", d=64), [b_vd], [b_vw], "nc3")
        P.dma("sp", addm, cst["c_addm"], [], [b_c2], "nc0")
        P.dma("sp", mulm, cst["c_mulm"], [], [b_c2], "nc0")
        for g in range(G):
            P.dma("sp", ksaug[0:64, g, :], kT_d[512 + g * 64:512 + (g + 1) * 64, :], [b_kTd], [b_ks], "nc2")
            P.dma("sp", ksaug[64:96, g, :], cst["c_ek"], [], [b_ks], "nc2")
        for kt in range(16):
            P.dma("sp", vsx[:, kt, :, 0:64], v_d[kt * 128:(kt + 1) * 128, 0:256].rearrange("p (g d) -> p g d", d=64), [b_vd], [b_vs], "nc3")
        P.dma("sp", wout, nwoutb.rearrange("(kc p) n -> p kc n", p=128), [b_nwoutb], [b_wout], "nc1")
        Qc = [A.alloc([128, 512], BF16) for _ in range(3)]
        Qsw = [A.alloc([128, 512], BF16) for _ in range(3)]
        b_Qc = [Buf() for _ in range(3)]
        b_Qsw = [Buf() for _ in range(3)]
        for i in range(3):
            P.op("dve", lambda e, i=i: e.memset(Qsw[i][64:128, :], 0.0), [], [b_Qsw[i]])
        PT = [A.alloc([128, 512], BF16) for _ in range(5)]
        b_PT = [Buf() for _ in range(5)]
        o_acc = A.alloc([128, 4, D], F32)
        o_bf = A.alloc([128, 4, D], BF16)
        b_oacc, b_obf = Buf(), Buf()
        oT = A.alloc([128, 8, 512], BF16)
        b_oT = Buf()
        xres = [A.alloc([128, D], F32) for _ in range(2)]
        b_xres = [Buf() for _ in range(2)]
        gts = A.alloc([128, 4, 48], F32)
        b_gts = Buf()
        imp = A.alloc([128, 4, 32], F32)
        sc = A.alloc([128, 4, 32], F32)
        sc2 = A.alloc([128, 4, 32], F32)
        mx8 = A.alloc([128, 8], F32)
        mkb = A.alloc([128, 4, 96], BF16)
        b_imp, b_sc = Buf(), Buf()
        P.op("dve", lambda e: e.memset(mkb, 0.0), [], [b_sc])
        sm = [A.alloc([128, 4], F32) for _ in range(6)]
        b_sm = [Buf() for _ in range(6)]
        tmpo = [A.alloc([128, 4, 64], F32) for _ in range(2)]
        b_tmpo = [Buf() for _ in range(2)]
        tmpi = A.alloc([128, 4, 32], F32)
        b_tmpi = Buf()
        ucnt = [0]
        qcc = [0]
        qsc = [0]

        selTs = A.alloc([128, G, 512], BF16)
        b_selTs = [Buf() for _ in range(G)]
        stgo = [A.alloc([128, 4 * 97], F32) for _ in range(3)]
        b_stgo = [Buf() for _ in range(3)]
        for qc in range(4):
            q0 = qc * 512
            P.dma("sp", gts, gates_d[q0:q0 + 512, :].rearrange("(j p) c -> p j c", p=128), [b_gd, b_gts], [b_gts], "gts")
            use_sel = qc >= 2

            def epilogue(ob, h, br, ncol, first_branch, with_imp, hg):
                k = ucnt[0] % 6
                k2 = (ucnt[0] + 3) % 6
                ucnt[0] += 1
                if ucnt[0] % 12 == 0:
                    pop_cast()
                po = banks[ob][:, 0:4 * ncol].rearrange("p (j c) -> p j c", c=ncol)
                b_po = b_bank[ob]
                if br == 0 and qc == 0:
                    ts("dve", sm[k], po[:, :, 64], 1e-30, None, ALU.max, None, [b_po], [b_sm[k]])
                    P.op("dve", lambda e: e.reciprocal(out=sm[k], in_=sm[k]), [b_sm[k]], [b_sm[k]])
                else:
                    P.op("dve", lambda e: e.reciprocal(out=sm[k], in_=po[:, :, 64]), [b_po], [b_sm[k]])
                if with_imp and use_sel:
                    rb_ = sm[k][:, :].unsqueeze(2).broadcast_to([128, 4, 32])
                    if hg == 0:
                        tt("dve", imp, po[:, :, 65:97], rb_, ALU.mult, [b_po, b_sm[k]], [b_imp])
                    else:
                        tt("dve", tmpi, po[:, :, 65:97], rb_, ALU.mult, [b_po, b_sm[k]], [b_tmpi])
                        tt("dve", imp, imp, tmpi, ALU.add, [b_tmpi, b_imp], [b_imp])
                tt("dve", sm[k2], sm[k], gts[:, :, br * 16 + h], ALU.mult, [b_sm[k], b_gts], [b_sm[k2]])
                rg = sm[k2][:, :].unsqueeze(2).broadcast_to([128, 4, 64])
                osl = o_acc[:, :, h * 64:(h + 1) * 64]
                if first_branch:
                    tt("dve", osl, po[:, :, 0:64], rg, ALU.mult, [b_po, b_sm[k2]], [b_oacc])
                else:
                    ti = ucnt[0] % 2
                    tt("dve", tmpo[ti], po[:, :, 0:64], rg, ALU.mult, [b_po, b_sm[k2]], [b_tmpo[ti]])
                    tt("dve", osl, osl, tmpo[ti], ALU.add, [b_tmpo[ti], b_oacc], [b_oacc])

            def selection(g):
                bk_sel = PT_BANKS[g % 2]
                pvw = bank_bf(bk_sel)
                tt("dve", sc, imp, mulm[:, qc * 4:(qc + 1) * 4, :], ALU.mult, [b_imp, b_c2], [b_sc])
                tt("dve", sc, sc, addm[:, qc * 4:(qc + 1) * 4, :], ALU.add, [b_sc, b_c2], [b_sc])
                for j in range(4):
                    P.op("dve", lambda e, j=j: e.max(out=mx8, in_=sc[:, j, :]), [b_sc], [b_sc])
                    P.op("dve", lambda e, j=j: e.match_replace(out=sc2[:, j, :], in_to_replace=mx8, in_values=sc[:, j, :], imm_value=-3e38),
                         [b_sc], [b_sc])
                    P.op("dve", lambda e, j=j: e.max(out=mx8, in_=sc2[:, j, :]), [b_sc], [b_sc])
                    ts("dve", sc2[:, j, :], sc[:, j, :], mx8[:, 7:8], None, ALU.is_ge, None, [b_sc], [b_sc])
                ts("dve", mkb[:, :, 64:96], sc2, -NEG, NEG, ALU.mult, ALU.add, [b_sc], [b_sc])
                for j in range(4):
                    tr(pvw[0:96, j * 128:(j + 1) * 128], mkb[:, j, :], ident, [b_sc, b_ident], [b_bank[bk_sel]])
                cp("dve", selTs[64:96, g, :], pvw[64:96, 0:512], [b_bank[bk_sel]], [b_selTs[g]])

            def near_corr(mms, h, delta):
                if delta > 128:
                    return
                a0 = max(0, -delta)
                a1 = min(512, 256 - delta)
                mms.append((a0, a1, jmat, cd[:, h, delta + a0:delta + a1], [b_c]))

            units = []
            Mc = min(32 * (qc + 1), 127)
            for g in range(G):
                for hg in range(4):
                    h = g * 4 + hg
                    qi = qcc[0] % 3
                    qcc[0] += 1

                    def pre(h=h, qi=qi):
                        P.dma("sp", Qc[qi][0:64, :], qT_d[h * 64:(h + 1) * 64, q0:q0 + 512], [b_qTd, b_Qc[qi]], [b_Qc[qi]], "nQc%d" % qi)
                        P.dma("sp", Qc[qi][64:128, :], bass.AP(t1_d.tensor, h * T1W + T1Z + 481 - 1008, [[16, 64], [1, 512]]),
                              [b_t1d, b_Qc[qi]], [b_Qc[qi]], "nQcb%d" % qi)
                    mms = [(0, 512, kcaug[:, g, qc, 0:Mc], Qc[qi], [b_kcmp, b_Qc[qi]])]
                    st = dict(M=Mc, c0=0, c1=512, mms=mms, v=(vcx[0:Mc, g, :], [b_vcx]), ncol=97)

                    def epi_c(ob, h=h, hg=hg, g=g):
                        epilogue(ob, h, 0, 97, True, True, hg)
                        if hg == 3 and use_sel:
                            selection(g)
                    units.append(dict(steps=[st], pre=pre, epi=epi_c))
                    steps = []
                    for kt in range(max(0, 4 * qc - 4), 4 * qc + 4):
                        k0 = kt * 128
                        delta = q0 - k0
                        c0 = max(0, -delta)
                        c1 = min(512, 640 - delta)
                        mms = [(c0, c1, kwaug[:, g, k0:k0 + 128], Qc[qi][:, c0:c1], [b_kw, b_kpad, b_Qc[qi]])]
                        near_corr(mms, h, delta)
                        if delta >= 128:
                            f0 = 512 - delta
                            mms.append((f0, f0 + 128, ident, we, [b_c, b_ident]))
                        steps.append(dict(M=128, c0=c0, c1=c1, mms=mms, v=(vwx[:, kt, g, :], [b_vw, b_vpad]), ncol=65))
                    units.append(dict(steps=steps, pre=None, epi=(lambda ob, h=h, hg=hg: epilogue(ob, h, 2, 65, False, False, hg))))
            for g in range(G):
                for hg in range(4):
                    h = g * 4 + hg
                    qi = qsc[0] % 3
                    qsc[0] += 1

                    def pre(h=h, qi=qi, g=g):
                        P.dma("sp", Qsw[qi][0:64, :], qT_d[h * 64:(h + 1) * 64, q0:q0 + 512], [b_qTd, b_Qsw[qi]], [b_Qsw[qi]], "nQs%d" % qi)
                        if use_sel:
                            cp("dve", Qsw[qi][64:96, :], selTs[64:96, g, :], [b_selTs[g], b_Qsw[qi]], [b_Qsw[qi]])
                    steps = []
                    for kt in range(4 * (qc + 1)):
                        k0 = kt * 128
                        delta = q0 - k0
                        c0 = max(0, -delta)
                        mms = [(c0, 512, ksaug[:, g, k0:k0 + 128], Qsw[qi][:, c0:512], [b_ks, b_kpad, b_Qsw[qi]])]
                        near_corr(mms, h, delta)
                        steps.append(dict(M=128, c0=c0, c1=512, mms=mms, v=(vsx[:, kt, g, :], [b_vs, b_vpad]), ncol=65))
                    units.append(dict(steps=steps, pre=pre, epi=(lambda ob, h=h, hg=hg: epilogue(ob, h, 1, 65, False, False, hg))))
            run_units(units, PT, b_PT, (0, 1, 6), (2, 3, 7))
            cp("dve", o_bf, o_acc, [b_oacc], [b_obf])
            outproj_residual(s, qc, o_bf, b_obf, wout, b_wout, src_x, oT, b_oT, xres, b_xres, (6, 7))
        flush_casts()
        P.barrier()
        A.reset(m00)

    def fox_layer(s, src_x):
        m0 = A.mark()
        wring = [A.alloc([128, 8, 512], BF16) for _ in range(3)]
        b_wring = [Buf() for _ in range(3)]
        stg = [A.alloc([128, 512], BF16) for _ in range(4)]
        b_stg = [Buf() for _ in range(4)]
        proj_phase(None, None, fwinb, b_fwinb, 3072, [], [], wring, b_wring, stg, b_stg, (6, 7), mode="prefetch_only")
        hT, b_hT = load_hT(s, src_x, 1)
        fm = []
        for i in range(8):
            fm.append((i * 128, qT_d[i * 128:(i + 1) * 128, :], b_qTd, 0.125))
        for i in range(8):
            fm.append((1024 + i * 128, kT_d[i * 128:(i + 1) * 128, :], b_kTd, None))
        tm = [(2048, 512, v_d[:, 0:512], b_vd, "copy"), (2560, 512, v_d[:, 512:1024], b_vd, "copy")]
        wf = A.alloc([128, 8, H], BF16)
        b_wf = Buf()
        P.dma("sp", wf, fwinb.rearrange("(kc p) n -> p kc n", p=128)[:, :, 3072:3088], [b_fwinb], [b_wf], "wf")
        fl = A.alloc([H, S], F32)
        b_fl = Buf()
        bfv = A.alloc([H, 1], F32)
        P.dma("sp", bfv, fox_b_f.rearrange("o h -> h o"), [], [b_fl], "bfv", allow_slow_non_contiguous=True)
        for tc in range(4):
            bk = 6 + tc % 2
            for kc in range(8):
                mm(banks[bk][0:H, :], wf[:, kc, :], hT[:, kc, tc * 512:(tc + 1) * 512], kc == 0, kc == 7, [b_wf, b_hT[tc]], [b_bank[bk]])
            act(fl[:, tc * 512:(tc + 1) * 512], banks[bk][0:H, :], AF.Identity, [b_bank[bk], b_fl], [b_fl], bias=bfv[:, 0:1])
        az = A.alloc([H, S], F32)
        mz = A.alloc([H, S], F32)
        onesr = A.alloc([H, S], F32)
        cpp = A.alloc([H, 6, S], BF16)
        P.op("pool", lambda e: e.memset(onesr, 1.0), [], [b_fl])
        act(az, fl, AF.Abs, [b_fl], [b_fl])
        act(az, az, AF.Exp, [b_fl], [b_fl], scale=-1.0)
        act(az, az, AF.Ln, [b_fl], [b_fl], bias=1.0)
        ts("dve", mz, fl, 0.0, None, ALU.min, None, [b_fl], [b_fl])
        tt("dve", mz, mz, az, ALU.subtract, [b_fl], [b_fl])
        P.op("dve", lambda e: e.tensor_tensor_scan(out=az, data0=onesr, data1=mz, initial=0.0, op0=ALU.mult, op1=ALU.add), [b_fl], [b_fl])
        cp("dve", cpp[:, 0, :], az, [b_fl], [b_fl])
        tt("dve", mz, az, cpp[:, 0, :], ALU.subtract, [b_fl], [b_fl])
        cp("dve", cpp[:, 1, :], mz, [b_fl], [b_fl])
        tt("dve", mz, mz, cpp[:, 1, :], ALU.subtract, [b_fl], [b_fl])
        cp("dve", cpp[:, 2, :], mz, [b_fl], [b_fl])
        ts("dve", cpp[:, 3:6, :], cpp[:, 0:3, :], -1.0, None, ALU.mult, None, [b_fl], [b_fl])
        P.dma("pool", cpart_d.rearrange("i h t -> h i t"), cpp, [b_fl], [b_cpd], "cpd")
        proj_phase(hT, b_hT, fwinb, b_fwinb, 3072, fm, tm, wring, b_wring, stg, b_stg, (6, 7), mode="skip_prefetch")
        P.barrier()
        A.reset(m0)

        cm = A.alloc([128, 128], BF16)
        b_c = Buf()
        P.dma("sp", cm, cst["c_cm"], [], [b_c], "fc0")
        wout = A.alloc([128, 8, D], BF16)
        b_wout = Buf()
        vfx = A.alloc([128, 16, H, 65], BF16)
        b_vk = [Buf() for _ in range(16)]
        b_vpad = Buf()
        P.op("dve", lambda e: e.memset(vfx[:, :, :, 64:65], 1.0), [], [b_vpad])
        for kt in range(16):
            P.dma("sp", vfx[:, kt, :, 0:64], v_d[kt * 128:(kt + 1) * 128, :].rearrange("p (h d) -> p h d", d=64), [b_vd], [b_vk[kt]], "fc3")
        P.dma("sp", wout, fwoutb.rearrange("(kc p) n -> p kc n", p=128), [b_fwoutb], [b_wout], "fc1")
        Kh = [A.alloc([128, S], BF16) for _ in range(2)]
        Qh = [A.alloc([128, 512], BF16) for _ in range(3)]
        b_Kh = [Buf() for _ in range(2)]
        b_Qh = [Buf() for _ in range(3)]
        for i in range(2):
            P.op("dve", lambda e, i=i: e.memset(Kh[i][64:128, :], 0.0), [], [b_Kh[i]])
            P.op("dve", lambda e, i=i: e.memset(Kh[i][64:67, :], 1.0), [b_Kh[i]], [b_Kh[i]])
        for i in range(3):
            P.op("dve", lambda e, i=i: e.memset(Qh[i][64:128, :], 0.0), [], [b_Qh[i]])
            P.op("dve", lambda e, i=i: e.memset(Qh[i][96:99, :], 1.0), [b_Qh[i]], [b_Qh[i]])
        PT = [A.alloc([128, 512], BF16) for _ in range(5)]
        b_PT = [Buf() for _ in range(5)]
        o_bf = A.alloc([128, 16, D], BF16)
        b_obf = Buf()
        oT = A.alloc([128, 8, 512], BF16)
        b_oT = Buf()
        xres = [A.alloc([128, D], F32) for _ in range(2)]
        b_xres = [Buf() for _ in range(2)]
        sm = [A.alloc([128, 4], F32) for _ in range(4)]
        b_sm = [Buf() for _ in range(4)]
        ucnt = [0]
        qcnt = [0]
        units = []
        for h in range(H):
            ki = h % 2
            for qc in range(4):
                q0 = qc * 512
                qi = qcnt[0] % 3
                qcnt[0] += 1

                def pre(h=h, ki=ki, qc=qc, q0=q0, qi=qi):
                    if qc == 0:
                        P.dma("sp", Kh[ki][0:64, :], kT_d[h * 64:(h + 1) * 64, :], [b_kTd, b_Kh[ki]], [b_Kh[ki]], "fK%d" % ki)
                        P.dma("sp", Kh[ki][96:99, :], cpart_d[3:6, h, :], [b_cpd, b_Kh[ki]], [b_Kh[ki]], "fKb%d" % ki)
                    P.dma("sp", Qh[qi][0:64, :], qT_d[h * 64:(h + 1) * 64, q0:q0 + 512], [b_qTd, b_Qh[qi]], [b_Qh[qi]], "fQ%d" % qi)
                    P.dma("sp", Qh[qi][64:67, :], cpart_d[0:3, h, q0:q0 + 512], [b_cpd, b_Qh[qi]], [b_Qh[qi]], "fQb%d" % qi)
                steps = []
                for kt in range(4 * (qc + 1)):
                    k0 = kt * 128
                    delta = q0 - k0
                    c0 = max(0, -delta)
                    mms = [(c0, 512, Kh[ki][:, k0:k0 + 128], Qh[qi][:, c0:512], [b_Kh[ki], b_Qh[qi]])]
                    if delta <= 0:
                        mms.append((c0, c0 + 128, ident, cm, [b_c, b_ident]))
                    steps.append(dict(M=128, c0=c0, c1=512, mms=mms, v=(vfx[:, kt, h, :], [b_vk[kt], b_vpad]), ncol=65))

                def epi(ob, h=h, qc=qc):
                    k = ucnt[0] % 4
                    ucnt[0] += 1
                    po = banks[ob][:, 0:260].rearrange("p (j c) -> p j c", c=65)
                    P.op("dve", lambda e: e.reciprocal(out=sm[k], in_=po[:, :, 64]), [b_bank[ob]], [b_sm[k]])
                    rg = sm[k][:, :].unsqueeze(2).broadcast_to([128, 4, 64])
                    tt("dve", o_bf[:, qc * 4:(qc + 1) * 4, h * 64:(h + 1) * 64], po[:, :, 0:64], rg, ALU.mult, [b_bank[ob], b_sm[k]], [b_obf])
                units.append(dict(steps=steps, epi=epi, pre=pre))
        run_units(units, PT, b_PT, (0, 1, 6), (2, 3, 7))
        for qc in range(4):
            outproj_residual(s, qc, o_bf[:, qc * 4:(qc + 1) * 4, :], b_obf, wout, b_wout, src_x, oT, b_oT, xres, b_xres, (6, 7))
        P.barrier()
        A.reset(m0)

    def mlp_layer(l, src_x, last):
        m0 = A.mark()
        w2s = A.alloc([128, 32, D], BF16)
        b_w2s = Buf()
        for q4 in range(4):
            P.dma("sp", w2s[:, q4 * 8:(q4 + 1) * 8, :], w2b[l].rearrange("(fc p) n -> p fc n", p=128)[:, q4 * 8:(q4 + 1) * 8, :],
                  [b_w2b[l]], [b_w2s], "w2s")
        NW1 = 2
        w1s = [A.alloc([128, 8, 512], BF16) for _ in range(NW1)]
        b_w1s = [Buf() for _ in range(NW1)]
        aT = A.alloc([128, 32, 512], BF16)
        b_aT = [Buf() for _ in range(32)]
        hTc = [A.alloc([128, 8, 512], BF16) for _ in range(2)]
        b_hTc = [[Buf() for _ in range(4)] for _ in range(2)]
        xt = [A.alloc([128, D], F32) for _ in range(8)]
        b_xt = [Buf() for _ in range(8)]
        rtmp = [A.alloc([128, 512], F32) for _ in range(2)]
        b_rtmp = [Buf() for _ in range(2)]
        yo = [A.alloc([128, D], F32) for _ in range(2)]
        b_yo = [Buf() for _ in range(2)]
        w1cnt = p1cnt = p2cnt = 0
        chunks = [(s_, c_) for s_ in range(nseq) for c_ in range(4)]
        xi_of = {}

        def prep_tile(k, j):
            s_, c_ = chunks[k]
            cb_ = k % 2
            t16 = c_ * 4 + j
            xi = (k * 4 + j) % 8
            xi_of[(k, j)] = xi
            P.dma("sp", xt[xi], src_x[s_, t16 * 128:(t16 + 1) * 128, :], [b_xs[s_][t16]], [b_xt[xi]], "mxt%d" % xi)
            norm_T(xt[xi], b_xt[xi], 2 + l, hTc[cb_][:, :, j * 128:(j + 1) * 128], b_hTc[cb_][j])

        for j in range(4):
            prep_tile(0, j)
        for k, (s, c) in enumerate(chunks):
            cb = k % 2
            for blk in range(8):
                wi = w1cnt % NW1
                w1cnt += 1
                P.dma("sp", w1s[wi], w1b[l].rearrange("(kc p) n -> p kc n", p=128)[:, :, blk * 512:(blk + 1) * 512],
                      [b_w1b[l]], [b_w1s[wi]], "w1s%d" % wi)
                for f4 in range(4):
                    fc = blk * 4 + f4
                    pi = p1cnt % 2
                    p1cnt += 1
                    for kc in range(8):
                        mm(banks[pi][:, :], w1s[wi][:, kc, f4 * 128:(f4 + 1) * 128], hTc[cb][:, kc, :], kc == 0, kc == 7,
                           [b_w1s[wi]] + b_hTc[cb], [b_bank[pi]])
                    act(rtmp[pi], banks[pi][:, :], AF.Relu, [b_bank[pi]], [b_rtmp[pi]])
                    tt("dve", aT[:, fc, :], rtmp[pi], rtmp[pi], ALU.mult, [b_rtmp[pi]], [b_aT[fc]])
                if 1 <= blk <= 4 and k + 1 < len(chunks):
                    prep_tile(k + 1, blk - 1)
            for j in range(4):
                t16 = c * 4 + j
                xi = xi_of[(k, j)]
                for nh in range(2):
                    pi = 2 + p2cnt % 2
                    p2cnt += 1
                    for fc in range(32):
                        mm(banks[pi][:, :], aT[:, fc, j * 128:(j + 1) * 128], w2s[:, fc, nh * 512:(nh + 1) * 512], fc == 0, fc == 31,
                           [b_aT[fc], b_w2s], [b_bank[pi]])
                    tt("dve", xt[xi][:, nh * 512:(nh + 1) * 512], banks[pi][:, :], xt[xi][:, nh * 512:(nh + 1) * 512], ALU.add,
                       [b_bank[pi], b_xt[xi]], [b_xt[xi]])
                if not last:
                    P.dma("pool", xs[s, t16 * 128:(t16 + 1) * 128, :], xt[xi], [b_xt[xi]], [b_xs[s][t16]], "mxst%d" % xi)
                else:
                    rs, b_r, jj = rstd_of(xt[xi], b_xt[xi])
                    yi = t16 % 2
                    P.op("dve", lambda e, xi=xi, yi=yi, rs=rs: e.scalar_tensor_tensor(
                        out=yo[yi], in0=xt[xi], scalar=rs[:, 0:1], in1=gfin, op0=ALU.mult, op1=ALU.mult),
                        [b_xt[xi], b_r, b_gfin], [b_yo[yi]])
                    P.dma("pool", out[s, t16 * 128:(t16 + 1) * 128, :], yo[yi], [b_yo[yi]], [], "yo%d" % yi, is_output=True)
        P.barrier()
        A.reset(m0)

    CW = {"w1s": {}, "w2s": {}, "peT": {}, "b": Buf()}
    if "nsa" in parts:
        for nm, w1, w2, pe in (("k", nsa_wk1, nsa_wk2, nsa_pe_k), ("v", nsa_wv1, nsa_wv2, nsa_pe_v)):
            CW["w1s"][nm] = A.alloc([64, 32, DH], BF16)
            CW["w2s"][nm] = A.alloc([64, 64], BF16)
            CW["peT"][nm] = A.alloc([64, 34], BF16)
            P.op("dve", lambda e, nm=nm: e.memset(CW["peT"][nm], 0.0), [], [CW["b"]])
            P.dma("pool", CW["w1s"][nm], w1[0].rearrange("(l d) o -> d l o", d=DH), [], [CW["b"]], "nb1")
            P.dma("pool", CW["w2s"][nm], w2[0], [], [CW["b"]], "nb1")
            P.dma("pool", CW["peT"][nm][:, 0:32], pe[0].rearrange("l d -> d l"), [CW["b"]], [CW["b"]], "nb1", allow_slow_non_contiguous=True)
        nsa_setup_tables()
        cast_w(nwoutb, nsa_w_out[0], D, b_nwoutb, "cw_b", 2)
    pending_casts = []

    def cast_pieces(dst, src, rows, b, key, nsplit=4):
        step = rows // nsplit
        for r in range(nsplit):
            pending_casts.append(lambda r=r: P.dma("pool", dst[r * step:(r + 1) * step, :], src[r * step:(r + 1) * step, :], [], [b], key))

    def pop_cast():
        if pending_casts:
            pending_casts.pop(0)()

    def flush_casts():
        while pending_casts:
            pending_casts.pop(0)()

    def deferred_casts(s_):
        if s_ == 0:
            if "mlp" in parts:
                cast_pieces(w1b[0], mlp_w1[0], D, b_w1b[0], "cw_c")
                cast_pieces(w2b[0], mlp_w2[0], DFF, b_w2b[0], "cw_d")
        if s_ == min(1, nseq - 1):
            if "fox" in parts:
                cast_pieces(fwinb, fox_w_in[0], D, b_fwinb, "cw_e")
                cast_pieces(fwoutb, fox_w_out[0], D, b_fwoutb, "cw_f", 2)
            if "mlp" in parts and depth > 1:
                cast_pieces(w1b[1], mlp_w1[1], D, b_w1b[1], "cw_g")
                cast_pieces(w2b[1], mlp_w2[1], DFF, b_w2b[1], "cw_h")

    if "nsa" not in parts:
        for s_ in range(nseq):
            deferred_casts(s_)
        flush_casts()
    cur = x_in
    for l in range(depth):
        if l == 0 and "nsa" in parts:
            for s in range(nseq):
                nsa_layer(s, cur)
            cur = xs
        if l == 1 and "fox" in parts:
            for s in range(nseq):
                fox_layer(s, cur)
            cur = xs
        if "mlp" in parts:
            mlp_layer(l, cur, last=(l == depth - 1))
            cur = xs
    P.emit()
    return nc, es


_CACHE = {}
IN_NAMES = ["rel_bias", "norm_mix", "norm_mlp", "nsa_w_in", "nsa_pe_k", "nsa_wk1", "nsa_wk2", "nsa_pe_v", "nsa_wv1", "nsa_wv2",
            "nsa_w_out", "fox_w_in", "fox_b_f", "fox_w_out", "mlp_w1", "mlp_w2", "final_norm"]


def kernel(**inputs):
    n = 8
    x = np.ascontiguousarray(np.asarray(inputs["x"], dtype=np.float32))
    nseq = x.shape[0] // n
    if "nc" not in _CACHE:
        _CACHE["nc"] = build(nseq=nseq)
    nc, _ = _CACHE["nc"]
    consts = host_consts()
    shared = {k: np.ascontiguousarray(np.asarray(inputs[k], dtype=np.float32)) for k in IN_NAMES}
    shared.update(consts)
    in_maps = []
    for c in range(n):
        m = dict(shared)
        m["x"] = x[c * nseq:(c + 1) * nseq]
        in_maps.append(m)
    res = run_bass_kernel_spmd(nc, in_maps, core_ids=list(range(n)))
    return np.concatenate([r["out"] for r in res.results], axis=0).astype(np.float32)
```
